# Optimizing a Trainium2 kernel written in Bass

```python
import jax, jax.numpy as jnp
from jax import lax
import numpy as np

D_MODEL = 1024
BATCH = 8
SEQ = 4096
DEPTH = 2
DEC_BATCH = 1
DEC_SEQ = 16384
PAST_LEN = 128

HEAD_DIM = 64
N_META = 16
GRID_W = 64
BLOCK = 128
RMS_EPS = 1e-6
A_HEADS = 12
A_KV_HEADS = 4
A_THETA = 10000.0
B_GROUPS = 4
B_DIM = 64
C_HEADS = 8
C_KV_HEADS = 2
WINDOW = 128
ROPE_THETA = 500000.0
ROPE_DIMS = HEAD_DIM // 4
D_HEADS = 8
D_WIDTH = D_HEADS * HEAD_DIM
DECAY_RANK = 64
ICLR_RANK = 64
GATE_RANK = 128
LNX_EPS = 64e-5
D_FF = ((8 * D_MODEL // 3 + 255) // 256) * 256
N_EVEN = (DEPTH + 1) // 2
N_ODD = DEPTH // 2
A_Q = A_HEADS * HEAD_DIM
A_KV = A_KV_HEADS * HEAD_DIM
B_W = B_GROUPS * B_DIM
EVEN_IN = A_Q + 2 * A_KV + B_W
EVEN_OUT = A_Q + B_W
C_Q = C_HEADS * HEAD_DIM
C_KV = C_KV_HEADS * HEAD_DIM
D_IN = 3 * D_WIDTH + 2 * DECAY_RANK + 2 * ICLR_RANK + GATE_RANK
ODD_IN = C_Q + 2 * C_KV + D_IN
ODD_OUT = C_Q + D_WIDTH
D_SPLITS = (D_WIDTH, 2 * D_WIDTH, 3 * D_WIDTH,
            3 * D_WIDTH + DECAY_RANK, 3 * D_WIDTH + 2 * DECAY_RANK,
            3 * D_WIDTH + 2 * DECAY_RANK + ICLR_RANK, 3 * D_WIDTH + 2 * DECAY_RANK + 2 * ICLR_RANK)

kernel_name = 'hybrid_axialgqa_fnet_swa_rwkv7_encoder'


def rms_norm(x, g):
    xf = x.astype(jnp.float32)
    return xf * lax.rsqrt(jnp.mean(xf * xf, axis=-1, keepdims=True) + RMS_EPS) * g.astype(jnp.float32)


def rotate_half(x, ang):
    c = jnp.cos(ang)[:, None, :]
    s = jnp.sin(ang)[:, None, :]
    h = x.shape[-1] // 2
    x1, x2 = x[..., :h], x[..., h:]
    return jnp.concatenate([x1 * c - x2 * s, x1 * s + x2 * c], axis=-1)


def axial_angles(n_tok):
    rows = n_tok // GRID_W
    row = jnp.repeat(jnp.arange(rows), GRID_W)
    col = jnp.arange(rows * GRID_W) % GRID_W
    meta = jnp.arange(N_META) - N_META
    row = jnp.concatenate([meta, row]).astype(jnp.float32)
    col = jnp.concatenate([meta, col]).astype(jnp.float32)
    half = HEAD_DIM // 2
    inv = A_THETA ** (-jnp.arange(0, half, 2, dtype=jnp.float32) / half)
    return row[:, None] * inv, col[:, None] * inv


def partial_angles(n_pos):
    inv = ROPE_THETA ** (-jnp.arange(0, ROPE_DIMS, 2, dtype=jnp.float32) / ROPE_DIMS)
    return jnp.arange(n_pos, dtype=jnp.float32)[:, None] * inv


def dense_attention(qb, k, v):
    s = jnp.einsum('bqhgd,bshd->bhgqs', qb, k)
    p = jax.nn.softmax(s, axis=-1)
    return jnp.einsum('bhgqs,bshd->bqhgd', p, v)


def mixer_a(qa, ka, va, ang_row, ang_col, q_gain, k_gain):
    B, L = qa.shape[:2]
    G = A_HEADS // A_KV_HEADS
    half = HEAD_DIM // 2
    q = rms_norm(qa, q_gain)
    k = rms_norm(ka, k_gain)
    q = jnp.concatenate([rotate_half(q[..., :half], ang_row), rotate_half(q[..., half:], ang_col)], axis=-1) * HEAD_DIM ** -0.5
    k = jnp.concatenate([rotate_half(k[..., :half], ang_row), rotate_half(k[..., half:], ang_col)], axis=-1)
    v = va.astype(jnp.float32)
    q = q.reshape(B, L, A_KV_HEADS, G, HEAD_DIM)
    out_meta = dense_attention(q[:, :N_META], k, v)
    nb = (L - N_META) // BLOCK
    q_blocks = jnp.moveaxis(q[:, N_META:].reshape(B, nb, BLOCK, A_KV_HEADS, G, HEAD_DIM), 1, 0)
    out_real = lax.map(lambda qb: dense_attention(qb, k, v), q_blocks)
    out_real = jnp.moveaxis(out_real, 0, 1).reshape(B, L - N_META, A_KV_HEADS, G, HEAD_DIM)
    return jnp.concatenate([out_meta, out_real], axis=1).reshape(B, L, A_Q)


def mixer_b(fb, norm_g, w_lin, b_lin):
    B, L = fb.shape[:2]
    f = rms_norm(fb, norm_g)
    spec = jnp.fft.fft2(f, axes=(1, 3), norm='ortho').real.astype(jnp.float32)
    y = jnp.einsum('blgc,gcd->blgd', spec, w_lin) + b_lin
    return y.reshape(B, L, B_W)


def partial_rope(x, ang):
    return jnp.concatenate([rotate_half(x[..., :ROPE_DIMS], ang), x[..., ROPE_DIMS:]], axis=-1)


def attend_with_sink(s, mask, sink):
    s = jnp.where(mask, s, -jnp.inf)
    sk = jnp.broadcast_to(sink[:, :, None, None], s.shape[:-1] + (1,))
    return jax.nn.softmax(jnp.concatenate([s, sk], axis=-1), axis=-1)[..., :-1]


def mixer_c(qc, kc, vc, ang, sink):
    B, L = qc.shape[:2]
    S = L - N_META
    nb = S // BLOCK
    G = C_HEADS // C_KV_HEADS
    q = partial_rope(qc.astype(jnp.float32), ang) * HEAD_DIM ** -0.5
    k = partial_rope(kc.astype(jnp.float32), ang)
    v = vc.astype(jnp.float32)
    q = q.reshape(B, L, C_KV_HEADS, G, HEAD_DIM)
    sink = sink.astype(jnp.float32).reshape(C_KV_HEADS, G)
    pos_m = jnp.arange(N_META)
    t0 = jnp.arange(BLOCK)
    mask_m = jnp.concatenate([jnp.ones((N_META, N_META), bool),
                              (t0[None, :] + N_META - pos_m[:, None]) <= WINDOW], axis=1)
    k_m = jnp.concatenate([k[:, :N_META], k[:, N_META:N_META + BLOCK]], axis=1)
    v_m = jnp.concatenate([v[:, :N_META], v[:, N_META:N_META + BLOCK]], axis=1)
    s_m = jnp.einsum('bqhgd,bshd->bhgqs', q[:, :N_META], k_m)
    p_m = attend_with_sink(s_m, mask_m, sink)
    out_meta = jnp.einsum('bhgqs,bshd->bqhgd', p_m, v_m).reshape(B, N_META, C_Q)

    def band(t):
        tp = jnp.pad(t[:, N_META:], ((0, 0), (BLOCK, BLOCK), (0, 0), (0, 0)))
        tp = tp.reshape(B, nb + 2, BLOCK, C_KV_HEADS, HEAD_DIM)
        win = jnp.concatenate([tp[:, :-2], tp[:, 1:-1], tp[:, 2:]], axis=2)
        lead = jnp.broadcast_to(t[:, None, :N_META], (B, nb, N_META, C_KV_HEADS, HEAD_DIM))
        return jnp.concatenate([lead, win], axis=2)

    k_b, v_b = band(k), band(v)
    c = jnp.arange(nb)[:, None, None]
    i = jnp.arange(BLOCK)[None, :, None]
    j = jnp.arange(3 * BLOCK)[None, None, :]
    tq = c * BLOCK + i
    tk = (c - 1) * BLOCK + j
    mask_band = (jnp.abs(tq - tk) <= WINDOW) & (tk >= 0) & (tk < S)
    mask_r = jnp.concatenate([jnp.ones((nb, BLOCK, N_META), bool), mask_band], axis=-1)
    q_r = q[:, N_META:].reshape(B, nb, BLOCK, C_KV_HEADS, G, HEAD_DIM)
    s_r = jnp.einsum('bnqhgd,bnshd->bnhgqs', q_r, k_b)
    p_r = attend_with_sink(s_r, mask_r[:, None, None], sink)
    out_real = jnp.einsum('bnhgqs,bnshd->bnqhgd', p_r, v_b).reshape(B, S, C_Q)
    return jnp.concatenate([out_meta, out_real], axis=1)


def wkv_scan(r, w, k, v, a, b, reverse):
    B, L, H, N = r.shape
    xs = tuple(jnp.moveaxis(t, 1, 0) for t in (r, w, k, v, a, b))

    def step(S, inp):
        rt, wt, kt, vt, at, bt = inp
        sa = jnp.einsum('bhij,bhj->bhi', S, at)
        S = S * wt[:, :, None, :] + sa[..., None] * bt[:, :, None, :] + vt[..., None] * kt[:, :, None, :]
        return S, jnp.einsum('bhij,bhj->bhi', S, rt)

    S0 = jnp.zeros((B, H, N, N), jnp.float32)
    _, o = lax.scan(step, S0, xs, reverse=reverse)
    return jnp.moveaxis(o, 0, 1)


def mixer_d(u, mu_prev, mu_next, w0, w_up, a0, a_up, g_up, k_k, k_a, r_k, ln_g, ln_b):
    B, L, _ = u.shape
    u = u.astype(jnp.float32)
    u_prev = jnp.pad(u, ((0, 0), (1, 0), (0, 0)))[:, :-1]
    u_next = jnp.pad(u, ((0, 0), (0, 1), (0, 0)))[:, 1:]
    u = u + mu_prev * (u_prev - u) + mu_next * (u_next - u)
    r, k, v, wfd, wbd, afd, abd, gd = jnp.split(u, D_SPLITS, axis=-1)
    heads = lambda t: t.reshape(B, L, D_HEADS, HEAD_DIM)
    g = jax.nn.sigmoid(gd) @ g_up
    kk = heads(k * k_k)
    kk = kk * lax.rsqrt(jnp.maximum(jnp.sum(kk * kk, axis=-1, keepdims=True), 1e-24))
    r, k, v = heads(r), heads(k), heads(v)
    k_a = k_a.reshape(D_HEADS, HEAD_DIM)
    outs = []
    bonuses = []
    for d, (wd, ad) in enumerate(((wfd, afd), (wbd, abd))):
        w_log = -jax.nn.softplus(-(w0[d] + jnp.tanh(wd) @ w_up[d])) - 0.5
        decay = heads(jnp.exp(-jnp.exp(w_log)))
        a = heads(jax.nn.sigmoid(a0[d] + ad @ a_up[d]))
        kd = k * (1.0 + (a - 1.0) * k_a)
        outs.append(wkv_scan(r, decay, kd, v, -kk, kk * a, reverse=(d == 1)))
        bonuses.append(jnp.sum(r * kd * r_k, axis=-1, keepdims=True) * v)
    o = outs[0] + outs[1]
    mean = jnp.mean(o, axis=-1, keepdims=True)
    var = jnp.mean(jnp.square(o - mean), axis=-1, keepdims=True)
    y = (o - mean) * lax.rsqrt(var + LNX_EPS) * ln_g.reshape(D_HEADS, HEAD_DIM) + ln_b.reshape(D_HEADS, HEAD_DIM)
    y = y + bonuses[0] + bonuses[1]
    return y.reshape(B, L, D_WIDTH) * g


def swiglu(h, w_gate, w_up, w_down):
    return (jax.nn.silu(h @ w_gate) * (h @ w_up)) @ w_down


def trunk(x, p):
    B, S, _ = x.shape
    L = S + N_META
    meta = jnp.broadcast_to(p['meta_tokens'].astype(x.dtype)[None], (B, N_META, D_MODEL))
    h = jnp.concatenate([meta, x], axis=1)
    ang_row, ang_col = axial_angles(S)
    ang_1d = partial_angles(L)
    for i in range(DEPTH):
        hn = rms_norm(h, p['pre_mix_g'][i])
        if i % 2 == 0:
            e = i // 2
            proj = hn @ p['even_w_in'][e]
            qa = proj[..., :A_Q].reshape(B, L, A_HEADS, HEAD_DIM)
            ka = proj[..., A_Q:A_Q + A_KV].reshape(B, L, A_KV_HEADS, HEAD_DIM)
            va = proj[..., A_Q + A_KV:A_Q + 2 * A_KV].reshape(B, L, A_KV_HEADS, HEAD_DIM)
            fb = proj[..., A_Q + 2 * A_KV:].reshape(B, L, B_GROUPS, B_DIM)
            ya = mixer_a(qa, ka, va, ang_row, ang_col, p['a_q_gain'][e], p['a_k_gain'][e])
            yb = mixer_b(fb, p['b_norm_g'][e], p['b_w'][e], p['b_b'][e])
            mix = jnp.concatenate([ya, yb], axis=-1) @ p['even_w_out'][e]
        else:
            o = i // 2
            proj = hn @ p['odd_w_in'][o]
            qc = proj[..., :C_Q].reshape(B, L, C_HEADS, HEAD_DIM)
            kc = proj[..., C_Q:C_Q + C_KV].reshape(B, L, C_KV_HEADS, HEAD_DIM)
            vc = proj[..., C_Q + C_KV:C_Q + 2 * C_KV].reshape(B, L, C_KV_HEADS, HEAD_DIM)
            yc = mixer_c(qc, kc, vc, ang_1d, p['c_sink'][o])
            yd = mixer_d(proj[..., C_Q + 2 * C_KV:], p['d_mu_prev'][o], p['d_mu_next'][o],
                         p['d_w0'][o], p['d_w_up'][o], p['d_a0'][o], p['d_a_up'][o], p['d_g_up'][o],
                         p['d_k_k'][o], p['d_k_a'][o], p['d_r_k'][o], p['d_ln_g'][o], p['d_ln_b'][o])
            mix = jnp.concatenate([yc, yd], axis=-1) @ p['odd_w_out'][o]
        h = h + rms_norm(mix, p['post_mix_g'][i]).astype(h.dtype)
        hn = rms_norm(h, p['pre_ffn_g'][i])
        f = swiglu(hn, p['ffn_w_gate'][i], p['ffn_w_up'][i], p['ffn_w_down'][i])
        h = h + rms_norm(f, p['post_ffn_g'][i]).astype(h.dtype)
    return h[:, N_META:]


def setup_inputs(seed: int = 0) -> dict:
    key = jax.random.key(seed)
    ks = jax.random.split(key, 32)

    def nrm(i, shape, scale):
        return jax.random.normal(ks[i], shape, jnp.float32) * scale

    return {
        'x_prompt': nrm(0, (BATCH, SEQ, D_MODEL), 1.0),
        'x_sample': nrm(1, (DEC_BATCH, DEC_SEQ, D_MODEL), 1.0),
        'meta_tokens': nrm(2, (N_META, D_MODEL), 1.0),
        'pre_mix_g': 1.0 + nrm(3, (DEPTH, D_MODEL), 0.05),
        'post_mix_g': 1.0 + nrm(4, (DEPTH, D_MODEL), 0.05),
        'pre_ffn_g': 1.0 + nrm(5, (DEPTH, D_MODEL), 0.05),
        'post_ffn_g': 1.0 + nrm(6, (DEPTH, D_MODEL), 0.05),
        'even_w_in': nrm(7, (N_EVEN, D_MODEL, EVEN_IN), D_MODEL ** -0.5),
        'even_w_out': nrm(8, (N_EVEN, EVEN_OUT, D_MODEL), EVEN_OUT ** -0.5),
        'a_q_gain': 1.0 + nrm(9, (N_EVEN, HEAD_DIM), 0.05),
        'a_k_gain': 1.0 + nrm(10, (N_EVEN, HEAD_DIM), 0.05),
        'b_norm_g': 1.0 + nrm(11, (N_EVEN, B_GROUPS, B_DIM), 0.05),
        'b_w': nrm(12, (N_EVEN, B_GROUPS, B_DIM, B_DIM), B_DIM ** -0.5),
        'b_b': nrm(13, (N_EVEN, B_GROUPS, B_DIM), 0.01),
        'odd_w_in': nrm(14, (N_ODD, D_MODEL, ODD_IN), D_MODEL ** -0.5),
        'odd_w_out': nrm(15, (N_ODD, ODD_OUT, D_MODEL), ODD_OUT ** -0.5),
        'c_sink': nrm(16, (N_ODD, C_HEADS), 0.5),
        'd_mu_prev': jax.random.uniform(ks[17], (N_ODD, D_IN), jnp.float32, 0.0, 0.5),
        'd_mu_next': jax.random.uniform(ks[18], (N_ODD, D_IN), jnp.float32, 0.0, 0.5),
        'd_w0': jnp.linspace(-6.5, -1.5, D_WIDTH, dtype=jnp.float32)[None, None] + nrm(19, (N_ODD, 2, D_WIDTH), 0.1),
        'd_w_up': nrm(20, (N_ODD, 2, DECAY_RANK, D_WIDTH), 0.1 * DECAY_RANK ** -0.5),
        'd_a0': nrm(21, (N_ODD, 2, D_WIDTH), 0.1),
        'd_a_up': nrm(22, (N_ODD, 2, ICLR_RANK, D_WIDTH), 0.1 * ICLR_RANK ** -0.5),
        'd_g_up': nrm(23, (N_ODD, GATE_RANK, D_WIDTH), GATE_RANK ** -0.5),
        'd_k_k': 0.85 + nrm(24, (N_ODD, D_WIDTH), 0.05),
        'd_k_a': 1.0 + nrm(25, (N_ODD, D_WIDTH), 0.05),
        'd_r_k': nrm(26, (N_ODD, D_HEADS, HEAD_DIM), 0.1),
        'd_ln_g': 1.0 + nrm(27, (N_ODD, D_WIDTH), 0.1),
        'd_ln_b': nrm(28, (N_ODD, D_WIDTH), 0.01),
        'ffn_w_gate': nrm(29, (DEPTH, D_MODEL, D_FF), D_MODEL ** -0.5),
        'ffn_w_up': nrm(30, (DEPTH, D_MODEL, D_FF), D_MODEL ** -0.5),
        'ffn_w_down': nrm(31, (DEPTH, D_FF, D_MODEL), D_FF ** -0.5),
    }


def reference(x_prompt, x_sample, meta_tokens, pre_mix_g, post_mix_g, pre_ffn_g, post_ffn_g,
              even_w_in, even_w_out, a_q_gain, a_k_gain, b_norm_g, b_w, b_b,
              odd_w_in, odd_w_out, c_sink, d_mu_prev, d_mu_next, d_w0, d_w_up, d_a0, d_a_up,
              d_g_up, d_k_k, d_k_a, d_r_k, d_ln_g, d_ln_b, ffn_w_gate, ffn_w_up, ffn_w_down):
    params = dict(meta_tokens=meta_tokens, pre_mix_g=pre_mix_g, post_mix_g=post_mix_g,
                  pre_ffn_g=pre_ffn_g, post_ffn_g=post_ffn_g,
                  even_w_in=even_w_in, even_w_out=even_w_out, a_q_gain=a_q_gain, a_k_gain=a_k_gain,
                  b_norm_g=b_norm_g, b_w=b_w, b_b=b_b,
                  odd_w_in=odd_w_in, odd_w_out=odd_w_out, c_sink=c_sink,
                  d_mu_prev=d_mu_prev, d_mu_next=d_mu_next, d_w0=d_w0, d_w_up=d_w_up,
                  d_a0=d_a0, d_a_up=d_a_up, d_g_up=d_g_up, d_k_k=d_k_k, d_k_a=d_k_a, d_r_k=d_r_k,
                  d_ln_g=d_ln_g, d_ln_b=d_ln_b,
                  ffn_w_gate=ffn_w_gate, ffn_w_up=ffn_w_up, ffn_w_down=ffn_w_down)
    y_prompt = trunk(x_prompt, params)
    y_sample = trunk(x_sample, params)
    return (y_prompt, y_sample)
```

```python
import math
import contextlib
import numpy as np
import concourse.bass as bass
import concourse.mybir as mybir
from concourse.bass_utils import run_bass_kernel_spmd

F32 = mybir.dt.float32
BF16 = mybir.dt.bfloat16
I32 = mybir.dt.int32
AF = mybir.ActivationFunctionType
ALU = mybir.AluOpType
AX = mybir.AxisListType

D = 1024
NCORE = 8
EPS = 1e-6
ENG_NAMES = ("pe", "act", "dve", "pool", "sp")
SAME_ENGINE_SYNC = True


class Buf:
    __slots__ = ("name", "w", "r", "dsem", "dcnt", "excl")

    def __init__(self, name):
        self.name = name
        self.excl = False
        self.w = {}
        self.r = {}
        self.dsem = None
        self.dcnt = 0


class TL:
    def __init__(self, t, name):
        self.t = t
        self.b = Buf(name)

    def __getitem__(self, i):
        return self.t[i]


def _b(x):
    return x.b if isinstance(x, TL) else x


class Sched:
    def __init__(self, nc):
        self.nc = nc
        self.ops = {e: [] for e in ENG_NAMES}
        self.sem = {e: nc.alloc_semaphore(name="s_" + e) for e in ENG_NAMES}
        self.cnt = {e: 0 for e in ENG_NAMES}
        self.seen = {e: {} for e in ENG_NAMES}
        self.dpool = []
        self.dlive = []
        self.dall = {}
        self.nsem = 0

    def _need(self, e, toks, is_dma):
        waits = []
        for sid, (sem, val, src) in toks.items():
            if not is_dma and src == e and (e == "pe" or not SAME_ENGINE_SYNC):
                continue
            if self.seen[e].get(sid, 0) >= val:
                continue
            self.seen[e][sid] = val
            waits.append((sem, val))
        return waits

    def _collect(self, e, reads, writes, is_dma):
        waits = []
        for b in reads:
            waits += self._need(e, b.w, is_dma)
        for b in writes:
            waits += self._need(e, b.w, is_dma)
            waits += self._need(e, b.r, is_dma)
        return waits

    def op(self, e, fn, reads=(), writes=()):
        reads = [_b(x) for x in reads]
        writes = [_b(x) for x in writes]
        writes = writes + [b for b in reads if b.excl and b not in writes]
        waits = self._collect(e, reads, writes, False)
        self.cnt[e] += 1
        tok = (self.sem[e], self.cnt[e], e)
        sid = id(self.sem[e])
        for b in reads:
            b.r[sid] = tok
        for b in writes:
            b.w[sid] = tok
        self.ops[e].append((waits, fn, (self.sem[e], 1)))

    def dma(self, e, fn, reads=(), writes=(), key=None):
        reads = [_b(x) for x in reads]
        writes = [_b(x) for x in writes]
        key = _b(key)
        waits = self._collect(e, reads, writes, True)
        if key.dsem is None:
            if self.dpool:
                key.dsem, key.dcnt = self.dpool.pop()
            else:
                key.dsem = self.nc.alloc_semaphore(name="d%d" % self.nsem)
                self.nsem += 1
                key.dcnt = 0
            self.dlive.append(key)
        key.dcnt += 16
        tok = (key.dsem, key.dcnt, "dma")
        sid = id(key.dsem)
        self.dall[sid] = [key.dsem, key.dcnt]
        for b in reads:
            b.r[sid] = tok
        for b in writes:
            b.w[sid] = tok
        self.ops[e].append((waits, fn, (key.dsem, 16)))

    def barrier(self, engines=ENG_NAMES):
        toks = [(self.sem[e], self.cnt[e]) for e in ENG_NAMES if self.cnt[e] > 0]
        toks += [(s, c) for s, c in self.dall.values()]
        for e in engines:
            waits = []
            for sem, val in toks:
                if self.seen[e].get(id(sem), 0) >= val:
                    continue
                self.seen[e][id(sem)] = val
                waits.append((sem, val))
            if waits:
                self.ops[e].append((waits, None, None))
        for b in self.dlive:
            self.dpool.append((b.dsem, b.dcnt))
            b.dsem = None
        self.dlive = []

    def emit(self):
        nc = self.nc
        with nc.Block() as block:
            for e in ENG_NAMES:
                ops = self.ops[e]
                if not ops:
                    continue

                def body(eng, ops=ops):
                    for waits, fn, inc in ops:
                        for sem, val in waits:
                            eng.wait_ge(sem, val)
                        if fn is not None:
                            fn(eng).then_inc(inc[0], inc[1])

                dec = {"pe": block.tensor, "act": block.scalar, "dve": block.vector,
                       "pool": block.gpsimd, "sp": block.sync}[e]
                dec(body)


class Ring:
    def __init__(self, tiles):
        self.tiles = tiles
        self.i = 0

    def next(self):
        t = self.tiles[self.i % len(self.tiles)]
        self.i += 1
        return t


class Ctx:
    uid = 0

    def __init__(self, nc, S):
        self.nc = nc
        self.S = S
        self.es = contextlib.ExitStack()
        self.dq_i = 0

    def __enter__(self):
        self.es.__enter__()
        return self

    def __exit__(self, *a):
        self.S.barrier()
        return self.es.__exit__(*a)

    def sb(self, name, shape, dt):
        Ctx.uid += 1
        n = "%s_%d" % (name, Ctx.uid)
        return TL(self.es.enter_context(self.nc.sbuf_tensor(n, list(shape), dt)), n)

    def ps(self, name, shape, dt):
        Ctx.uid += 1
        n = "%s_%d" % (name, Ctx.uid)
        esz = 4 if dt in (F32, I32) else 2
        tot = 1
        for d in shape[1:]:
            tot *= d
        assert tot * esz <= 2048, (name, shape)
        full = self.es.enter_context(self.nc.psum_tensor(n, [128, 2048 // esz], dt))
        view = full[0:shape[0], 0:tot]
        if len(shape) == 3:
            view = view.rearrange("p (a b) -> p a b", b=shape[2])
        elif len(shape) == 4:
            view = view.rearrange("p (a b c) -> p a b c", b=shape[2], c=shape[3])
        t = TL(view, n)
        t.b.excl = True
        return t

    def ring(self, name, n, shape, dt, ps=False):
        f = self.ps if ps else self.sb
        return Ring([f(name + str(i), shape, dt) for i in range(n)])

    def dq(self):
        self.dq_i += 1
        return "sp" if self.dq_i % 2 else "pool"

    def load(self, dst_tl, dst_ap, src_tl, src_ap, q=None):
        self.S.dma(q or self.dq(), lambda e: e.dma_start(out=dst_ap, in_=src_ap),
                   reads=[src_tl], writes=[dst_tl], key=dst_tl)

    def store(self, dst_tl, dst_ap, src_tl, src_ap, q=None, slow=False):
        self.S.dma(q or self.dq(), lambda e: e.dma_start(out=dst_ap, in_=src_ap, allow_slow_non_contiguous=slow),
                   reads=[src_tl], writes=[dst_tl], key=src_tl)


class Seq:
    def __init__(self, name, S_len):
        self.name = name
        self.S = S_len
        self.L = S_len + 16
        self.NT = (self.L + 127) // 128
        self.Lp = self.NT * 128
        self.last = self.L - (self.NT - 1) * 128
        self.NB = (self.Lp + 511) // 512

    def rows(self, t):
        return self.last if t == self.NT - 1 else 128


def host_consts(seq):
    L, Lp, NT, NB = seq.L, seq.Lp, seq.NT, seq.NB
    S_len = seq.S
    c = {}
    rows = S_len // 64
    row = np.repeat(np.arange(rows), 64)
    col = np.arange(rows * 64) % 64
    meta = np.arange(16) - 16
    row = np.concatenate([meta, row]).astype(np.float32)
    col = np.concatenate([meta, col]).astype(np.float32)
    inv = (np.float32(10000.0) ** (-np.arange(0, 32, 2, dtype=np.float32) / np.float32(32))).astype(np.float32)
    ang = np.zeros((Lp, 2, 16), np.float32)
    ang[:L, 0] = row[:, None] * inv
    ang[:L, 1] = col[:, None] * inv
    ropeA = np.zeros((Lp, 2, 2, 16), np.float32)
    ropeA[:, 0] = np.cos(ang)
    ropeA[:, 1] = np.sin(ang)
    c["ropeA"] = ropeA
    inv1 = (np.float32(500000.0) ** (-np.arange(0, 16, 2, dtype=np.float32) / np.float32(16))).astype(np.float32)
    ang1 = np.zeros((Lp, 8), np.float32)
    ang1[:L] = np.arange(L, dtype=np.float32)[:, None] * inv1
    ropeC = np.zeros((Lp, 2, 8), np.float32)
    ropeC[:, 0] = np.cos(ang1)
    ropeC[:, 1] = np.sin(ang1)
    c["ropeC"] = ropeC
    p = np.arange(128)
    lv = (np.arange(NT)[None, :] * 128 + p[:, None]).astype(np.int64)
    c["lvals"] = lv.astype(np.float32)
    l0 = (np.arange(NB) * 512).astype(np.int64)
    sp = (lv[:, :, None] * l0[None, None, :]) % L
    c["dftsp"] = sp.reshape(128, NT * NB).astype(np.float32)
    return c


def global_consts():
    c = {}
    c["ident"] = np.eye(128, dtype=np.float32)
    c["iota512"] = np.tile(np.arange(512, dtype=np.float32)[None, :], (128, 1))
    k = np.arange(64)
    ang = 2 * np.pi * ((k[:, None] * k[None, :]) % 64) / 64.0
    c64 = (np.cos(ang) / 8.0).astype(np.float32)
    s64 = (np.sin(ang) / 8.0).astype(np.float32)
    c["cs64"] = np.concatenate([c64, c64, s64, s64], axis=1)
    i = np.arange(128)
    mk = np.zeros((3, 128, 128), np.float32)
    mk[0] = (i[:, None] >= i[None, :])
    mk[1] = (i[:, None] <= i[None, :])
    mk[2] = mk[0]; mk[2][:16, :] = 1.0
    c["cmask"] = np.ascontiguousarray(mk.transpose(1, 0, 2))
    j = np.arange(64)
    m4 = np.zeros((4, 64, 64), np.float32)
    m4[0] = (j[None, :] > j[:, None])
    m4[1] = (j[None, :] >= j[:, None])
    m4[2] = (j[None, :] < j[:, None])
    m4[3] = (j[None, :] <= j[:, None])
    c["m4"] = np.ascontiguousarray(m4.transpose(1, 0, 2))
    rs = np.ones((64, 512), np.float32); rs[:, 0::64] = 0.0
    c["reset"] = rs
    return c


class Prog:
    def __init__(self, seqs, debug=False):
        self.seqs = seqs
        self.debug = debug
        self.stop_after = 1
        nc = self.nc = bass.Bass("TRN2", target_bir_lowering=False)
        self.S = Sched(nc)
        self.dr = {}
        self.in_names = []
        self.out_names = []

    def din(self, name, shape, dt=F32):
        t = TL(self.nc.dram_tensor(name, list(shape), dt, kind="ExternalInput").ap(), name)
        self.dr[name] = t
        self.in_names.append(name)
        return t

    def dout(self, name, shape, dt=F32):
        t = TL(self.nc.dram_tensor(name, list(shape), dt, kind="ExternalOutput").ap(), name)
        self.dr[name] = t
        self.out_names.append(name)
        return t

    def dscr(self, name, shape, dt):
        if self.debug:
            return self.dout(name, shape, dt)
        t = TL(self.nc.dram_tensor(name, list(shape), dt).ap(), name)
        self.dr[name] = t
        return t

    def dbg(self, cx, name, tl, ap, shape, dt):
        if not self.debug or name in self.dr:
            return
        o = self.dout("dbg_" + name, shape, dt)
        cx.store(o, o[tuple(slice(None) for _ in shape)], tl, ap)

    def build(self):
        nc, S = self.nc, self.S
        W = self.W = {}
        wspec = dict(
            pre_mix_g=(2, D), post_mix_g=(2, D), pre_ffn_g=(2, D), post_ffn_g=(2, D),
            even_w_in=(1, D, 1536), even_w_out=(1, D, D), a_q_gain=(1, 64), a_k_gain=(1, 64),
            b_norm_g=(1, 4, 64), b_w=(1, 4, 64, 64), b_b=(1, 4, 64),
            odd_w_in=(1, D, 2688), odd_w_out=(1, D, D), c_sink=(1, 8),
            d_mu_prev=(1, 1920), d_mu_next=(1, 1920), d_w0=(1, 2, 512), d_w_up=(1, 2, 64, 512),
            d_a0=(1, 2, 512), d_a_up=(1, 2, 64, 512), d_g_up=(1, 128, 512), d_k_k=(1, 512),
            d_k_a=(1, 512), d_r_k=(1, 8, 64), d_ln_g=(1, 512), d_ln_b=(1, 512),
            ffn_w_gate=(2, D, 2816), ffn_w_up=(2, D, 2816), ffn_w_down=(2, 2816, D))
        for k, shp in wspec.items():
            W[k] = self.din(k, shp)
        G = self.G = {}
        for k, v in global_consts().items():
            G[k] = self.din("c_" + k, v.shape)
        maxLp = max(s.Lp for s in self.seqs)
        sc = self.sc = {}
        sc["QT"] = self.dscr("QT", [768, maxLp], BF16)
        sc["KT"] = self.dscr("KT", [256, maxLp], BF16)
        sc["VX"] = self.dscr("VX", [maxLp, 4, 65], BF16)
        sc["GG"] = self.dscr("GG", [maxLp, 512], BF16)
        sc["YA"] = self.dscr("YA", [maxLp, 768], BF16)
        sc["YB"] = self.dscr("YB", [maxLp, 256], BF16)
        sc["H1"] = self.dscr("H1", [maxLp, D], F32)
        sc["QCT"] = self.dscr("QCT", [8, 65, maxLp], BF16)
        sc["KCT"] = self.dscr("KCT", [2, 65, maxLp], BF16)
        sc["VCX"] = self.dscr("VCX", [maxLp, 2, 65], BF16)
        sc["BQ"] = self.dscr("BQ", [maxLp, 8], F32)
        sc["UT"] = self.dscr("UT", [1920, maxLp + 2], F32)
        sc["YC"] = self.dscr("YC", [maxLp, 512], BF16)
        sc["OF"] = self.dscr("OF", [maxLp, 512], F32)
        sc["YD"] = self.dscr("YD", [maxLp, 512], BF16)
        for s in self.seqs:
            s.h0 = self.din("h0_" + s.name, [s.Lp, D])
            s.y = self.dout("y_" + s.name, [s.S, D])
            s.c = {k: self.din("c_%s_%s" % (k, s.name), v.shape) for k, v in host_consts(s).items()}

        self.phase_consts()
        for s in self.seqs:
            self.phase_even_proj(s)
            self.phase_attn_a(s)
            self.phase_dft(s)
            self.phase_mix_ffn(s, 0, s.h0, sc["H1"], ("YA", 768), ("YB", 256), W["even_w_out"], final=False)
            if self.stop_after == 0:
                continue
            self.phase_odd_proj(s)
            self.phase_attn_c(s)
            self.phase_rwkv(s, 0)
            self.phase_rwkv(s, 1)
            self.phase_mix_ffn(s, 1, sc["H1"], s.y, ("YC", 512), ("YD", 512), W["odd_w_out"], final=True)
        with nc.allow_non_contiguous_dma(reason="small strided parameter / layout loads"):
            S.emit()
        return nc

    def phase_consts(self):
        nc, S = self.nc, self.S
        self.pes = contextlib.ExitStack()
        cx = self.pcx = Ctx(nc, S)
        cx.es = self.pes
        K = self.K = {}
        G = self.G
        K["identf"] = cx.sb("identf", [128, 128], F32)
        K["identb"] = cx.sb("identb", [128, 128], BF16)
        cx.load(K["identf"], K["identf"][:], G["ident"], G["ident"][:, :])
        S.op("dve", lambda e: e.tensor_copy(out=K["identb"][:], in_=K["identf"][:]), [K["identf"]], [K["identb"]])
        K["eps"] = cx.sb("eps", [128, 1], F32)
        S.op("dve", lambda e: e.memset(K["eps"][:], EPS), [], [K["eps"]])
        K["halfpi"] = cx.sb("halfpi", [128, 1], F32)
        S.op("dve", lambda e: e.memset(K["halfpi"][:], math.pi / 2), [], [K["halfpi"]])
        S.barrier()

    def load_w(self, cx, dst, src_tl, src_ap, nk, ncols, stage_ring, row0=0):
        S = self.S
        for k in range(nk):
            for c0 in range(0, ncols, 512):
                w = min(512, ncols - c0)
                st = stage_ring.next()
                cx.load(st, st[:, 0:w], src_tl, src_ap[row0 + k * 128: row0 + (k + 1) * 128, c0:c0 + w])
                S.op("pool", lambda e, st=st, k=k, c0=c0, w=w: e.tensor_copy(out=dst[:, k, c0:c0 + w], in_=st[:, 0:w]), [st], [dst])

    def rstd(self, cx, ss_tl, ss_ap, out_tl, out_ap, tmp_tl, tmp_ap, scale, eps_ap):
        S = self.S
        S.op("act", lambda e: e.activation(out=tmp_ap, in_=ss_ap, func=AF.Sqrt, bias=eps_ap, scale=scale),
             [ss_tl], [tmp_tl])
        S.op("dve", lambda e: e.reciprocal(out=out_ap, in_=tmp_ap), [tmp_tl], [out_tl])

    def norm_transpose(self, cx, x_tl, gb_tl, hnT, TP, xs_ring, sm_ring, junk):
        S, K = self.S, self.K
        sm = sm_ring.next()
        S.op("act", lambda e: e.activation(out=junk[:], in_=x_tl[:], func=AF.Square, accum_out=sm[:, 0:1]),
             [x_tl], [junk, sm])
        self.rstd(cx, sm, sm[:, 0:1], sm, sm[:, 2:3], sm, sm[:, 1:2], 1.0 / D, K["eps"][:, 0:1])
        xs = xs_ring.next()
        S.op("dve", lambda e: e.scalar_tensor_tensor(out=xs[:], in0=x_tl[:], scalar=sm[:, 2:3], in1=gb_tl[:],
                                                     op0=ALU.mult, op1=ALU.mult), [x_tl, sm, gb_tl], [xs])
        for k in range(8):
            S.op("pe", lambda e, k=k: e.transpose(out=TP[:, k, :], in_=xs[:, k * 128:(k + 1) * 128],
                                                  identity=K["identb"][:]), [xs, K["identb"]], [TP])
        S.op("act", lambda e: e.copy(out=hnT[:], in_=TP[:]), [TP], [hnT])

    def phase_even_proj(self, seq):
        nc, S, W, K, sc = self.nc, self.S, self.W, self.K, self.sc
        with Ctx(nc, S) as cx:
            stage = cx.ring("wst", 3, [128, 512], F32)
            Win = cx.sb("win", [128, 8, 1536], BF16)
            self.load_w(cx, Win, W["even_w_in"], W["even_w_in"][0], 8, 1536, stage)
            gpre = cx.sb("gpre", [128, D], F32)
            cx.load(gpre, gpre[:], W["pre_mix_g"], W["pre_mix_g"][0:1, :].partition_broadcast(128))
            g16 = cx.sb("g16", [128, 16, 64], F32)
            cx.load(g16, g16[:, 0:12, :], W["a_q_gain"], W["a_q_gain"][0:1, :].partition_broadcast(128).broadcast_to([128, 12, 64]))
            cx.load(g16, g16[:, 12:16, :], W["a_k_gain"], W["a_k_gain"][0:1, :].partition_broadcast(128).broadcast_to([128, 4, 64]))
            S.op("dve", lambda e: e.tensor_scalar(out=g16[:, 0:12, :], in0=g16[:, 0:12, :], scalar1=0.125, scalar2=None,
                                                  op0=ALU.mult), [g16], [g16])
            bg = cx.sb("bg", [128, 256], F32)
            cx.load(bg, bg[:], W["b_norm_g"], W["b_norm_g"][0].rearrange("g c -> (g c)").partition_broadcast(128))
            cs64 = cx.sb("cs64", [64, 256], F32)
            cx.load(cs64, cs64[:], self.G["cs64"], self.G["cs64"][:, :])
            bw = cx.sb("bw", [64, 4, 64], F32)
            cx.load(bw, bw[:], W["b_w"], W["b_w"][0].rearrange("g c d -> c g d"))
            Wcs = cx.sb("wcs", [128, 2, 512], BF16)
            S.op("pool", lambda e: e.memset(Wcs[:], 0.0), [], [Wcs])
            pw = cx.ps("pw", [128, 512], F32)
            for g in range(4):
                for j in range(2):
                    S.op("pe", lambda e, g=g, j=j: e.matmul(pw[:, (g * 2 + j) * 64:(g * 2 + j + 1) * 64],
                                                             lhsT=cs64[:, j * 128:(j + 1) * 128], rhs=bw[:, g, :],
                                                             start=True, stop=True), [cs64, bw], [pw])
            for g in range(4):
                h0 = (g % 2) * 64
                for j in range(2):
                    S.op("dve", lambda e, g=g, j=j, h0=h0: e.tensor_copy(
                        out=Wcs[h0:h0 + 64, g // 2, j * 256 + g * 64: j * 256 + (g + 1) * 64],
                        in_=pw[h0:h0 + 64, (g * 2 + j) * 64:(g * 2 + j + 1) * 64]), [pw], [Wcs])

            self.dbg(cx, "win0", Win, Win[:, 0, :], [128, 1536], BF16)
            self.dbg(cx, "gpre", gpre, gpre[:], [128, D], F32)
            self.dbg(cx, "g16", g16, g16[:], [128, 16, 64], F32)
            self.dbg(cx, "wcs", Wcs, Wcs[:], [128, 2, 512], BF16)
            xring = cx.ring("x", 2, [128, D], F32)
            xs_ring = cx.ring("xs", 2, [128, D], BF16)
            sm_ring = cx.ring("sm", 2, [128, 4], F32)
            junk = cx.sb("junk", [128, D], BF16)
            TP = cx.ps("TP", [128, 8, 128], BF16)
            hnT_ring = cx.ring("hnT", 2, [128, 8, 128], BF16)
            PJ = [cx.ps("PJ%d" % c, [128, 512], F32) for c in range(3)]
            sqt = cx.sb("sqt", [128, 16, 64], F32)
            ssq_ring = cx.ring("ssq", 2, [128, 3, 16], F32)
            qk = cx.sb("qk", [128, 16, 64], F32)
            tA = cx.sb("tA", [128, 16, 2, 16], F32)
            tB = cx.sb("tB", [128, 16, 2, 16], F32)
            qkr_ring = cx.ring("qkr", 2, [128, 16, 64], BF16)
            rope_ring = cx.ring("rope", 2, [128, 2, 2, 16], F32)
            TP2 = cx.ps("TP2", [128, 8, 128], BF16)
            qkT_ring = cx.ring("qkT", 2, [128, 8, 128], BF16)
            vx_ring = cx.ring("vx", 2, [128, 4, 65], BF16)
            for vx in vx_ring.tiles:
                S.op("pool", lambda e, vx=vx: e.memset(vx[:], 1.0), [], [vx])
            sqf = cx.sb("sqf", [128, 4, 64], F32)
            fn32 = cx.sb("fn32", [128, 4, 64], F32)
            fn_ring = cx.ring("fn", 2, [128, 256], BF16)
            TP3 = cx.ps("TP3", [128, 2, 128], BF16)
            fnT_ring = cx.ring("fnT", 2, [128, 2, 128], BF16)
            PG = cx.ps("PG", [128, 512], F32)
            g_ring = cx.ring("gsb", 2, [128, 512], BF16)
            QTv = sc["QT"].t.rearrange("(c p) n -> p c n", p=128)
            KTv = sc["KT"].t.rearrange("(c p) n -> p c n", p=128)

            def body(t):
                c0 = t * 128
                xt = xring.next()
                cx.load(xt, xt[:], seq.h0, seq.h0[c0:c0 + 128, :])
                rp = rope_ring.next()
                cx.load(rp, rp[:], seq.c["ropeA"], seq.c["ropeA"][c0:c0 + 128])
                hnT = hnT_ring.next()
                self.norm_transpose(cx, xt, gpre, hnT, TP, xs_ring, sm_ring, junk)
                for c in range(3):
                    for k in range(8):
                        S.op("pe", lambda e, c=c, k=k: e.matmul(PJ[c][:], lhsT=hnT[:, k, :], rhs=Win[:, k, c * 512:(c + 1) * 512],
                                                                 start=(k == 0), stop=(k == 7)), [hnT, Win], [PJ[c]])
                if t == 0 and self.debug:
                    self.dbg(cx, "hnT", hnT, hnT[:], [128, 8, 128], BF16)
                    pjd = cx.sb("pjd", [128, 1536], F32)
                    for c in range(3):
                        S.op("dve", lambda e, c=c: e.tensor_copy(out=pjd[:, c * 512:(c + 1) * 512], in_=PJ[c][:]), [PJ[c]], [pjd])
                    self.dbg(cx, "pj", pjd, pjd[:], [128, 1536], F32)
                for c in range(2):
                    S.op("act", lambda e, c=c: e.activation(out=sqt[:, c * 8:(c + 1) * 8, :],
                                                            in_=PJ[c][:].rearrange("p (h d) -> p h d", d=64), func=AF.Square),
                         [PJ[c]], [sqt])
                sq = ssq_ring.next()
                S.op("dve", lambda e: e.tensor_reduce(out=sq[:, 0, :], in_=sqt[:], axis=AX.X, op=ALU.add), [sqt], [sq])
                self.rstd(cx, sq, sq[:, 0, :], sq, sq[:, 2, :], sq, sq[:, 1, :], 1.0 / 64, K["eps"][:, 0:1])
                for c in range(2):
                    S.op("dve", lambda e, c=c: e.tensor_tensor(
                        out=qk[:, c * 8:(c + 1) * 8, :], in0=PJ[c][:].rearrange("p (h d) -> p h d", d=64),
                        in1=sq[:, 2, c * 8:(c + 1) * 8].unsqueeze(2).broadcast_to([128, 8, 64]), op=ALU.mult),
                        [PJ[c], sq], [qk])
                S.op("dve", lambda e: e.tensor_tensor(out=qk[:], in0=qk[:], in1=g16[:], op=ALU.mult), [qk, g16], [qk])
                qkr = qkr_ring.next()
                q5 = qk[:].rearrange("p h (a c d) -> p h a c d", a=2, c=2)
                o5 = qkr[:].rearrange("p h (a c d) -> p h a c d", a=2, c=2)
                x1, x2 = q5[:, :, :, 0, :], q5[:, :, :, 1, :]
                cosb = rp[:, 0].unsqueeze(1).broadcast_to([128, 16, 2, 16])
                sinb = rp[:, 1].unsqueeze(1).broadcast_to([128, 16, 2, 16])
                S.op("dve", lambda e: e.tensor_tensor(out=tA[:], in0=x1, in1=cosb, op=ALU.mult), [qk, rp], [tA])
                S.op("pool", lambda e: e.tensor_tensor(out=tB[:], in0=x2, in1=sinb, op=ALU.mult), [qk, rp], [tB])
                S.op("dve", lambda e: e.tensor_tensor(out=o5[:, :, :, 0, :], in0=tA[:], in1=tB[:], op=ALU.subtract), [tA, tB], [qkr])
                S.op("dve", lambda e: e.tensor_tensor(out=tA[:], in0=x1, in1=sinb, op=ALU.mult), [qk, rp], [tA])
                S.op("pool", lambda e: e.tensor_tensor(out=tB[:], in0=x2, in1=cosb, op=ALU.mult), [qk, rp], [tB])
                S.op("dve", lambda e: e.tensor_tensor(out=o5[:, :, :, 1, :], in0=tA[:], in1=tB[:], op=ALU.add), [tA, tB], [qkr])
                qf = qkr[:].rearrange("p h d -> p (h d)")
                for c in range(8):
                    S.op("pe", lambda e, c=c: e.transpose(out=TP2[:, c, :], in_=qf[:, c * 128:(c + 1) * 128],
                                                          identity=K["identb"][:]), [qkr, K["identb"]], [TP2])
                qkT = qkT_ring.next()
                S.op("act", lambda e: e.copy(out=qkT[:], in_=TP2[:]), [TP2], [qkT])
                cx.store(sc["QT"], QTv[:, :, c0:c0 + 128], qkT, qkT[:, 0:6, :])
                cx.store(sc["KT"], KTv[:, :, c0:c0 + 128], qkT, qkT[:, 6:8, :])
                vx = vx_ring.next()
                S.op("act", lambda e: e.copy(out=vx[:, :, 0:64], in_=PJ[2][:, 0:256].rearrange("p (h d) -> p h d", d=64)),
                     [PJ[2]], [vx])
                cx.store(sc["VX"], sc["VX"][c0:c0 + 128, :, :], vx, vx[:])
                fv = PJ[2][:, 256:512].rearrange("p (h d) -> p h d", d=64)
                S.op("act", lambda e: e.activation(out=sqf[:], in_=fv, func=AF.Square), [PJ[2]], [sqf])
                S.op("dve", lambda e: e.tensor_reduce(out=sq[:, 0, 0:4], in_=sqf[:], axis=AX.X, op=ALU.add), [sqf], [sq])
                self.rstd(cx, sq, sq[:, 0, 0:4], sq, sq[:, 2, 0:4], sq, sq[:, 1, 0:4], 1.0 / 64, K["eps"][:, 0:1])
                S.op("dve", lambda e: e.tensor_tensor(out=fn32[:], in0=fv, in1=sq[:, 2, 0:4].unsqueeze(2).broadcast_to([128, 4, 64]),
                                                      op=ALU.mult), [PJ[2], sq], [fn32])
                fn = fn_ring.next()
                S.op("dve", lambda e: e.tensor_tensor(out=fn[:], in0=fn32[:].rearrange("p h d -> p (h d)"), in1=bg[:], op=ALU.mult),
                     [fn32, bg], [fn])
                for c in range(2):
                    S.op("pe", lambda e, c=c: e.transpose(out=TP3[:, c, :], in_=fn[:, c * 128:(c + 1) * 128],
                                                          identity=K["identb"][:]), [fn, K["identb"]], [TP3])
                fnT = fnT_ring.next()
                S.op("act", lambda e: e.copy(out=fnT[:], in_=TP3[:]), [TP3], [fnT])
                for c in range(2):
                    S.op("pe", lambda e, c=c: e.matmul(PG[:], lhsT=fnT[:, c, :], rhs=Wcs[:, c, :], start=(c == 0), stop=(c == 1)),
                         [fnT, Wcs], [PG])
                gs = g_ring.next()
                S.op("act", lambda e: e.copy(out=gs[:], in_=PG[:]), [PG], [gs])
                cx.store(sc["GG"], sc["GG"][c0:c0 + 128, :], gs, gs[:])

            for t in range(seq.NT):
                body(t)

    def phase_attn_a(self, seq):
        nc, S, W, K, sc = self.nc, self.S, self.W, self.K, self.sc
        NT, Lp = seq.NT, seq.Lp
        with Ctx(nc, S) as cx:
            gq = cx.sb("gq", [128, 2, 64], F32)
            cx.load(gq, gq[:, 0, :], W["a_q_gain"], W["a_q_gain"][0:1, :].partition_broadcast(128))
            cx.load(gq, gq[:, 1, :], W["a_k_gain"], W["a_k_gain"][0:1, :].partition_broadcast(128))
            gm = cx.sb("gm", [128, 4], F32)
            S.op("dve", lambda e: e.tensor_reduce(out=gm[:, 0:2], in_=gq[:], axis=AX.X, op=ALU.max, apply_absolute_value=True),
                 [gq], [gm])
            S.op("dve", lambda e: e.tensor_tensor(out=gm[:, 2:3], in0=gm[:, 0:1], in1=gm[:, 1:2], op=ALU.mult), [gm], [gm])
            S.op("dve", lambda e: e.tensor_scalar(out=gm[:, 3:4], in0=gm[:, 2:3], scalar1=-8.0, scalar2=None, op0=ALU.mult), [gm], [gm])
            KTs = cx.sb("KTs", [64, Lp], BF16)
            VXs = cx.sb("VXs", [128, NT, 65], BF16)
            q_ring = cx.ring("qb", 2, [64, 512], BF16)
            ST = cx.ring("ST", 2, [128, 512], F32, ps=True)
            OT = cx.ring("OT", 2, [65, 512], F32, ps=True)
            PT = cx.ring("PT", 3, [128, 512], BF16)
            osb_ring = cx.ring("osb", 2, [65, 512], F32)
            TPo = cx.ps("TPo", [128, 4, 65], F32)
            rec_ring = cx.ring("rec", 2, [128, 4, 1], F32)
            y_ring = cx.ring("ya", 2, [128, 4, 64], BF16)
            YAv = sc["YA"].t.rearrange("(s p) c -> p s c", p=128)
            def qblock(g, h, qb):
                if True:
                    if True:
                        n0 = qb * 512
                        wd = min(512, Lp - n0)
                        nsub = wd // 128
                        qt = q_ring.next()
                        cx.load(qt, qt[:, 0:wd], sc["QT"], sc["QT"][h * 64:(h + 1) * 64, n0:n0 + wd])
                        ot = OT.next()
                        for kt in range(NT):
                            r = seq.rows(kt)
                            st = ST.next()
                            S.op("pe", lambda e, st=st, kt=kt, r=r: e.matmul(st[0:r, 0:wd], lhsT=KTs[:, kt * 128:kt * 128 + r],
                                                                               rhs=qt[:, 0:wd], start=True, stop=True), [KTs, qt], [st])
                            pt = PT.next()
                            S.op("act", lambda e, st=st, pt=pt, r=r: e.activation(out=pt[0:r, 0:wd], in_=st[0:r, 0:wd], func=AF.Exp,
                                                                                  bias=gm[0:r, 3:4], scale=1.0), [st, gm], [pt])
                            S.op("pe", lambda e, pt=pt, kt=kt, r=r: e.matmul(ot[:, 0:wd], lhsT=VXs[0:r, kt, :], rhs=pt[0:r, 0:wd],
                                                                               start=(kt == 0), stop=(kt == NT - 1)), [VXs, pt], [ot])
                        osb = osb_ring.next()
                        S.op("dve", lambda e, osb=osb: e.tensor_copy(out=osb[:, 0:wd], in_=ot[:, 0:wd]), [ot], [osb])
                        for s in range(nsub):
                            S.op("pe", lambda e, s=s, osb=osb: e.transpose(out=TPo[:, s, :], in_=osb[:, s * 128:(s + 1) * 128],
                                                                            identity=K["identf"][0:65, 0:65]), [osb, K["identf"]], [TPo])
                        rec = rec_ring.next()
                        S.op("dve", lambda e, rec=rec: e.reciprocal(out=rec[:, 0:nsub, :], in_=TPo[:, 0:nsub, 64:65]), [TPo], [rec])
                        ya = y_ring.next()
                        S.op("dve", lambda e, rec=rec, ya=ya: e.tensor_tensor(out=ya[:, 0:nsub, :], in0=TPo[:, 0:nsub, 0:64],
                                                                              in1=rec[:, 0:nsub, :].broadcast_to([128, nsub, 64]),
                                                                              op=ALU.mult), [TPo, rec], [ya])
                        cx.store(sc["YA"], YAv[:, qb * 4:qb * 4 + nsub, h * 64:(h + 1) * 64], ya, ya[:, 0:nsub, :])

            for g in range(4):
                cx.load(KTs, KTs[:, :], sc["KT"], sc["KT"][g * 64:(g + 1) * 64, 0:Lp])
                for t0 in range(0, NT, 16):
                    t1_ = min(NT, t0 + 16)
                    cx.load(VXs, VXs[:, t0:t1_, :], sc["VX"], sc["VX"][t0 * 128:t1_ * 128, g, :].rearrange("(t p) d -> p t d", p=128))
                for j in range(3):
                    for qb in range(seq.NB):
                        qblock(g, g * 3 + j, qb)

    def phase_dft(self, seq):
        nc, S, W, K, sc = self.nc, self.S, self.W, self.K, self.sc
        NT, Lp, L, NB = seq.NT, seq.Lp, seq.L, seq.NB
        with Ctx(nc, S) as cx:
            GGs = cx.sb("GGs", [128, NT, 512], BF16)
            for t0 in range(0, NT, 16):
                t1 = min(NT, t0 + 16)
                cx.load(GGs, GGs[:, t0:t1, :], sc["GG"], sc["GG"][t0 * 128:t1 * 128, :].rearrange("(t p) c -> p t c", p=128))
            iota = cx.sb("iota", [128, 512], F32)
            cx.load(iota, iota[:], self.G["iota512"], self.G["iota512"][:, :])
            lv = cx.sb("lv", [128, NT], F32)
            cx.load(lv, lv[:], seq.c["lvals"], seq.c["lvals"][:, :])
            sp = cx.sb("sp", [128, NT * NB], F32)
            cx.load(sp, sp[:], seq.c["dftsp"], seq.c["dftsp"][:, :])
            bb = cx.sb("bb", [128, 256], F32)
            cx.load(bb, bb[:], W["b_b"], W["b_b"][0].rearrange("g c -> (g c)").partition_broadcast(128))
            t1r = cx.ring("t1", 2, [128, 512], F32)
            qr = cx.ring("q", 2, [128, 512], I32)
            rr = cx.ring("r", 2, [128, 512], F32)
            rar = cx.ring("ra", 2, [128, 512], F32)
            Cr = cx.ring("C", 3, [128, 512], BF16)
            Sr = cx.ring("Sn", 3, [128, 512], BF16)
            ACC = [cx.ps("acc%d" % i, [128, 512], F32) for i in range(4)]
            yb_ring = cx.ring("yb", 2, [128, 256], BF16)
            w0 = 2 * math.pi / L * (1.0 - 2e-6)
            isq = 1.0 / math.sqrt(L)
            def blk(B):
                n0 = B * 512
                wd = min(512, Lp - n0)
                nsub = wd // 128
                for lt in range(NT):
                    r = seq.rows(lt)
                    t1 = t1r.next(); q = qr.next(); rt = rr.next(); ra = rar.next(); C = Cr.next(); Sn = Sr.next()
                    idx = lt * NB + B
                    S.op("pool", lambda e, t1=t1, lt=lt, idx=idx: e.tensor_scalar(out=t1[:, 0:wd], in0=iota[:, 0:wd], scalar1=lv[:, lt:lt + 1],
                                                                                   scalar2=sp[:, idx:idx + 1], op0=ALU.mult, op1=ALU.add),
                         [iota, lv, sp], [t1])
                    S.op("pool", lambda e, t1=t1, q=q: e.tensor_scalar(out=q[:, 0:wd], in0=t1[:, 0:wd], scalar1=1.0 / L, scalar2=None,
                                                                        op0=ALU.mult), [t1], [q])
                    S.op("dve", lambda e, t1=t1, q=q, rt=rt: e.scalar_tensor_tensor(out=rt[:, 0:wd], in0=q[:, 0:wd], scalar=-float(L),
                                                                                    in1=t1[:, 0:wd], op0=ALU.mult, op1=ALU.add),
                         [q, t1], [rt])
                    S.op("dve", lambda e, rt=rt, ra=ra: e.scalar_tensor_tensor(out=ra[:, 0:wd], in0=rt[:, 0:wd], scalar=-1.0, in1=rt[:, 0:wd], op0=ALU.mult, op1=ALU.max),
                         [rt], [ra])
                    S.op("act", lambda e, rt=rt, Sn=Sn: e.activation(out=Sn[:, 0:wd], in_=rt[:, 0:wd], func=AF.Sin, scale=-w0), [rt], [Sn])
                    S.op("act", lambda e, ra=ra, C=C: e.activation(out=C[:, 0:wd], in_=ra[:, 0:wd], func=AF.Sin, bias=K["halfpi"][:, 0:1],
                                                                    scale=-w0), [ra, K["halfpi"]], [C])
                    for s in range(nsub):
                        S.op("pe", lambda e, s=s, C=C, lt=lt, r=r: e.matmul(ACC[s][:, 0:256], lhsT=C[0:r, s * 128:(s + 1) * 128],
                                                                             rhs=GGs[0:r, lt, 0:256], start=(lt == 0), stop=False),
                             [C, GGs], [ACC[s]])
                        S.op("pe", lambda e, s=s, Sn=Sn, lt=lt, r=r: e.matmul(ACC[s][:, 0:256], lhsT=Sn[0:r, s * 128:(s + 1) * 128],
                                                                               rhs=GGs[0:r, lt, 256:512], start=False, stop=(lt == NT - 1)),
                             [Sn, GGs], [ACC[s]])
                for s in range(nsub):
                    yb = yb_ring.next()
                    S.op("dve", lambda e, s=s, yb=yb: e.scalar_tensor_tensor(out=yb[:], in0=ACC[s][:, 0:256], scalar=isq, in1=bb[:],
                                                                             op0=ALU.mult, op1=ALU.add), [ACC[s], bb], [yb])
                    cx.store(sc["YB"], sc["YB"][n0 + s * 128:n0 + (s + 1) * 128, :], yb, yb[:])

            for B in range(NB):
                blk(B)

    def post_norm_res(self, cx, PM, gpost, hin, hout, sm_ring, junk):
        S, K = self.S, self.K
        sm = sm_ring.next()
        for c in range(2):
            S.op("act", lambda e, c=c: e.activation(out=junk[:, 0:512], in_=PM[c][:], func=AF.Square, accum_out=sm[:, c:c + 1]),
                 [PM[c]], [junk, sm])
        S.op("dve", lambda e: e.tensor_tensor(out=sm[:, 0:1], in0=sm[:, 0:1], in1=sm[:, 1:2], op=ALU.add), [sm], [sm])
        self.rstd(cx, sm, sm[:, 0:1], sm, sm[:, 2:3], sm, sm[:, 3:4], 1.0 / D, K["eps"][:, 0:1])
        for c in range(2):
            S.op("dve", lambda e, c=c: e.scalar_tensor_tensor(out=hout[:, c * 512:(c + 1) * 512], in0=PM[c][:], scalar=sm[:, 2:3],
                                                              in1=gpost[:, c * 512:(c + 1) * 512], op0=ALU.mult, op1=ALU.mult),
                 [PM[c], sm, gpost], [hout])
        S.op("pool", lambda e: e.tensor_tensor(out=hout[:], in0=hout[:], in1=hin[:], op=ALU.add), [hout, hin], [hout])

    def phase_mix_ffn(self, seq, layer, hin_d, hout_d, ya, yb, wout, final):
        nc, S, W, K, sc = self.nc, self.S, self.W, self.K, self.sc
        NT, Lp = seq.NT, seq.Lp
        with Ctx(nc, S) as cx:
            stage = cx.ring("wst", 3, [128, 512], F32)
            Wo = cx.sb("wo", [128, 8, D], BF16)
            self.load_w(cx, Wo, wout, wout[0], 8, D, stage)
            Wg = cx.sb("wg", [128, 8, 2816], BF16)
            self.load_w(cx, Wg, W["ffn_w_gate"], W["ffn_w_gate"][layer], 8, 2816, stage)
            Wu = cx.sb("wu", [128, 8, 2816], BF16)
            self.load_w(cx, Wu, W["ffn_w_up"], W["ffn_w_up"][layer], 8, 2816, stage)
            Wd = cx.sb("wd", [128, 22, D], BF16)
            self.load_w(cx, Wd, W["ffn_w_down"], W["ffn_w_down"][layer], 22, D, stage)
            gs = {}
            for nm in ("post_mix_g", "pre_ffn_g", "post_ffn_g"):
                gs[nm] = cx.sb(nm, [128, D], F32)
                cx.load(gs[nm], gs[nm][:], W[nm], W[nm][layer:layer + 1, :].partition_broadcast(128))
            y_ring = cx.ring("y", 2, [128, D], BF16)
            h_ring = cx.ring("h", 1, [128, D], F32)
            TP = cx.ps("TP", [128, 8, 128], BF16)
            yT_ring = cx.ring("yT", 1, [128, 8, 128], BF16)
            PM = [cx.ps("PM%d" % c, [128, 512], F32) for c in range(2)]
            sm_ring = cx.ring("sm", 4, [128, 4], F32)
            junk = cx.sb("junk", [128, D], BF16)
            h1_ring = cx.ring("h1", 1, [128, D], F32)
            xs_ring = cx.ring("xs", 1, [128, D], BF16)
            hT_ring = cx.ring("hT", 1, [128, 8, 128], BF16)
            PGt = cx.ring("PGt", 2, [128, 4, 128], F32, ps=True)
            PUt = cx.ring("PUt", 2, [128, 4, 128], F32, ps=True)
            sg_ring = cx.ring("sg", 2, [128, 4, 128], F32)
            AT_ring = cx.ring("AT", 1, [128, 22, 128], BF16)
            h2_ring = cx.ring("h2", 1, [128, D], F32)
            vmask = cx.sb("vmask", [128, 1], F32)
            S.op("dve", lambda e: e.memset(vmask[:], 0.0), [], [vmask])
            S.op("dve", lambda e: e.memset(vmask[0:seq.last, :], 1.0), [], [vmask])
            def body(t):
                c0 = t * 128
                y = y_ring.next()
                cx.load(y, y[:, 0:ya[1]], sc[ya[0]], sc[ya[0]][c0:c0 + 128, :])
                cx.load(y, y[:, ya[1]:D], sc[yb[0]], sc[yb[0]][c0:c0 + 128, :])
                h = h_ring.next()
                cx.load(h, h[:], hin_d, hin_d[c0:c0 + 128, :])
                for k in range(8):
                    S.op("pe", lambda e, k=k: e.transpose(out=TP[:, k, :], in_=y[:, k * 128:(k + 1) * 128], identity=K["identb"][:]),
                         [y, K["identb"]], [TP])
                yT = yT_ring.next()
                S.op("act", lambda e: e.copy(out=yT[:], in_=TP[:]), [TP], [yT])
                for c in range(2):
                    for k in range(8):
                        S.op("pe", lambda e, c=c, k=k: e.matmul(PM[c][:], lhsT=yT[:, k, :], rhs=Wo[:, k, c * 512:(c + 1) * 512],
                                                                 start=(k == 0), stop=(k == 7)), [yT, Wo], [PM[c]])
                h1 = h1_ring.next()
                self.post_norm_res(cx, PM, gs["post_mix_g"], h, h1, sm_ring, junk)
                hT = hT_ring.next()
                self.norm_transpose(cx, h1, gs["pre_ffn_g"], hT, TP, xs_ring, sm_ring, junk)
                AT = AT_ring.next()
                for fq in range(6):
                    nf = min(4, 22 - fq * 4)
                    pg = PGt.next(); pu = PUt.next()
                    for (pp, Wm) in ((pg, Wg), (pu, Wu)):
                        for f in range(nf):
                            fc = (fq * 4 + f) * 128
                            for k in range(8):
                                S.op("pe", lambda e, pp=pp, Wm=Wm, f=f, fc=fc, k=k: e.matmul(pp[:, f, :], lhsT=Wm[:, k, fc:fc + 128], rhs=hT[:, k, :],
                                                                                              start=(k == 0), stop=(k == 7)), [Wm, hT], [pp])
                    sg = sg_ring.next()
                    S.op("act", lambda e, pg=pg, sg=sg, nf=nf: e.activation(out=sg[:, 0:nf, :], in_=pg[:, 0:nf, :], func=AF.Silu), [pg], [sg])
                    S.op("dve", lambda e, pu=pu, sg=sg, nf=nf, fq=fq: e.tensor_tensor(out=AT[:, fq * 4:fq * 4 + nf, :], in0=sg[:, 0:nf, :],
                                                                                      in1=pu[:, 0:nf, :], op=ALU.mult), [sg, pu], [AT])
                for c in range(2):
                    for f in range(22):
                        S.op("pe", lambda e, c=c, f=f: e.matmul(PM[c][:], lhsT=AT[:, f, :], rhs=Wd[:, f, c * 512:(c + 1) * 512],
                                                                 start=(f == 0), stop=(f == 21)), [AT, Wd], [PM[c]])
                h2 = h2_ring.next()
                self.post_norm_res(cx, PM, gs["post_ffn_g"], h1, h2, sm_ring, junk)
                if not final:
                    if t == NT - 1:
                        S.op("dve", lambda e, h2=h2: e.tensor_scalar(out=h2[:], in0=h2[:], scalar1=vmask[:, 0:1], scalar2=None, op0=ALU.mult),
                             [h2, vmask], [h2])
                    cx.store(hout_d, hout_d[c0:c0 + 128, :], h2, h2[:])
                else:
                    lo = 16 if t == 0 else 0
                    hi = seq.rows(t)
                    cx.store(hout_d, hout_d[c0 + lo - 16:c0 + hi - 16, :], h2, h2[lo:hi, :])

            for t in range(NT):
                body(t)


    def phase_odd_proj(self, seq):
        nc, S, W, K, sc = self.nc, self.S, self.W, self.K, self.sc
        NT, Lp = seq.NT, seq.Lp
        with Ctx(nc, S) as cx:
            stage = cx.ring("wst", 3, [128, 512], F32)
            Win = cx.sb("win", [128, 8, 2688], BF16)
            self.load_w(cx, Win, W["odd_w_in"], W["odd_w_in"][0], 8, 2688, stage)
            gpre = cx.sb("gpre", [128, D], F32)
            cx.load(gpre, gpre[:], W["pre_mix_g"], W["pre_mix_g"][1:2, :].partition_broadcast(128))
            zc = cx.sb("zc", [128, 15], F32)
            S.op("dve", lambda e: e.memset(zc[:], 0.0), [], [zc])
            UTv = sc["UT"].t.rearrange("(b p) n -> p b n", p=128)
            cx.store(sc["UT"], UTv[:, :, 0:1], zc, zc[:].unsqueeze(2), slow=True)
            cx.store(sc["UT"], UTv[:, :, Lp + 1:Lp + 2], zc, zc[:].unsqueeze(2), slow=True)
            xring = cx.ring("x", 2, [128, D], F32)
            xs_ring = cx.ring("xs", 2, [128, D], BF16)
            sm_ring = cx.ring("sm", 2, [128, 4], F32)
            junk = cx.sb("junk", [128, D], BF16)
            TP = cx.ps("TP", [128, 8, 128], BF16)
            hnT_ring = cx.ring("hnT", 2, [128, 8, 128], BF16)
            PJ = [cx.ps("PJ%d" % c, [128, 512], F32) for c in range(2)]
            PU = cx.ring("PU", 2, [128, 4, 128], F32, ps=True)
            rope_ring = cx.ring("rope", 2, [128, 2, 8], F32)
            sqt = cx.sb("sqt", [128, 8, 64], F32)
            bq_ring = cx.ring("bq", 2, [128, 2, 8], F32)
            qc = cx.sb("qc", [128, 10, 64], F32)
            tA = cx.sb("tA", [128, 10, 8], F32)
            tB = cx.sb("tB", [128, 10, 8], F32)
            qa_ring = cx.ring("qa", 2, [128, 10, 65], BF16)
            for qa in qa_ring.tiles:
                S.op("pool", lambda e, qa=qa: e.memset(qa[:], 1.0), [], [qa])
            vc_ring = cx.ring("vc", 2, [128, 2, 65], BF16)
            for vc in vc_ring.tiles:
                S.op("pool", lambda e, vc=vc: e.memset(vc[:], 1.0), [], [vc])
            TP2 = cx.ps("TP2", [128, 5, 128], BF16)
            TP3 = cx.ps("TP3", [128, 5, 128], BF16)
            qT_ring = cx.ring("qT", 2, [65, 10, 128], BF16)
            u_ring = cx.ring("usb", 2, [128, 15, 128], F32)

            def body(t):
                c0 = t * 128
                xt = xring.next()
                cx.load(xt, xt[:], sc["H1"], sc["H1"][c0:c0 + 128, :])
                rp = rope_ring.next()
                cx.load(rp, rp[:], seq.c["ropeC"], seq.c["ropeC"][c0:c0 + 128])
                hnT = hnT_ring.next()
                self.norm_transpose(cx, xt, gpre, hnT, TP, xs_ring, sm_ring, junk)
                for c, (lo, wdt) in enumerate(((0, 512), (512, 256))):
                    for k in range(8):
                        S.op("pe", lambda e, c=c, k=k, lo=lo, wdt=wdt: e.matmul(PJ[c][:, 0:wdt], lhsT=hnT[:, k, :], rhs=Win[:, k, lo:lo + wdt],
                                                                                 start=(k == 0), stop=(k == 7)), [hnT, Win], [PJ[c]])
                usb = u_ring.next()
                for q4 in range(4):
                    nb = min(4, 15 - q4 * 4)
                    pu = PU.next()
                    for b in range(nb):
                        f0 = 768 + (q4 * 4 + b) * 128
                        for k in range(8):
                            S.op("pe", lambda e, pu=pu, b=b, f0=f0, k=k: e.matmul(pu[:, b, :], lhsT=Win[:, k, f0:f0 + 128], rhs=hnT[:, k, :],
                                                                                  start=(k == 0), stop=(k == 7)), [Win, hnT], [pu])
                    S.op("act" if q4 % 2 else "dve", (lambda e, pu=pu, q4=q4, nb=nb: e.copy(out=usb[:, q4 * 4:q4 * 4 + nb, :], in_=pu[:, 0:nb, :])) if q4 % 2 else
                         (lambda e, pu=pu, q4=q4, nb=nb: e.tensor_copy(out=usb[:, q4 * 4:q4 * 4 + nb, :], in_=pu[:, 0:nb, :])), [pu], [usb])
                cx.store(sc["UT"], UTv[:, :, 1 + c0:1 + c0 + 128], usb, usb[:])
                S.op("act", lambda e: e.activation(out=sqt[:], in_=PJ[0][:].rearrange("p (h d) -> p h d", d=64), func=AF.Square), [PJ[0]], [sqt])
                bq = bq_ring.next()
                S.op("dve", lambda e: e.tensor_reduce(out=bq[:, 0, :], in_=sqt[:], axis=AX.X, op=ALU.add), [sqt], [bq])
                S.op("dve", lambda e: e.tensor_scalar(out=bq[:, 1, :], in0=bq[:, 0, :], scalar1=1.0 / 16, scalar2=None, op0=ALU.mult), [bq], [bq])
                cx.store(sc["BQ"], sc["BQ"][c0:c0 + 128, :], bq, bq[:, 1, :])
                S.op("act", lambda e: e.activation(out=qc[:, 0:8, :], in_=PJ[0][:].rearrange("p (h d) -> p h d", d=64), func=AF.Copy, scale=0.125), [PJ[0]], [qc])
                S.op("act", lambda e: e.copy(out=qc[:, 8:10, :], in_=PJ[1][:, 0:128].rearrange("p (h d) -> p h d", d=64)), [PJ[1]], [qc])
                qa = qa_ring.next()
                x1, x2 = qc[:, :, 0:8], qc[:, :, 8:16]
                cosb = rp[:, 0, :].unsqueeze(1).broadcast_to([128, 10, 8])
                sinb = rp[:, 1, :].unsqueeze(1).broadcast_to([128, 10, 8])
                S.op("dve", lambda e: e.tensor_tensor(out=tA[:], in0=x1, in1=cosb, op=ALU.mult), [qc, rp], [tA])
                S.op("pool", lambda e: e.tensor_tensor(out=tB[:], in0=x2, in1=sinb, op=ALU.mult), [qc, rp], [tB])
                S.op("dve", lambda e: e.tensor_tensor(out=qa[:, :, 0:8], in0=tA[:], in1=tB[:], op=ALU.subtract), [tA, tB], [qa])
                S.op("dve", lambda e: e.tensor_tensor(out=tA[:], in0=x1, in1=sinb, op=ALU.mult), [qc, rp], [tA])
                S.op("pool", lambda e: e.tensor_tensor(out=tB[:], in0=x2, in1=cosb, op=ALU.mult), [qc, rp], [tB])
                S.op("dve", lambda e: e.tensor_tensor(out=qa[:, :, 8:16], in0=tA[:], in1=tB[:], op=ALU.add), [tA, tB], [qa])
                S.op("act", lambda e: e.copy(out=qa[:, :, 16:64], in_=qc[:, :, 16:64]), [qc], [qa])
                S.op("dve", lambda e: e.tensor_scalar(out=qa[:, 0:8, 64], in0=bq[:, 1, :], scalar1=-1.0, scalar2=None, op0=ALU.mult), [bq], [qa])
                for hh in range(10):
                    tp = TP2 if hh < 5 else TP3
                    S.op("pe", lambda e, hh=hh, tp=tp: e.transpose(out=tp[0:65, hh % 5, :], in_=qa[:, hh, :], identity=K["identb"][:]),
                         [qa, K["identb"]], [tp])
                qT = qT_ring.next()
                S.op("act", lambda e: e.copy(out=qT[:, 0:5, :], in_=TP2[0:65, :, :]), [TP2], [qT])
                S.op("dve", lambda e: e.tensor_copy(out=qT[:, 5:10, :], in_=TP3[0:65, :, :]), [TP3], [qT])
                cx.store(sc["QCT"], sc["QCT"][:, :, c0:c0 + 128].rearrange("h d n -> d h n"), qT, qT[:, 0:8, :])
                cx.store(sc["KCT"], sc["KCT"][:, :, c0:c0 + 128].rearrange("h d n -> d h n"), qT, qT[:, 8:10, :])
                vc = vc_ring.next()
                S.op("act", lambda e: e.copy(out=vc[:, :, 0:64], in_=PJ[1][:, 128:256].rearrange("p (h d) -> p h d", d=64)), [PJ[1]], [vc])
                cx.store(sc["VCX"], sc["VCX"][c0:c0 + 128, :, :], vc, vc[:])

            for t in range(NT):
                body(t)

    def phase_attn_c(self, seq):
        nc, S, W, K, sc = self.nc, self.S, self.W, self.K, self.sc
        NT, Lp = seq.NT, seq.Lp
        with Ctx(nc, S) as cx:
            KCs = cx.sb("KCs", [65, 2, Lp], BF16)
            cx.load(KCs, KCs[:], sc["KCT"], sc["KCT"][:, :, 0:Lp].rearrange("h d n -> d h n"))
            VCs = cx.sb("VCs", [128, NT, 2, 65], BF16)
            for t0 in range(0, NT, 16):
                t1_ = min(NT, t0 + 16)
                cx.load(VCs, VCs[:, t0:t1_], sc["VCX"], sc["VCX"][t0 * 128:t1_ * 128].rearrange("(t p) h d -> p t h d", p=128))
            cm = cx.sb("cm", [128, 3, 128], F32)
            cx.load(cm, cm[:], self.G["cmask"], self.G["cmask"][:, :, :])
            cmb = cx.sb("cmb", [128, 3, 128], BF16)
            S.op("dve", lambda e: e.tensor_copy(out=cmb[:], in_=cm[:]), [cm], [cmb])
            snk = cx.sb("snk", [128, 8], F32)
            cx.load(snk, snk[:], W["c_sink"], W["c_sink"][0:1, :].partition_broadcast(128))
            q_ring = cx.ring("qc", 2, [65, 8, 128], BF16)
            bq_ring = cx.ring("bq", 2, [128, 8], F32)
            es_ring = cx.ring("es", 2, [128, 8], F32)
            ST = cx.ring("ST", 2, [128, 4, 128], F32, ps=True)
            OT = cx.ring("OT", 2, [65, 4, 128], F32, ps=True)
            PT = cx.ring("PT", 3, [128, 4, 128], BF16)
            osb_ring = cx.ring("osb", 2, [65, 4, 128], F32)
            TPo = cx.ps("TPo", [128, 4, 65], F32)
            den_ring = cx.ring("den", 2, [128, 2, 4], F32)
            y_ring = cx.ring("yc", 2, [128, 8, 64], BF16)

            def body(t):
                c0 = t * 128
                qt = q_ring.next()
                cx.load(qt, qt[:], sc["QCT"], sc["QCT"][:, :, c0:c0 + 128].rearrange("h d n -> d h n"))
                bq = bq_ring.next()
                cx.load(bq, bq[:], sc["BQ"], sc["BQ"][c0:c0 + 128, :])
                es = es_ring.next()
                S.op("dve", lambda e: e.tensor_tensor(out=es[:], in0=snk[:], in1=bq[:], op=ALU.subtract), [snk, bq], [es])
                S.op("act", lambda e: e.activation(out=es[:], in_=es[:], func=AF.Exp), [es], [es])
                yc = y_ring.next()
                kbs = []
                if t >= 2:
                    kbs.append((0, 16, None))
                if t >= 1:
                    kbs.append((t - 1, 128, 2 if t == 1 else 0))
                kbs.append((t, seq.rows(t), None))
                if t + 1 < NT:
                    kbs.append((t + 1, seq.rows(t + 1), 1))
                def grp(g):
                    ot = OT.next()
                    for bi, (kt, r, mk) in enumerate(kbs):
                        st = ST.next()
                        S.op("pe", lambda e, st=st, kt=kt, r=r, g=g: e.matmul(st[0:r], lhsT=KCs[:, g, kt * 128:kt * 128 + r], rhs=qt[:, g * 4:(g + 1) * 4, :],
                                                                               start=True, stop=True), [KCs, qt], [st])
                        pt = PT.next()
                        S.op("act", lambda e, st=st, pt=pt, r=r: e.activation(out=pt[0:r], in_=st[0:r], func=AF.Exp), [st], [pt])
                        if mk is not None:
                            S.op("dve", lambda e, pt=pt, r=r, mk=mk: e.tensor_tensor(out=pt[0:r], in0=pt[0:r],
                                                                                     in1=cmb[0:r, mk, :].unsqueeze(1).broadcast_to([r, 4, 128]), op=ALU.mult),
                                 [pt, cmb], [pt])
                        S.op("pe", lambda e, pt=pt, kt=kt, r=r, g=g, bi=bi: e.matmul(ot[:], lhsT=VCs[0:r, kt, g, :], rhs=pt[0:r], start=(bi == 0),
                                                                                      stop=(bi == len(kbs) - 1)), [VCs, pt], [ot])
                    osb = osb_ring.next()
                    S.op("dve", lambda e, osb=osb, ot=ot: e.tensor_copy(out=osb[:], in_=ot[:]), [ot], [osb])
                    for hh in range(4):
                        S.op("pe", lambda e, hh=hh, osb=osb: e.transpose(out=TPo[:, hh, :], in_=osb[:, hh, :], identity=K["identf"][0:65, 0:65]),
                             [osb, K["identf"]], [TPo])
                    den = den_ring.next()
                    S.op("dve", lambda e, den=den, g=g: e.tensor_tensor(out=den[:, 0, :], in0=TPo[:, :, 64], in1=es[:, g * 4:(g + 1) * 4], op=ALU.add),
                         [TPo, es], [den])
                    S.op("dve", lambda e, den=den: e.reciprocal(out=den[:, 1, :], in_=den[:, 0, :]), [den], [den])
                    S.op("dve", lambda e, den=den, g=g: e.tensor_tensor(out=yc[:, g * 4:(g + 1) * 4, :], in0=TPo[:, :, 0:64],
                                                                        in1=den[:, 1, :].unsqueeze(2).broadcast_to([128, 4, 64]), op=ALU.mult),
                         [TPo, den], [yc])
                for g in range(2):
                    grp(g)
                cx.store(sc["YC"], sc["YC"][c0:c0 + 128, :], yc, yc[:].rearrange("p h d -> p (h d)"))

            for t in range(NT):
                body(t)

    def phase_rwkv(self, seq, d):
        nc, S, W, K, sc, G = self.nc, self.S, self.W, self.K, self.sc, self.G
        NT, Lp, L, NB = seq.NT, seq.Lp, seq.L, seq.NB
        NEG = -math.exp(-0.5)
        with Ctx(nc, S) as cx:
            m4 = cx.sb("m4", [64, 4, 64], F32)
            cx.load(m4, m4[:], G["m4"], G["m4"][:, :, :])
            reset = cx.sb("reset", [64, 512], F32)
            cx.load(reset, reset[:], G["reset"], G["reset"][:, :])
            ones64 = cx.sb("ones64", [64, 64], F32)
            S.op("dve", lambda e: e.memset(ones64[:], 1.0), [], [ones64])
            onesb = cx.sb("onesb", [64, 2], BF16)
            S.op("dve", lambda e: e.memset(onesb[:], 1.0), [], [onesb])
            lneps = cx.sb("lneps", [64, 1], F32)
            S.op("dve", lambda e: e.memset(lneps[:], 64e-5), [], [lneps])
            zero1 = cx.sb("zero1", [64, 1], F32)
            S.op("dve", lambda e: e.memset(zero1[:], 0.0), [], [zero1])
            mu = cx.sb("mu", [64, 3, 24], F32)
            cx.load(mu, mu[:, 0, :], W["d_mu_prev"], W["d_mu_prev"][0, 0:1536].rearrange("(s p) -> p s", p=64))
            cx.load(mu, mu[:, 1, :], W["d_mu_next"], W["d_mu_next"][0, 0:1536].rearrange("(s p) -> p s", p=64))
            muL = cx.sb("muL", [128, 3, 3], F32)
            cx.load(muL, muL[:, 0, :], W["d_mu_prev"], W["d_mu_prev"][0, 1536:1920].rearrange("(s p) -> p s", p=128))
            cx.load(muL, muL[:, 1, :], W["d_mu_next"], W["d_mu_next"][0, 1536:1920].rearrange("(s p) -> p s", p=128))
            for m_ in (mu, muL):
                S.op("dve", lambda e, m_=m_: e.tensor_tensor(out=m_[:, 2, :], in0=m_[:, 0, :], in1=m_[:, 1, :], op=ALU.add), [m_], [m_])
                S.op("dve", lambda e, m_=m_: e.tensor_scalar(out=m_[:, 2, :], in0=m_[:, 2, :], scalar1=-1.0, scalar2=1.0, op0=ALU.mult, op1=ALU.add), [m_], [m_])
            pc = cx.sb("pc", [64, 8, 8], F32)
            cx.load(pc, pc[:, 0:2, :], W["d_w0"], W["d_w0"][0].rearrange("d (h p) -> p d h", p=64))
            cx.load(pc, pc[:, 2:4, :], W["d_a0"], W["d_a0"][0].rearrange("d (h p) -> p d h", p=64))
            cx.load(pc, pc[:, 4, :], W["d_k_k"], W["d_k_k"][0].rearrange("(h p) -> p h", p=64))
            cx.load(pc, pc[:, 5, :], W["d_k_a"], W["d_k_a"][0].rearrange("(h p) -> p h", p=64))
            cx.load(pc, pc[:, 7, :], W["d_r_k"], W["d_r_k"][0].rearrange("h p -> p h"))
            S.op("dve", lambda e: e.tensor_scalar(out=pc[:, 6, :], in0=pc[:, 5, :], scalar1=-1.0, scalar2=1.0, op0=ALU.mult, op1=ALU.add), [pc], [pc])
            stg = cx.sb("stg", [128, 512], F32)
            wup = cx.sb("wup", [128, 512], BF16)
            aup = cx.sb("aup", [128, 512], BF16)
            gup = cx.sb("gup", [128, 512], BF16)
            for (dst, srcs) in ((wup, [(W["d_w_up"], W["d_w_up"][0, 0]), (W["d_w_up"], W["d_w_up"][0, 1])]),
                                (aup, [(W["d_a_up"], W["d_a_up"][0, 0]), (W["d_a_up"], W["d_a_up"][0, 1])]),
                                (gup, [(W["d_g_up"], W["d_g_up"][0, 0:64]), (W["d_g_up"], W["d_g_up"][0, 64:128])])):
                for i_, (tl_, ap_) in enumerate(srcs):
                    cx.load(stg, stg[i_ * 64:(i_ + 1) * 64, :], tl_, ap_)
                S.op("dve", lambda e, dst=dst: e.tensor_copy(out=dst[:], in_=stg[:]), [stg], [dst])
            lng = cx.sb("lng", [64, 512], F32)
            lnb = cx.sb("lnb", [64, 512], F32)
            cx.load(lng, lng[:], W["d_ln_g"], W["d_ln_g"][0:1, :].partition_broadcast(64))
            cx.load(lnb, lnb[:], W["d_ln_b"], W["d_ln_b"][0:1, :].partition_broadcast(64))
            idb64 = K["identb"][0:64, 0:64]
            idf64 = K["identf"][0:64, 0:64].unsqueeze(1).broadcast_to([64, 8, 64])

            TPa = cx.ps("TPa", [64, 2, 8, 64], BF16)
            TPb = cx.ps("TPb", [64, 2, 8, 64], BF16)
            GP = cx.ring("GP", 5, [64, 8, 64], F32, ps=True)
            PP = cx.ps("PP", [64, 512], F32)

            TW = cx.sb("TW", [128, 512], BF16)
            AW = cx.sb("AW", [128, 512], BF16)
            SG = cx.sb("SG", [128, 512], BF16)
            ubL = cx.sb("ubL", [128, 514], F32)
            mlt = cx.sb("mlt", [128, 512], F32)
            ub3 = cx.sb("ub3", [64, 3, 514], F32)
            f32t = {n: cx.sb(n, [64, 512], F32) for n in ("rr", "kr", "vv", "lw", "cu", "e1", "e2", "e3", "aa", "kn", "t1", "kd", "bb", "a0t")}
            RT = cx.sb("RT", [64, 8, 512], BF16)
            KT = cx.sb("KT", [64, 8, 512], BF16)
            BT = cx.sb("BT", [64, 8, 512], BF16)
            AT = cx.sb("AT", [64, 8, 512], BF16)
            VT = cx.sb("VT", [64, 8, 512], BF16)
            RKD = cx.sb("RKD", [64, 8, 512], BF16)
            GE = cx.sb("GE", [64, 8, 8], F32)
            Mf = cx.sb("Mf", [64, 8, 64], F32)
            Mb = cx.sb("Mb", [64, 8, 64], BF16)
            S.op("dve", lambda e: e.memset(Mf[:], 0.0), [], [Mf])
            S.op("dve", lambda e: e.memset(Mb[:], 0.0), [], [Mb])
            tm_ring = cx.ring("tm", 2, [64, 4, 8, 64], BF16)
            bt = lambda nm, n=2: cx.ring(nm, n, [64, 8, 64], BF16)
            Yr, P0r, Lakr, Lrbr, Lrkr = bt("Y"), bt("P0"), bt("Lak"), bt("Lrb"), bt("Lrk")
            ybuf, pbuf, rbuf = bt("yb", 2), bt("pb", 2), bt("rb", 3)
            Xr, Wr, Ur = bt("X"), bt("WT"), bt("U")
            o_ring = cx.ring("o", 2, [64, 8, 64], F32)
            of_ring = cx.ring("of", 2, [64, 8, 64], F32)
            cen = cx.sb("cen", [64, 8, 64], F32)
            sqn = cx.sb("sqn", [64, 8, 64], F32)
            st_ring = cx.ring("st", 2, [64, 6, 8], F32)
            yd_ring = cx.ring("yd", 2, [64, 8, 64], BF16)
            UT = sc["UT"]
            UT3 = UT.t[0:1536, :].rearrange("(s q) n -> q s n", q=512)
            strict, incl, strictT = (0, 1, 2) if d == 0 else (2, 3, 0)
            bc8 = lambda ap: ap.unsqueeze(2).broadcast_to([64, 8, 64])

            def mix(dst_ap, ub_prev, ub_c, ub_next, mu_t, idx, dst_tl, ub_tl):
                S.op("dve", lambda e: e.tensor_scalar(out=dst_ap, in0=ub_c, scalar1=mu_t[:, 2, idx:idx + 1], scalar2=None, op0=ALU.mult), [ub_tl, mu_t], [dst_tl])
                S.op("dve", lambda e: e.scalar_tensor_tensor(out=dst_ap, in0=ub_prev, scalar=mu_t[:, 0, idx:idx + 1], in1=dst_ap, op0=ALU.mult, op1=ALU.add),
                     [ub_tl, mu_t, dst_tl], [dst_tl])
                S.op("dve", lambda e: e.scalar_tensor_tensor(out=dst_ap, in0=ub_next, scalar=mu_t[:, 1, idx:idx + 1], in1=dst_ap, op0=ALU.mult, op1=ALU.add),
                     [ub_tl, mu_t, dst_tl], [dst_tl])

            def prep(bi):
                n0 = bi * 512
                Wd = min(512, Lp - n0)
                nch = Wd // 64
                pv = min(Wd, max(0, L - n0))
                for li, (dst, fn_) in enumerate(((TW, AF.Tanh), (AW, AF.Copy), (SG, AF.Sigmoid))):
                    cx.load(ubL, ubL[:, 0:Wd + 2], UT, UT[1536 + li * 128:1536 + (li + 1) * 128, n0:n0 + Wd + 2])
                    mix(mlt[:, 0:Wd], ubL[:, 0:Wd], ubL[:, 1:Wd + 1], ubL[:, 2:Wd + 2], muL, li, mlt, ubL)
                    S.op("act", lambda e, dst=dst, fn_=fn_: e.activation(out=dst[:, 0:Wd], in_=mlt[:, 0:Wd], func=fn_), [mlt], [dst])
                for h in range(8):
                    prep_head(h, n0, Wd, nch, pv)

            def prep_head(h, n0, Wd, nch, pv):
                T = f32t
                w = slice(0, Wd)
                cx.load(ub3, ub3[:, :, 0:Wd + 2], UT, UT3[h * 64:(h + 1) * 64, :, n0:n0 + Wd + 2])
                for sec, nm in ((0, "rr"), (1, "kr"), (2, "vv")):
                    mix(T[nm][:, w], ub3[:, sec, 0:Wd], ub3[:, sec, 1:Wd + 1], ub3[:, sec, 2:Wd + 2], mu, sec * 8 + h, T[nm], ub3)
                    if pv < Wd:
                        S.op("pool", lambda e, nm=nm: e.memset(T[nm][:, pv:Wd], 0.0), [], [T[nm]])
                hs = slice(h * 64, (h + 1) * 64)
                ds_ = slice(d * 64, (d + 1) * 64)
                S.op("pe", lambda e: e.matmul(PP[:, w], lhsT=wup[ds_, hs], rhs=TW[ds_, w], start=True, stop=True), [wup, TW], [PP])
                S.op("act", lambda e: e.activation(out=T["lw"][:, w], in_=PP[:, w], func=AF.Sigmoid, bias=pc[:, d, h:h + 1], scale=1.0), [PP, pc], [T["lw"]])
                S.op("dve", lambda e: e.tensor_scalar(out=T["lw"][:, w], in0=T["lw"][:, w], scalar1=NEG, scalar2=None, op0=ALU.mult), [T["lw"]], [T["lw"]])
                S.op("dve", lambda e: e.tensor_tensor_scan(out=T["cu"][:, w], data0=reset[:, w], data1=T["lw"][:, w], initial=0.0, op0=ALU.mult, op1=ALU.add),
                     [reset, T["lw"]], [T["cu"]])
                if d == 1:
                    cu3 = T["cu"][:, w].rearrange("p (c t) -> p c t", t=64)
                    S.op("dve", lambda e: e.tensor_tensor(out=T["t1"][:, w], in0=T["lw"][:, w], in1=T["cu"][:, w], op=ALU.subtract), [T["lw"], T["cu"]], [T["t1"]])
                    S.op("dve", lambda e: e.tensor_tensor(out=T["e1"][:, w].rearrange("p (c t) -> p c t", t=64), in0=T["t1"][:, w].rearrange("p (c t) -> p c t", t=64),
                                                          in1=cu3[:, :, 63:64].broadcast_to([64, nch, 64]), op=ALU.add), [T["t1"], T["cu"]], [T["e1"]])
                    S.op("dve", lambda e: e.tensor_copy(out=T["cu"][:, w], in_=T["e1"][:, w]), [T["e1"]], [T["cu"]])
                S.op("act", lambda e: e.activation(out=T["e1"][:, w], in_=T["cu"][:, w], func=AF.Exp), [T["cu"]], [T["e1"]])
                S.op("act", lambda e: e.activation(out=T["e2"][:, w], in_=T["cu"][:, w], func=AF.Exp, scale=-1.0), [T["cu"]], [T["e2"]])
                S.op("dve", lambda e: e.tensor_tensor(out=T["t1"][:, w], in0=T["cu"][:, w], in1=T["lw"][:, w], op=ALU.subtract), [T["cu"], T["lw"]], [T["t1"]])
                S.op("act", lambda e: e.activation(out=T["e3"][:, w], in_=T["t1"][:, w], func=AF.Exp), [T["t1"]], [T["e3"]])
                S.op("pe", lambda e: e.matmul(PP[:, w], lhsT=aup[ds_, hs], rhs=AW[ds_, w], start=True, stop=True), [aup, AW], [PP])
                S.op("act", lambda e: e.activation(out=T["aa"][:, w], in_=PP[:, w], func=AF.Sigmoid, bias=pc[:, 2 + d, h:h + 1], scale=1.0), [PP, pc], [T["aa"]])
                S.op("dve", lambda e: e.tensor_scalar(out=T["kn"][:, w], in0=T["kr"][:, w], scalar1=pc[:, 4, h:h + 1], scalar2=None, op0=ALU.mult), [T["kr"], pc], [T["kn"]])
                S.op("act", lambda e: e.activation(out=T["t1"][:, w], in_=T["kn"][:, w], func=AF.Square), [T["kn"]], [T["t1"]])
                S.op("pe", lambda e: e.matmul(PP[:, w], lhsT=ones64[:], rhs=T["t1"][:, w], start=True, stop=True), [ones64, T["t1"]], [PP])
                S.op("dve", lambda e: e.tensor_scalar(out=T["t1"][:, w], in0=PP[:, w], scalar1=1e-24, scalar2=None, op0=ALU.max), [PP], [T["t1"]])
                S.op("act", lambda e: e.activation(out=T["t1"][:, w], in_=T["t1"][:, w], func=AF.Sqrt, bias=zero1[:, 0:1], scale=1.0), [T["t1"], zero1], [T["t1"]])
                S.op("dve", lambda e: e.reciprocal(out=T["t1"][:, w], in_=T["t1"][:, w]), [T["t1"]], [T["t1"]])
                S.op("dve", lambda e: e.tensor_tensor(out=T["kn"][:, w], in0=T["kn"][:, w], in1=T["t1"][:, w], op=ALU.mult), [T["kn"], T["t1"]], [T["kn"]])
                S.op("dve", lambda e: e.tensor_scalar(out=T["kd"][:, w], in0=T["aa"][:, w], scalar1=pc[:, 5, h:h + 1], scalar2=pc[:, 6, h:h + 1], op0=ALU.mult, op1=ALU.add),
                     [T["aa"], pc], [T["kd"]])
                S.op("dve", lambda e: e.tensor_tensor(out=T["kd"][:, w], in0=T["kd"][:, w], in1=T["kr"][:, w], op=ALU.mult), [T["kd"], T["kr"]], [T["kd"]])
                S.op("pool", lambda e: e.tensor_tensor(out=T["bb"][:, w], in0=T["kn"][:, w], in1=T["aa"][:, w], op=ALU.mult), [T["kn"], T["aa"]], [T["bb"]])
                S.op("dve", lambda e: e.tensor_tensor(out=RT[:, h, w], in0=T["rr"][:, w], in1=T["e1"][:, w], op=ALU.mult), [T["rr"], T["e1"]], [RT])
                S.op("pool", lambda e: e.tensor_tensor(out=KT[:, h, w], in0=T["kd"][:, w], in1=T["e2"][:, w], op=ALU.mult), [T["kd"], T["e2"]], [KT])
                S.op("pool", lambda e: e.tensor_tensor(out=BT[:, h, w], in0=T["bb"][:, w], in1=T["e2"][:, w], op=ALU.mult), [T["bb"], T["e2"]], [BT])
                S.op("dve", lambda e: e.scalar_tensor_tensor(out=AT[:, h, w], in0=T["kn"][:, w], scalar=-1.0, in1=T["e3"][:, w], op0=ALU.mult, op1=ALU.mult),
                     [T["kn"], T["e3"]], [AT])
                S.op("act", lambda e: e.copy(out=VT[:, h, w], in_=T["vv"][:, w]), [T["vv"]], [VT])
                e13 = T["e1"][:, w].rearrange("p (c t) -> p c t", t=64)
                gcol = 63 if d == 0 else 0
                S.op("dve", lambda e: e.tensor_copy(out=GE[:, 0:nch, h], in_=e13[:, :, gcol]), [T["e1"]], [GE])
                if d == 1:
                    do = slice(0, 64)
                    S.op("pe", lambda e: e.matmul(PP[:, w], lhsT=aup[do, hs], rhs=AW[do, w], start=True, stop=True), [aup, AW], [PP])
                    S.op("act", lambda e: e.activation(out=T["a0t"][:, w], in_=PP[:, w], func=AF.Sigmoid, bias=pc[:, 2, h:h + 1], scale=1.0), [PP, pc], [T["a0t"]])
                    S.op("dve", lambda e: e.tensor_scalar(out=T["a0t"][:, w], in0=T["a0t"][:, w], scalar1=pc[:, 5, h:h + 1], scalar2=pc[:, 6, h:h + 1], op0=ALU.mult, op1=ALU.add),
                         [T["a0t"], pc], [T["a0t"]])
                    S.op("dve", lambda e: e.tensor_tensor(out=T["a0t"][:, w], in0=T["a0t"][:, w], in1=T["kr"][:, w], op=ALU.mult), [T["a0t"], T["kr"]], [T["a0t"]])
                    S.op("dve", lambda e: e.tensor_tensor(out=T["a0t"][:, w], in0=T["a0t"][:, w], in1=T["kd"][:, w], op=ALU.add), [T["a0t"], T["kd"]], [T["a0t"]])
                    S.op("dve", lambda e: e.scalar_tensor_tensor(out=RKD[:, h, w], in0=T["rr"][:, w], scalar=pc[:, 7, h:h + 1], in1=T["a0t"][:, w], op0=ALU.mult, op1=ALU.mult),
                         [T["rr"], pc, T["a0t"]], [RKD])

            def mm8(ps, lhs_fn, rhs_fn, lt, rt, extra=()):
                for h in range(8):
                    terms = [(lhs_fn, rhs_fn)] + list(extra)
                    for i_, (lf, rf) in enumerate(terms):
                        S.op("pe", lambda e, h=h, lf=lf, rf=rf, i_=i_, n_=len(terms): e.matmul(ps[:, h, :], lhsT=lf(h), rhs=rf(h), start=(i_ == 0), stop=(i_ == n_ - 1)),
                             lt, [ps])

            def chunk(n0, c):
                cs = slice(c * 64, (c + 1) * 64)
                row0 = n0 + c * 64
                for ki, X in enumerate((AT, BT)):
                    for h in range(8):
                        S.op("pe", lambda e, ki=ki, X=X, h=h: e.transpose(out=TPa[:, ki, h, :], in_=X[:, h, cs], identity=idb64), [X, K["identb"]], [TPa])
                for ki, X in enumerate((KT, VT)):
                    for h in range(8):
                        S.op("pe", lambda e, ki=ki, X=X, h=h: e.transpose(out=TPb[:, ki, h, :], in_=X[:, h, cs], identity=idb64), [X, K["identb"]], [TPb])
                tm = tm_ring.next()
                S.op("act", lambda e: e.copy(out=tm[:, 0:2], in_=TPa[:]), [TPa], [tm])
                S.op("dve", lambda e: e.tensor_copy(out=tm[:, 2:4], in_=TPb[:]), [TPb], [tm])
                A_tm, B_tm, K_tm, V_tm = (lambda h: tm[:, 0, h, :]), (lambda h: tm[:, 1, h, :]), (lambda h: tm[:, 2, h, :]), (lambda h: tm[:, 3, h, :])

                def score(Lh, Rh, mi, ring):
                    ps = GP.next()
                    mm8(ps, lambda h: Lh[:, h, cs], lambda h: Rh[:, h, cs], [Lh, Rh], None)
                    o = ring.next()
                    S.op("dve", lambda e: e.tensor_tensor(out=o[:], in0=ps[:], in1=m4[:, mi, :].unsqueeze(1).broadcast_to([64, 8, 64]), op=ALU.mult), [ps, m4], [o])
                    return o
                Y = score(BT, AT, strict, Yr)
                P0 = score(AT, BT, strictT, P0r)
                LakT = score(KT, AT, strict, Lakr)
                LrbT = score(BT, RT, incl, Lrbr)
                LrkT = score(KT, RT, incl, Lrkr)
                R = rbuf.next()
                S.op("dve", lambda e, R=R: e.tensor_tensor(out=R[:], in0=Y[:], in1=idf64, op=ALU.add), [Y, K["identf"]], [R])
                Yk, Pk = Y, P0
                for lvl in range(5):
                    psP = GP.next()
                    mm8(psP, lambda h, Yk=Yk: Yk[:, h, :], lambda h, Pk=Pk: Pk[:, h, :], [Yk, Pk], None)
                    Pn = pbuf.next()
                    S.op("dve", lambda e, Pn=Pn, psP=psP: e.tensor_copy(out=Pn[:], in_=psP[:]), [psP], [Pn])
                    if lvl < 4:
                        psY = GP.next()
                        mm8(psY, lambda h, Pk=Pk: Pk[:, h, :], lambda h, Yk=Yk: Yk[:, h, :], [Yk, Pk], None)
                        Yn = ybuf.next()
                        S.op("act", lambda e, Yn=Yn, psY=psY: e.copy(out=Yn[:], in_=psY[:]), [psY], [Yn])
                    psR = GP.next()
                    mm8(psR, lambda h, Pn=Pn: Pn[:, h, :], lambda h, R=R: R[:, h, :], [Pn, R], None)
                    Rn = rbuf.next()
                    S.op("dve", lambda e, Rn=Rn, psR=psR, R=R: e.tensor_tensor(out=Rn[:], in0=psR[:], in1=R[:], op=ALU.add), [psR, R], [Rn])
                    R = Rn
                    Yk, Pk = (Yn if lvl < 4 else Yk), Pn
                psX = GP.next()
                mm8(psX, lambda h: LakT[:, h, :], V_tm, [LakT, tm], None)
                Xs = Xr.next()
                S.op("act", lambda e: e.copy(out=Xs[:], in_=psX[:]), [psX], [Xs])
                psW = GP.next()
                mm8(psW, A_tm, lambda h: R[:, h, :], [tm, R], None)
                WTs = Wr.next()
                S.op("dve", lambda e: e.tensor_copy(out=WTs[:], in_=psW[:]), [psW], [WTs])
                psU = GP.next()
                mm8(psU, lambda h: WTs[:, h, :], lambda h: Mb[:, h, :], [WTs, Mb, R, Xs], None, extra=[(lambda h: R[:, h, :], lambda h: Xs[:, h, :])])
                Us = Ur.next()
                S.op("act", lambda e: e.copy(out=Us[:], in_=psU[:]), [psU], [Us])
                psO = GP.next()
                mm8(psO, lambda h: RT[:, h, cs], lambda h: Mb[:, h, :], [RT, Mb, LrbT, Us, LrkT, tm], None,
                    extra=[(lambda h: LrbT[:, h, :], lambda h: Us[:, h, :]), (lambda h: LrkT[:, h, :], V_tm)])
                psM = GP.next()
                mm8(psM, B_tm, lambda h: Us[:, h, :], [tm, Us], None, extra=[(K_tm, V_tm)])
                ob = o_ring.next()
                if d == 0:
                    S.op("act", lambda e: e.copy(out=ob[:], in_=psO[:]), [psO], [ob])
                    cx.store(sc["OF"], sc["OF"][row0:row0 + 64, :], ob, ob[:].rearrange("p h i -> p (h i)"))
                else:
                    of = of_ring.next()
                    cx.load(of, of[:].rearrange("p h i -> p (h i)"), sc["OF"], sc["OF"][row0:row0 + 64, :])
                    S.op("dve", lambda e: e.tensor_tensor(out=ob[:], in0=psO[:], in1=of[:], op=ALU.add), [psO, of], [ob])
                S.op("dve", lambda e: e.tensor_tensor(out=Mf[:], in0=psM[:], in1=Mf[:], op=ALU.add), [psM, Mf], [Mf])
                S.op("pool", lambda e: e.tensor_tensor(out=Mf[:], in0=Mf[:], in1=bc8(GE[:, c, :]), op=ALU.mult), [Mf, GE], [Mf])
                S.op("act", lambda e: e.copy(out=Mb[:], in_=Mf[:]), [Mf], [Mb])
                if d == 1:
                    st = st_ring.next()
                    S.op("dve", lambda e: e.tensor_reduce(out=st[:, 0, :], in_=ob[:], axis=AX.X, op=ALU.add), [ob], [st])
                    S.op("dve", lambda e: e.tensor_scalar(out=st[:, 1, :], in0=st[:, 0, :], scalar1=-1.0 / 64, scalar2=None, op0=ALU.mult), [st], [st])
                    S.op("dve", lambda e: e.tensor_tensor(out=cen[:], in0=ob[:], in1=bc8(st[:, 1, :]), op=ALU.add), [ob, st], [cen])
                    S.op("act", lambda e: e.activation(out=sqn[:], in_=cen[:], func=AF.Square), [cen], [sqn])
                    S.op("dve", lambda e: e.tensor_reduce(out=st[:, 2, :], in_=sqn[:], axis=AX.X, op=ALU.add), [sqn], [st])
                    self.rstd(cx, st, st[:, 2, :], st, st[:, 4, :], st, st[:, 3, :], 1.0 / 64, lneps[:, 0:1])
                    S.op("dve", lambda e: e.tensor_tensor(out=cen[:], in0=cen[:], in1=bc8(st[:, 4, :]), op=ALU.mult), [cen, st], [cen])
                    S.op("pool", lambda e: e.tensor_tensor(out=cen[:], in0=cen[:], in1=lng[:].rearrange("p (h i) -> p h i", i=64), op=ALU.mult), [cen, lng], [cen])
                    S.op("pool", lambda e: e.tensor_tensor(out=cen[:], in0=cen[:], in1=lnb[:].rearrange("p (h i) -> p h i", i=64), op=ALU.add), [cen, lnb], [cen])
                    psB = GP.next()
                    for h in range(8):
                        S.op("pe", lambda e, h=h: e.matmul(psB[:, h, 0:1], lhsT=RKD[:, h, cs], rhs=onesb[:, 0:1], start=True, stop=True), [RKD, onesb], [psB])
                    S.op("act", lambda e: e.copy(out=st[:, 5, :], in_=psB[:, :, 0]), [psB], [st])
                    S.op("dve", lambda e: e.tensor_tensor(out=sqn[:], in0=tm[:, 3], in1=bc8(st[:, 5, :]), op=ALU.mult), [tm, st], [sqn])
                    S.op("dve", lambda e: e.tensor_tensor(out=cen[:], in0=cen[:], in1=sqn[:], op=ALU.add), [cen, sqn], [cen])
                    psG = GP.next()
                    S.op("pe", lambda e: e.matmul(psG[:].rearrange("p h i -> p (h i)"), lhsT=SG[:, cs], rhs=gup[:], start=True, stop=True), [SG, gup], [psG])
                    yd = yd_ring.next()
                    S.op("dve", lambda e: e.tensor_tensor(out=yd[:], in0=cen[:], in1=psG[:], op=ALU.mult), [cen, psG], [yd])
                    cx.store(sc["YD"], sc["YD"][row0:row0 + 64, :], yd, yd[:].rearrange("p h i -> p (h i)"))

            blocks = range(NB) if d == 0 else range(NB - 1, -1, -1)
            for bi in blocks:
                prep(bi)
                n0 = bi * 512
                nch = min(512, Lp - n0) // 64
                for c in (range(nch) if d == 0 else range(nch - 1, -1, -1)):
                    chunk(n0, c)


def make_inputs(prog, inputs, core):
    m = {}
    for k in prog.W:
        m[k] = np.ascontiguousarray(np.asarray(inputs[k], dtype=np.float32))
    for k, v in global_consts().items():
        m["c_" + k] = v
    meta = np.asarray(inputs["meta_tokens"], dtype=np.float32)
    for s in prog.seqs:
        x = s.xsrc(inputs, core)
        h0 = np.zeros((s.Lp, D), np.float32)
        h0[:16] = meta
        h0[16:s.L] = x
        m["h0_" + s.name] = h0
        for k, v in host_consts(s).items():
            m["c_%s_%s" % (k, s.name)] = v
    return m


def kernel(**inputs):
    xp = np.asarray(inputs["x_prompt"], dtype=np.float32)
    xs = np.asarray(inputs["x_sample"], dtype=np.float32)
    sp = Seq("p", xp.shape[1])
    ss = Seq("s", xs.shape[1])
    sp.xsrc = lambda inp, core: xp[core]
    ss.xsrc = lambda inp, core: xs[0]
    prog = Prog([sp, ss], debug=False)
    nc = prog.build()
    in_maps = []
    for c in range(NCORE):
        m = make_inputs(prog, inputs, c)
        in_maps.append({k: m[k] for k in prog.in_names})
    res = run_bass_kernel_spmd(nc, in_maps, core_ids=list(range(NCORE)))
    r = res.results
    y_prompt = np.stack([np.asarray(r[c]["y_p"], dtype=np.float32) for c in range(NCORE)], axis=0)
    n = xs.shape[1] // NCORE
    y_sample = np.concatenate([np.asarray(r[c]["y_s"], dtype=np.float32)[c * n:(c + 1) * n] for c in range(NCORE)], axis=0)[None]
    return (y_prompt, y_sample)
```

```python
import math
import contextlib
import numpy as np
import concourse.bass as bass
import concourse.mybir as mybir
from concourse.bass_utils import run_bass_kernel_spmd

F32 = mybir.dt.float32
BF16 = mybir.dt.bfloat16
I32 = mybir.dt.int32
AF = mybir.ActivationFunctionType
ALU = mybir.AluOpType
AX = mybir.AxisListType

D = 1024
NCORE = 8
EPS = 1e-6
ENG_NAMES = ("pe", "act", "dve", "pool", "sp")
SAME_ENGINE_SYNC = True


class Buf:
    __slots__ = ("name", "w", "r", "dsem", "dcnt", "excl")

    def __init__(self, name):
        self.name = name
        self.excl = False
        self.w = {}
        self.r = {}
        self.dsem = None
        self.dcnt = 0


class TL:
    def __init__(self, t, name):
        self.t = t
        self.b = Buf(name)

    def __getitem__(self, i):
        return self.t[i]


def _b(x):
    return x.b if isinstance(x, TL) else x


class Sched:
    def __init__(self, nc):
        self.nc = nc
        self.ops = {e: [] for e in ENG_NAMES}
        self.sem = {e: nc.alloc_semaphore(name="s_" + e) for e in ENG_NAMES}
        self.cnt = {e: 0 for e in ENG_NAMES}
        self.seen = {e: {} for e in ENG_NAMES}
        self.dpool = []
        self.dlive = []
        self.dall = {}
        self.nsem = 0

    def _need(self, e, toks, is_dma):
        waits = []
        for sid, (sem, val, src) in toks.items():
            if not is_dma and src == e and (e == "pe" or not SAME_ENGINE_SYNC):
                continue
            if self.seen[e].get(sid, 0) >= val:
                continue
            self.seen[e][sid] = val
            waits.append((sem, val))
        return waits

    def _collect(self, e, reads, writes, is_dma):
        waits = []
        for b in reads:
            waits += self._need(e, b.w, is_dma)
        for b in writes:
            waits += self._need(e, b.w, is_dma)
            waits += self._need(e, b.r, is_dma)
        return waits

    def op(self, e, fn, reads=(), writes=()):
        reads = [_b(x) for x in reads]
        writes = [_b(x) for x in writes]
        writes = writes + [b for b in reads if b.excl and b not in writes]
        waits = self._collect(e, reads, writes, False)
        self.cnt[e] += 1
        tok = (self.sem[e], self.cnt[e], e)
        sid = id(self.sem[e])
        for b in reads:
            b.r[sid] = tok
        for b in writes:
            b.w[sid] = tok
        self.ops[e].append((waits, fn, (self.sem[e], 1)))

    def dma(self, e, fn, reads=(), writes=(), key=None):
        reads = [_b(x) for x in reads]
        writes = [_b(x) for x in writes]
        key = _b(key)
        waits = self._collect(e, reads, writes, True)
        if key.dsem is None:
            if self.dpool:
                key.dsem, key.dcnt = self.dpool.pop()
            else:
                key.dsem = self.nc.alloc_semaphore(name="d%d" % self.nsem)
                self.nsem += 1
                key.dcnt = 0
            self.dlive.append(key)
        key.dcnt += 16
        tok = (key.dsem, key.dcnt, "dma")
        sid = id(key.dsem)
        self.dall[sid] = [key.dsem, key.dcnt]
        for b in reads:
            b.r[sid] = tok
        for b in writes:
            b.w[sid] = tok
        self.ops[e].append((waits, fn, (key.dsem, 16)))

    def barrier(self, engines=ENG_NAMES):
        toks = [(self.sem[e], self.cnt[e]) for e in ENG_NAMES if self.cnt[e] > 0]
        toks += [(s, c) for s, c in self.dall.values()]
        for e in engines:
            waits = []
            for sem, val in toks:
                if self.seen[e].get(id(sem), 0) >= val:
                    continue
                self.seen[e][id(sem)] = val
                waits.append((sem, val))
            if waits:
                self.ops[e].append((waits, None, None))
        for b in self.dlive:
            self.dpool.append((b.dsem, b.dcnt))
            b.dsem = None
        self.dlive = []

    def emit(self):
        nc = self.nc
        with nc.Block() as block:
            for e in ENG_NAMES:
                ops = self.ops[e]
                if not ops:
                    continue

                def body(eng, ops=ops):
                    for waits, fn, inc in ops:
                        for sem, val in waits:
                            eng.wait_ge(sem, val)
                        if fn is not None:
                            fn(eng).then_inc(inc[0], inc[1])

                dec = {"pe": block.tensor, "act": block.scalar, "dve": block.vector,
                       "pool": block.gpsimd, "sp": block.sync}[e]
                dec(body)


class Ring:
    def __init__(self, tiles):
        self.tiles = tiles
        self.i = 0

    def next(self):
        t = self.tiles[self.i % len(self.tiles)]
        self.i += 1
        return t


class Ctx:
    uid = 0

    def __init__(self, nc, S):
        self.nc = nc
        self.S = S
        self.es = contextlib.ExitStack()
        self.dq_i = 0

    def __enter__(self):
        self.es.__enter__()
        return self

    def __exit__(self, *a):
        self.S.barrier()
        return self.es.__exit__(*a)

    def sb(self, name, shape, dt):
        Ctx.uid += 1
        n = "%s_%d" % (name, Ctx.uid)
        return TL(self.es.enter_context(self.nc.sbuf_tensor(n, list(shape), dt)), n)

    def ps(self, name, shape, dt):
        Ctx.uid += 1
        n = "%s_%d" % (name, Ctx.uid)
        esz = 4 if dt in (F32, I32) else 2
        tot = 1
        for d in shape[1:]:
            tot *= d
        assert tot * esz <= 2048, (name, shape)
        full = self.es.enter_context(self.nc.psum_tensor(n, [128, 2048 // esz], dt))
        view = full[0:shape[0], 0:tot]
        if len(shape) == 3:
            view = view.rearrange("p (a b) -> p a b", b=shape[2])
        elif len(shape) == 4:
            view = view.rearrange("p (a b c) -> p a b c", b=shape[2], c=shape[3])
        t = TL(view, n)
        t.b.excl = True
        return t

    def ring(self, name, n, shape, dt, ps=False):
        f = self.ps if ps else self.sb
        return Ring([f(name + str(i), shape, dt) for i in range(n)])

    def dq(self):
        self.dq_i += 1
        return "sp" if self.dq_i % 2 else "pool"

    def load(self, dst_tl, dst_ap, src_tl, src_ap, q=None):
        self.S.dma(q or self.dq(), lambda e: e.dma_start(out=dst_ap, in_=src_ap),
                   reads=[src_tl], writes=[dst_tl], key=dst_tl)

    def store(self, dst_tl, dst_ap, src_tl, src_ap, q=None, slow=False):
        self.S.dma(q or self.dq(), lambda e: e.dma_start(out=dst_ap, in_=src_ap, allow_slow_non_contiguous=slow),
                   reads=[src_tl], writes=[dst_tl], key=src_tl)


class Seq:
    def __init__(self, name, S_len):
        self.name = name
        self.S = S_len
        self.L = S_len + 16
        self.NT = (self.L + 127) // 128
        self.Lp = self.NT * 128
        self.last = self.L - (self.NT - 1) * 128
        self.NB = (self.Lp + 511) // 512

    def rows(self, t):
        return self.last if t == self.NT - 1 else 128


def host_consts(seq):
    L, Lp, NT, NB = seq.L, seq.Lp, seq.NT, seq.NB
    S_len = seq.S
    c = {}
    rows = S_len // 64
    row = np.repeat(np.arange(rows), 64)
    col = np.arange(rows * 64) % 64
    meta = np.arange(16) - 16
    row = np.concatenate([meta, row]).astype(np.float32)
    col = np.concatenate([meta, col]).astype(np.float32)
    inv = (np.float32(10000.0) ** (-np.arange(0, 32, 2, dtype=np.float32) / np.float32(32))).astype(np.float32)
    ang = np.zeros((Lp, 2, 16), np.float32)
    ang[:L, 0] = row[:, None] * inv
    ang[:L, 1] = col[:, None] * inv
    ropeA = np.zeros((Lp, 2, 2, 16), np.float32)
    ropeA[:, 0] = np.cos(ang)
    ropeA[:, 1] = np.sin(ang)
    c["ropeA"] = ropeA
    inv1 = (np.float32(500000.0) ** (-np.arange(0, 16, 2, dtype=np.float32) / np.float32(16))).astype(np.float32)
    ang1 = np.zeros((Lp, 8), np.float32)
    ang1[:L] = np.arange(L, dtype=np.float32)[:, None] * inv1
    ropeC = np.zeros((Lp, 2, 8), np.float32)
    ropeC[:, 0] = np.cos(ang1)
    ropeC[:, 1] = np.sin(ang1)
    c["ropeC"] = ropeC
    p = np.arange(128)
    lv = (np.arange(NT)[None, :] * 128 + p[:, None]).astype(np.int64)
    c["lvals"] = lv.astype(np.float32)
    l0 = (np.arange(NB) * 512).astype(np.int64)
    sp = (lv[:, :, None] * l0[None, None, :]) % L
    c["dftsp"] = sp.reshape(128, NT * NB).astype(np.float32)
    return c


def global_consts():
    c = {}
    c["ident"] = np.eye(128, dtype=np.float32)
    c["iota512"] = np.tile(np.arange(512, dtype=np.float32)[None, :], (128, 1))
    k = np.arange(64)
    ang = 2 * np.pi * ((k[:, None] * k[None, :]) % 64) / 64.0
    c64 = (np.cos(ang) / 8.0).astype(np.float32)
    s64 = (np.sin(ang) / 8.0).astype(np.float32)
    c["cs64"] = np.concatenate([c64, c64, s64, s64], axis=1)
    i = np.arange(128)
    mk = np.zeros((3, 128, 128), np.float32)
    mk[0] = (i[:, None] >= i[None, :])
    mk[1] = (i[:, None] <= i[None, :])
    mk[2] = mk[0]; mk[2][:16, :] = 1.0
    c["cmask"] = np.ascontiguousarray(mk.transpose(1, 0, 2))
    j = np.arange(64)
    m4 = np.zeros((4, 64, 64), np.float32)
    m4[0] = (j[None, :] > j[:, None])
    m4[1] = (j[None, :] >= j[:, None])
    m4[2] = (j[None, :] < j[:, None])
    m4[3] = (j[None, :] <= j[:, None])
    c["m4"] = np.ascontiguousarray(m4.transpose(1, 0, 2))
    rs = np.ones((64, 512), np.float32); rs[:, 0::64] = 0.0
    c["reset"] = rs
    return c


class Prog:
    def __init__(self, seqs, debug=False):
        self.seqs = seqs
        self.debug = debug
        self.stop_after = 1
        nc = self.nc = bass.Bass("TRN2", target_bir_lowering=False)
        self.S = Sched(nc)
        self.dr = {}
        self.in_names = []
        self.out_names = []

    def din(self, name, shape, dt=F32):
        t = TL(self.nc.dram_tensor(name, list(shape), dt, kind="ExternalInput").ap(), name)
        self.dr[name] = t
        self.in_names.append(name)
        return t

    def dout(self, name, shape, dt=F32):
        t = TL(self.nc.dram_tensor(name, list(shape), dt, kind="ExternalOutput").ap(), name)
        self.dr[name] = t
        self.out_names.append(name)
        return t

    def dscr(self, name, shape, dt):
        if self.debug:
            return self.dout(name, shape, dt)
        t = TL(self.nc.dram_tensor(name, list(shape), dt).ap(), name)
        self.dr[name] = t
        return t

    def dbg(self, cx, name, tl, ap, shape, dt):
        if not self.debug or name in self.dr:
            return
        o = self.dout("dbg_" + name, shape, dt)
        cx.store(o, o[tuple(slice(None) for _ in shape)], tl, ap)

    def build(self):
        nc, S = self.nc, self.S
        W = self.W = {}
        wspec = dict(
            pre_mix_g=(2, D), post_mix_g=(2, D), pre_ffn_g=(2, D), post_ffn_g=(2, D),
            even_w_in=(1, D, 1536), even_w_out=(1, D, D), a_q_gain=(1, 64), a_k_gain=(1, 64),
            b_norm_g=(1, 4, 64), b_w=(1, 4, 64, 64), b_b=(1, 4, 64),
            odd_w_in=(1, D, 2688), odd_w_out=(1, D, D), c_sink=(1, 8),
            d_mu_prev=(1, 1920), d_mu_next=(1, 1920), d_w0=(1, 2, 512), d_w_up=(1, 2, 64, 512),
            d_a0=(1, 2, 512), d_a_up=(1, 2, 64, 512), d_g_up=(1, 128, 512), d_k_k=(1, 512),
            d_k_a=(1, 512), d_r_k=(1, 8, 64), d_ln_g=(1, 512), d_ln_b=(1, 512),
            ffn_w_gate=(2, D, 2816), ffn_w_up=(2, D, 2816), ffn_w_down=(2, 2816, D))
        for k, shp in wspec.items():
            W[k] = self.din(k, shp)
        G = self.G = {}
        for k, v in global_consts().items():
            G[k] = self.din("c_" + k, v.shape)
        maxLp = max(s.Lp for s in self.seqs)
        sc = self.sc = {}
        sc["QT"] = self.dscr("QT", [768, maxLp], BF16)
        sc["KT"] = self.dscr("KT", [256, maxLp], BF16)
        sc["VX"] = self.dscr("VX", [maxLp, 4, 65], BF16)
        sc["GG"] = self.dscr("GG", [maxLp, 512], BF16)
        sc["YA"] = self.dscr("YA", [maxLp, 768], BF16)
        sc["YB"] = self.dscr("YB", [maxLp, 256], BF16)
        sc["H1"] = self.dscr("H1", [maxLp, D], F32)
        sc["QCT"] = self.dscr("QCT", [8, 65, maxLp], BF16)
        sc["KCT"] = self.dscr("KCT", [2, 65, maxLp], BF16)
        sc["VCX"] = self.dscr("VCX", [maxLp, 2, 65], BF16)
        sc["BQ"] = self.dscr("BQ", [maxLp, 8], F32)
        sc["UT"] = self.dscr("UT", [1920, maxLp + 2], F32)
        sc["YC"] = self.dscr("YC", [maxLp, 512], BF16)
        sc["OF"] = self.dscr("OF", [maxLp, 512], F32)
        sc["YD"] = self.dscr("YD", [maxLp, 512], BF16)
        for s in self.seqs:
            s.h0 = self.din("h0_" + s.name, [s.Lp, D])
            s.y = self.dout("y_" + s.name, [s.S, D])
            s.c = {k: self.din("c_%s_%s" % (k, s.name), v.shape) for k, v in host_consts(s).items()}

        self.phase_consts()
        sk = getattr(self, "skip", ())
        for s in self.seqs:
            if "even" not in sk: self.phase_even_proj(s)
            if "attn_a" not in sk: self.phase_attn_a(s)
            if "dft" not in sk: self.phase_dft(s)
            if "mix0" not in sk: self.phase_mix_ffn(s, 0, s.h0, sc["H1"], ("YA", 768), ("YB", 256), W["even_w_out"], final=False)
            if self.stop_after == 0:
                continue
            if "odd" not in sk: self.phase_odd_proj(s)
            if "attn_c" not in sk: self.phase_attn_c(s)
            if "rwkv0" not in sk: self.phase_rwkv(s, 0)
            if "rwkv1" not in sk: self.phase_rwkv(s, 1)
            if "mix1" not in sk: self.phase_mix_ffn(s, 1, sc["H1"], s.y, ("YC", 512), ("YD", 512), W["odd_w_out"], final=True)
        with nc.allow_non_contiguous_dma(reason="small strided parameter / layout loads"):
            S.emit()
        return nc

    def phase_consts(self):
        nc, S = self.nc, self.S
        self.pes = contextlib.ExitStack()
        cx = self.pcx = Ctx(nc, S)
        cx.es = self.pes
        K = self.K = {}
        G = self.G
        K["identf"] = cx.sb("identf", [128, 128], F32)
        K["identb"] = cx.sb("identb", [128, 128], BF16)
        cx.load(K["identf"], K["identf"][:], G["ident"], G["ident"][:, :])
        S.op("dve", lambda e: e.tensor_copy(out=K["identb"][:], in_=K["identf"][:]), [K["identf"]], [K["identb"]])
        K["eps"] = cx.sb("eps", [128, 1], F32)
        S.op("dve", lambda e: e.memset(K["eps"][:], EPS), [], [K["eps"]])
        K["halfpi"] = cx.sb("halfpi", [128, 1], F32)
        S.op("dve", lambda e: e.memset(K["halfpi"][:], math.pi / 2), [], [K["halfpi"]])
        S.barrier()

    def load_w(self, cx, dst, src_tl, src_ap, nk, ncols, stage_ring, row0=0):
        S = self.S
        for k in range(nk):
            for c0 in range(0, ncols, 512):
                w = min(512, ncols - c0)
                st = stage_ring.next()
                cx.load(st, st[:, 0:w], src_tl, src_ap[row0 + k * 128: row0 + (k + 1) * 128, c0:c0 + w])
                S.op("pool", lambda e, st=st, k=k, c0=c0, w=w: e.tensor_copy(out=dst[:, k, c0:c0 + w], in_=st[:, 0:w]), [st], [dst])

    def rstd(self, cx, ss_tl, ss_ap, out_tl, out_ap, tmp_tl, tmp_ap, scale, eps_ap):
        S = self.S
        S.op("act", lambda e: e.activation(out=tmp_ap, in_=ss_ap, func=AF.Sqrt, bias=eps_ap, scale=scale),
             [ss_tl], [tmp_tl])
        S.op("dve", lambda e: e.reciprocal(out=out_ap, in_=tmp_ap), [tmp_tl], [out_tl])

    def norm_transpose(self, cx, x_tl, gb_tl, hnT, TP, xs_ring, sm_ring, junk):
        S, K = self.S, self.K
        sm = sm_ring.next()
        S.op("act", lambda e: e.activation(out=junk[:], in_=x_tl[:], func=AF.Square, accum_out=sm[:, 0:1]),
             [x_tl], [junk, sm])
        self.rstd(cx, sm, sm[:, 0:1], sm, sm[:, 2:3], sm, sm[:, 1:2], 1.0 / D, K["eps"][:, 0:1])
        xs = xs_ring.next()
        S.op("dve", lambda e: e.scalar_tensor_tensor(out=xs[:], in0=x_tl[:], scalar=sm[:, 2:3], in1=gb_tl[:],
                                                     op0=ALU.mult, op1=ALU.mult), [x_tl, sm, gb_tl], [xs])
        for k in range(8):
            S.op("pe", lambda e, k=k: e.transpose(out=TP[:, k, :], in_=xs[:, k * 128:(k + 1) * 128],
                                                  identity=K["identb"][:]), [xs, K["identb"]], [TP])
        S.op("act", lambda e: e.copy(out=hnT[:], in_=TP[:]), [TP], [hnT])

    def phase_even_proj(self, seq):
        nc, S, W, K, sc = self.nc, self.S, self.W, self.K, self.sc
        with Ctx(nc, S) as cx:
            stage = cx.ring("wst", 3, [128, 512], F32)
            Win = cx.sb("win", [128, 8, 1536], BF16)
            self.load_w(cx, Win, W["even_w_in"], W["even_w_in"][0], 8, 1536, stage)
            gpre = cx.sb("gpre", [128, D], F32)
            cx.load(gpre, gpre[:], W["pre_mix_g"], W["pre_mix_g"][0:1, :].partition_broadcast(128))
            g16 = cx.sb("g16", [128, 16, 64], F32)
            cx.load(g16, g16[:, 0:12, :], W["a_q_gain"], W["a_q_gain"][0:1, :].partition_broadcast(128).broadcast_to([128, 12, 64]))
            cx.load(g16, g16[:, 12:16, :], W["a_k_gain"], W["a_k_gain"][0:1, :].partition_broadcast(128).broadcast_to([128, 4, 64]))
            S.op("dve", lambda e: e.tensor_scalar(out=g16[:, 0:12, :], in0=g16[:, 0:12, :], scalar1=0.125, scalar2=None,
                                                  op0=ALU.mult), [g16], [g16])
            bg = cx.sb("bg", [128, 256], F32)
            cx.load(bg, bg[:], W["b_norm_g"], W["b_norm_g"][0].rearrange("g c -> (g c)").partition_broadcast(128))
            cs64 = cx.sb("cs64", [64, 256], F32)
            cx.load(cs64, cs64[:], self.G["cs64"], self.G["cs64"][:, :])
            bw = cx.sb("bw", [64, 4, 64], F32)
            cx.load(bw, bw[:], W["b_w"], W["b_w"][0].rearrange("g c d -> c g d"))
            Wcs = cx.sb("wcs", [128, 2, 512], BF16)
            S.op("pool", lambda e: e.memset(Wcs[:], 0.0), [], [Wcs])
            pw = cx.ps("pw", [128, 512], F32)
            for g in range(4):
                for j in range(2):
                    S.op("pe", lambda e, g=g, j=j: e.matmul(pw[:, (g * 2 + j) * 64:(g * 2 + j + 1) * 64],
                                                             lhsT=cs64[:, j * 128:(j + 1) * 128], rhs=bw[:, g, :],
                                                             start=True, stop=True), [cs64, bw], [pw])
            for g in range(4):
                h0 = (g % 2) * 64
                for j in range(2):
                    S.op("dve", lambda e, g=g, j=j, h0=h0: e.tensor_copy(
                        out=Wcs[h0:h0 + 64, g // 2, j * 256 + g * 64: j * 256 + (g + 1) * 64],
                        in_=pw[h0:h0 + 64, (g * 2 + j) * 64:(g * 2 + j + 1) * 64]), [pw], [Wcs])

            self.dbg(cx, "win0", Win, Win[:, 0, :], [128, 1536], BF16)
            self.dbg(cx, "gpre", gpre, gpre[:], [128, D], F32)
            self.dbg(cx, "g16", g16, g16[:], [128, 16, 64], F32)
            self.dbg(cx, "wcs", Wcs, Wcs[:], [128, 2, 512], BF16)
            xring = cx.ring("x", 2, [128, D], F32)
            xs_ring = cx.ring("xs", 2, [128, D], BF16)
            sm_ring = cx.ring("sm", 2, [128, 4], F32)
            junk = cx.sb("junk", [128, D], BF16)
            TP = cx.ps("TP", [128, 8, 128], BF16)
            hnT_ring = cx.ring("hnT", 2, [128, 8, 128], BF16)
            PJ = [cx.ps("PJ%d" % c, [128, 512], F32) for c in range(3)]
            sqt = cx.sb("sqt", [128, 16, 64], F32)
            ssq_ring = cx.ring("ssq", 2, [128, 3, 16], F32)
            qk = cx.sb("qk", [128, 16, 64], F32)
            tA = cx.sb("tA", [128, 16, 2, 16], F32)
            tB = cx.sb("tB", [128, 16, 2, 16], F32)
            qkr_ring = cx.ring("qkr", 2, [128, 16, 64], BF16)
            rope_ring = cx.ring("rope", 2, [128, 2, 2, 16], F32)
            TP2 = cx.ps("TP2", [128, 8, 128], BF16)
            qkT_ring = cx.ring("qkT", 2, [128, 8, 128], BF16)
            vx_ring = cx.ring("vx", 2, [128, 4, 65], BF16)
            for vx in vx_ring.tiles:
                S.op("pool", lambda e, vx=vx: e.memset(vx[:], 1.0), [], [vx])
            sqf = cx.sb("sqf", [128, 4, 64], F32)
            fn32 = cx.sb("fn32", [128, 4, 64], F32)
            fn_ring = cx.ring("fn", 2, [128, 256], BF16)
            TP3 = cx.ps("TP3", [128, 2, 128], BF16)
            fnT_ring = cx.ring("fnT", 2, [128, 2, 128], BF16)
            PG = cx.ps("PG", [128, 512], F32)
            g_ring = cx.ring("gsb", 2, [128, 512], BF16)
            QTv = sc["QT"].t.rearrange("(c p) n -> p c n", p=128)
            KTv = sc["KT"].t.rearrange("(c p) n -> p c n", p=128)

            def body(t):
                c0 = t * 128
                xt = xring.next()
                cx.load(xt, xt[:], seq.h0, seq.h0[c0:c0 + 128, :])
                rp = rope_ring.next()
                cx.load(rp, rp[:], seq.c["ropeA"], seq.c["ropeA"][c0:c0 + 128])
                hnT = hnT_ring.next()
                self.norm_transpose(cx, xt, gpre, hnT, TP, xs_ring, sm_ring, junk)
                for c in range(3):
                    for k in range(8):
                        S.op("pe", lambda e, c=c, k=k: e.matmul(PJ[c][:], lhsT=hnT[:, k, :], rhs=Win[:, k, c * 512:(c + 1) * 512],
                                                                 start=(k == 0), stop=(k == 7)), [hnT, Win], [PJ[c]])
                if t == 0 and self.debug:
                    self.dbg(cx, "hnT", hnT, hnT[:], [128, 8, 128], BF16)
                    pjd = cx.sb("pjd", [128, 1536], F32)
                    for c in range(3):
                        S.op("dve", lambda e, c=c: e.tensor_copy(out=pjd[:, c * 512:(c + 1) * 512], in_=PJ[c][:]), [PJ[c]], [pjd])
                    self.dbg(cx, "pj", pjd, pjd[:], [128, 1536], F32)
                for c in range(2):
                    S.op("act", lambda e, c=c: e.activation(out=sqt[:, c * 8:(c + 1) * 8, :],
                                                            in_=PJ[c][:].rearrange("p (h d) -> p h d", d=64), func=AF.Square),
                         [PJ[c]], [sqt])
                sq = ssq_ring.next()
                S.op("dve", lambda e: e.tensor_reduce(out=sq[:, 0, :], in_=sqt[:], axis=AX.X, op=ALU.add), [sqt], [sq])
                self.rstd(cx, sq, sq[:, 0, :], sq, sq[:, 2, :], sq, sq[:, 1, :], 1.0 / 64, K["eps"][:, 0:1])
                for c in range(2):
                    S.op("dve", lambda e, c=c: e.tensor_tensor(
                        out=qk[:, c * 8:(c + 1) * 8, :], in0=PJ[c][:].rearrange("p (h d) -> p h d", d=64),
                        in1=sq[:, 2, c * 8:(c + 1) * 8].unsqueeze(2).broadcast_to([128, 8, 64]), op=ALU.mult),
                        [PJ[c], sq], [qk])
                S.op("dve", lambda e: e.tensor_tensor(out=qk[:], in0=qk[:], in1=g16[:], op=ALU.mult), [qk, g16], [qk])
                qkr = qkr_ring.next()
                q5 = qk[:].rearrange("p h (a c d) -> p h a c d", a=2, c=2)
                o5 = qkr[:].rearrange("p h (a c d) -> p h a c d", a=2, c=2)
                x1, x2 = q5[:, :, :, 0, :], q5[:, :, :, 1, :]
                cosb = rp[:, 0].unsqueeze(1).broadcast_to([128, 16, 2, 16])
                sinb = rp[:, 1].unsqueeze(1).broadcast_to([128, 16, 2, 16])
                S.op("dve", lambda e: e.tensor_tensor(out=tA[:], in0=x1, in1=cosb, op=ALU.mult), [qk, rp], [tA])
                S.op("pool", lambda e: e.tensor_tensor(out=tB[:], in0=x2, in1=sinb, op=ALU.mult), [qk, rp], [tB])
                S.op("dve", lambda e: e.tensor_tensor(out=o5[:, :, :, 0, :], in0=tA[:], in1=tB[:], op=ALU.subtract), [tA, tB], [qkr])
                S.op("dve", lambda e: e.tensor_tensor(out=tA[:], in0=x1, in1=sinb, op=ALU.mult), [qk, rp], [tA])
                S.op("pool", lambda e: e.tensor_tensor(out=tB[:], in0=x2, in1=cosb, op=ALU.mult), [qk, rp], [tB])
                S.op("dve", lambda e: e.tensor_tensor(out=o5[:, :, :, 1, :], in0=tA[:], in1=tB[:], op=ALU.add), [tA, tB], [qkr])
                qf = qkr[:].rearrange("p h d -> p (h d)")
                for c in range(8):
                    S.op("pe", lambda e, c=c: e.transpose(out=TP2[:, c, :], in_=qf[:, c * 128:(c + 1) * 128],
                                                          identity=K["identb"][:]), [qkr, K["identb"]], [TP2])
                qkT = qkT_ring.next()
                S.op("act", lambda e: e.copy(out=qkT[:], in_=TP2[:]), [TP2], [qkT])
                cx.store(sc["QT"], QTv[:, :, c0:c0 + 128], qkT, qkT[:, 0:6, :])
                cx.store(sc["KT"], KTv[:, :, c0:c0 + 128], qkT, qkT[:, 6:8, :])
                vx = vx_ring.next()
                S.op("act", lambda e: e.copy(out=vx[:, :, 0:64], in_=PJ[2][:, 0:256].rearrange("p (h d) -> p h d", d=64)),
                     [PJ[2]], [vx])
                cx.store(sc["VX"], sc["VX"][c0:c0 + 128, :, :], vx, vx[:])
                fv = PJ[2][:, 256:512].rearrange("p (h d) -> p h d", d=64)
                S.op("act", lambda e: e.activation(out=sqf[:], in_=fv, func=AF.Square), [PJ[2]], [sqf])
                S.op("dve", lambda e: e.tensor_reduce(out=sq[:, 0, 0:4], in_=sqf[:], axis=AX.X, op=ALU.add), [sqf], [sq])
                self.rstd(cx, sq, sq[:, 0, 0:4], sq, sq[:, 2, 0:4], sq, sq[:, 1, 0:4], 1.0 / 64, K["eps"][:, 0:1])
                S.op("dve", lambda e: e.tensor_tensor(out=fn32[:], in0=fv, in1=sq[:, 2, 0:4].unsqueeze(2).broadcast_to([128, 4, 64]),
                                                      op=ALU.mult), [PJ[2], sq], [fn32])
                fn = fn_ring.next()
                S.op("dve", lambda e: e.tensor_tensor(out=fn[:], in0=fn32[:].rearrange("p h d -> p (h d)"), in1=bg[:], op=ALU.mult),
                     [fn32, bg], [fn])
                for c in range(2):
                    S.op("pe", lambda e, c=c: e.transpose(out=TP3[:, c, :], in_=fn[:, c * 128:(c + 1) * 128],
                                                          identity=K["identb"][:]), [fn, K["identb"]], [TP3])
                fnT = fnT_ring.next()
                S.op("act", lambda e: e.copy(out=fnT[:], in_=TP3[:]), [TP3], [fnT])
                for c in range(2):
                    S.op("pe", lambda e, c=c: e.matmul(PG[:], lhsT=fnT[:, c, :], rhs=Wcs[:, c, :], start=(c == 0), stop=(c == 1)),
                         [fnT, Wcs], [PG])
                gs = g_ring.next()
                S.op("act", lambda e: e.copy(out=gs[:], in_=PG[:]), [PG], [gs])
                cx.store(sc["GG"], sc["GG"][c0:c0 + 128, :], gs, gs[:])

            for t in range(seq.NT):
                body(t)

    def phase_attn_a(self, seq):
        nc, S, W, K, sc = self.nc, self.S, self.W, self.K, self.sc
        NT, Lp = seq.NT, seq.Lp
        with Ctx(nc, S) as cx:
            gq = cx.sb("gq", [128, 2, 64], F32)
            cx.load(gq, gq[:, 0, :], W["a_q_gain"], W["a_q_gain"][0:1, :].partition_broadcast(128))
            cx.load(gq, gq[:, 1, :], W["a_k_gain"], W["a_k_gain"][0:1, :].partition_broadcast(128))
            gm = cx.sb("gm", [128, 4], F32)
            S.op("dve", lambda e: e.tensor_reduce(out=gm[:, 0:2], in_=gq[:], axis=AX.X, op=ALU.max, apply_absolute_value=True),
                 [gq], [gm])
            S.op("dve", lambda e: e.tensor_tensor(out=gm[:, 2:3], in0=gm[:, 0:1], in1=gm[:, 1:2], op=ALU.mult), [gm], [gm])
            S.op("dve", lambda e: e.tensor_scalar(out=gm[:, 3:4], in0=gm[:, 2:3], scalar1=-8.0, scalar2=None, op0=ALU.mult), [gm], [gm])
            KTs = cx.sb("KTs", [64, Lp], BF16)
            VXs = cx.sb("VXs", [128, NT, 65], BF16)
            q_ring = cx.ring("qb", 2, [64, 512], BF16)
            ST = cx.ring("ST", 3, [128, 512], F32, ps=True)
            OT = cx.ring("OT", 2, [65, 512], F32, ps=True)
            PT = cx.ring("PT", 4, [128, 512], BF16)
            osb_ring = cx.ring("osb", 2, [65, 512], F32)
            TPo = cx.ps("TPo", [128, 4, 65], F32)
            rec_ring = cx.ring("rec", 2, [128, 4, 1], F32)
            y_ring = cx.ring("ya", 2, [128, 4, 64], BF16)
            YAv = sc["YA"].t.rearrange("(s p) c -> p s c", p=128)
            def qblock(g, h, qb):
                if True:
                    if True:
                        n0 = qb * 512
                        wd = min(512, Lp - n0)
                        nsub = wd // 128
                        qt = q_ring.next()
                        cx.load(qt, qt[:, 0:wd], sc["QT"], sc["QT"][h * 64:(h + 1) * 64, n0:n0 + wd])
                        ot = OT.next()
                        sts = {}

                        def emitS(kt):
                            r = seq.rows(kt)
                            st = ST.next()
                            sts[kt] = st
                            S.op("pe", lambda e, st=st, kt=kt, r=r: e.matmul(st[0:r, 0:wd], lhsT=KTs[:, kt * 128:kt * 128 + r],
                                                                               rhs=qt[:, 0:wd], start=True, stop=True), [KTs, qt], [st])
                        LOOK = 2
                        for kt in range(min(LOOK, NT)):
                            emitS(kt)
                        for kt in range(NT):
                            r = seq.rows(kt)
                            if kt + LOOK < NT:
                                emitS(kt + LOOK)
                            st = sts.pop(kt)
                            pt = PT.next()
                            S.op("act", lambda e, st=st, pt=pt, r=r: e.activation(out=pt[0:r, 0:wd], in_=st[0:r, 0:wd], func=AF.Exp,
                                                                                  bias=gm[0:r, 3:4], scale=1.0), [st, gm], [pt])
                            S.op("pe", lambda e, pt=pt, kt=kt, r=r: e.matmul(ot[:, 0:wd], lhsT=VXs[0:r, kt, :], rhs=pt[0:r, 0:wd],
                                                                               start=(kt == 0), stop=(kt == NT - 1)), [VXs, pt], [ot])
                        osb = osb_ring.next()
                        S.op("dve", lambda e, osb=osb: e.tensor_copy(out=osb[:, 0:wd], in_=ot[:, 0:wd]), [ot], [osb])
                        for s in range(nsub):
                            S.op("pe", lambda e, s=s, osb=osb: e.transpose(out=TPo[:, s, :], in_=osb[:, s * 128:(s + 1) * 128],
                                                                            identity=K["identf"][0:65, 0:65]), [osb, K["identf"]], [TPo])
                        rec = rec_ring.next()
                        S.op("dve", lambda e, rec=rec: e.reciprocal(out=rec[:, 0:nsub, :], in_=TPo[:, 0:nsub, 64:65]), [TPo], [rec])
                        ya = y_ring.next()
                        S.op("dve", lambda e, rec=rec, ya=ya: e.tensor_tensor(out=ya[:, 0:nsub, :], in0=TPo[:, 0:nsub, 0:64],
                                                                              in1=rec[:, 0:nsub, :].broadcast_to([128, nsub, 64]),
                                                                              op=ALU.mult), [TPo, rec], [ya])
                        cx.store(sc["YA"], YAv[:, qb * 4:qb * 4 + nsub, h * 64:(h + 1) * 64], ya, ya[:, 0:nsub, :])

            for g in range(4):
                cx.load(KTs, KTs[:, :], sc["KT"], sc["KT"][g * 64:(g + 1) * 64, 0:Lp])
                for t0 in range(0, NT, 16):
                    t1_ = min(NT, t0 + 16)
                    cx.load(VXs, VXs[:, t0:t1_, :], sc["VX"], sc["VX"][t0 * 128:t1_ * 128, g, :].rearrange("(t p) d -> p t d", p=128))
                for j in range(3):
                    for qb in range(seq.NB):
                        qblock(g, g * 3 + j, qb)

    def phase_dft(self, seq):
        nc, S, W, K, sc = self.nc, self.S, self.W, self.K, self.sc
        NT, Lp, L, NB = seq.NT, seq.Lp, seq.L, seq.NB
        with Ctx(nc, S) as cx:
            GGs = cx.sb("GGs", [128, NT, 512], BF16)
            for t0 in range(0, NT, 16):
                t1 = min(NT, t0 + 16)
                cx.load(GGs, GGs[:, t0:t1, :], sc["GG"], sc["GG"][t0 * 128:t1 * 128, :].rearrange("(t p) c -> p t c", p=128))
            iota = cx.sb("iota", [128, 512], F32)
            cx.load(iota, iota[:], self.G["iota512"], self.G["iota512"][:, :])
            lv = cx.sb("lv", [128, NT], F32)
            cx.load(lv, lv[:], seq.c["lvals"], seq.c["lvals"][:, :])
            sp = cx.sb("sp", [128, NT * NB], F32)
            cx.load(sp, sp[:], seq.c["dftsp"], seq.c["dftsp"][:, :])
            bb = cx.sb("bb", [128, 256], F32)
            cx.load(bb, bb[:], W["b_b"], W["b_b"][0].rearrange("g c -> (g c)").partition_broadcast(128))
            t1r = cx.ring("t1", 3, [128, 512], F32)
            qr = cx.ring("q", 3, [128, 512], I32)
            rr = cx.ring("r", 3, [128, 512], F32)
            rar = cx.ring("ra", 3, [128, 512], F32)
            Cr = cx.ring("C", 3, [128, 512], BF16)
            Sr = cx.ring("Sn", 3, [128, 512], BF16)
            ACC = [cx.ps("acc%d" % i, [128, 512], F32) for i in range(4)]
            yb_ring = cx.ring("yb", 2, [128, 256], BF16)
            w0 = 2 * math.pi / L * (1.0 - 2e-6)
            isq = 1.0 / math.sqrt(L)
            def blk(B):
                n0 = B * 512
                wd = min(512, Lp - n0)
                nsub = wd // 128
                for lt in range(NT):
                    r = seq.rows(lt)
                    t1 = t1r.next(); q = qr.next(); rt = rr.next(); ra = rar.next(); C = Cr.next(); Sn = Sr.next()
                    idx = lt * NB + B
                    S.op("dve", lambda e, t1=t1, lt=lt, idx=idx: e.tensor_scalar(out=t1[:, 0:wd], in0=iota[:, 0:wd], scalar1=lv[:, lt:lt + 1],
                                                                                   scalar2=sp[:, idx:idx + 1], op0=ALU.mult, op1=ALU.add),
                         [iota, lv, sp], [t1])
                    S.op("dve", lambda e, t1=t1, q=q: e.tensor_scalar(out=q[:, 0:wd], in0=t1[:, 0:wd], scalar1=1.0 / L, scalar2=None,
                                                                        op0=ALU.mult), [t1], [q])
                    S.op("dve", lambda e, t1=t1, q=q, rt=rt: e.scalar_tensor_tensor(out=rt[:, 0:wd], in0=q[:, 0:wd], scalar=-float(L),
                                                                                    in1=t1[:, 0:wd], op0=ALU.mult, op1=ALU.add),
                         [q, t1], [rt])
                    S.op("dve", lambda e, rt=rt, ra=ra: e.scalar_tensor_tensor(out=ra[:, 0:wd], in0=rt[:, 0:wd], scalar=-1.0, in1=rt[:, 0:wd], op0=ALU.mult, op1=ALU.max),
                         [rt], [ra])
                    S.op("act", lambda e, rt=rt, Sn=Sn: e.activation(out=Sn[:, 0:wd], in_=rt[:, 0:wd], func=AF.Sin, scale=-w0), [rt], [Sn])
                    S.op("act", lambda e, ra=ra, C=C: e.activation(out=C[:, 0:wd], in_=ra[:, 0:wd], func=AF.Sin, bias=K["halfpi"][:, 0:1],
                                                                    scale=-w0), [ra, K["halfpi"]], [C])
                    for s in range(nsub):
                        S.op("pe", lambda e, s=s, C=C, lt=lt, r=r: e.matmul(ACC[s][:, 0:256], lhsT=C[0:r, s * 128:(s + 1) * 128],
                                                                             rhs=GGs[0:r, lt, 0:256], start=(lt == 0), stop=False),
                             [C, GGs], [ACC[s]])
                        S.op("pe", lambda e, s=s, Sn=Sn, lt=lt, r=r: e.matmul(ACC[s][:, 0:256], lhsT=Sn[0:r, s * 128:(s + 1) * 128],
                                                                               rhs=GGs[0:r, lt, 256:512], start=False, stop=(lt == NT - 1)),
                             [Sn, GGs], [ACC[s]])
                for s in range(nsub):
                    yb = yb_ring.next()
                    S.op("dve", lambda e, s=s, yb=yb: e.scalar_tensor_tensor(out=yb[:], in0=ACC[s][:, 0:256], scalar=isq, in1=bb[:],
                                                                             op0=ALU.mult, op1=ALU.add), [ACC[s], bb], [yb])
                    cx.store(sc["YB"], sc["YB"][n0 + s * 128:n0 + (s + 1) * 128, :], yb, yb[:])

            for B in range(NB):
                blk(B)

    def post_norm_res(self, cx, PM, gpost, hin, hout, sm_ring, junk):
        S, K = self.S, self.K
        sm = sm_ring.next()
        for c in range(2):
            S.op("act", lambda e, c=c: e.activation(out=junk[:, 0:512], in_=PM[c][:], func=AF.Square, accum_out=sm[:, c:c + 1]),
                 [PM[c]], [junk, sm])
        S.op("dve", lambda e: e.tensor_tensor(out=sm[:, 0:1], in0=sm[:, 0:1], in1=sm[:, 1:2], op=ALU.add), [sm], [sm])
        self.rstd(cx, sm, sm[:, 0:1], sm, sm[:, 2:3], sm, sm[:, 3:4], 1.0 / D, K["eps"][:, 0:1])
        for c in range(2):
            S.op("dve", lambda e, c=c: e.scalar_tensor_tensor(out=hout[:, c * 512:(c + 1) * 512], in0=PM[c][:], scalar=sm[:, 2:3],
                                                              in1=gpost[:, c * 512:(c + 1) * 512], op0=ALU.mult, op1=ALU.mult),
                 [PM[c], sm, gpost], [hout])
        S.op("pool", lambda e: e.tensor_tensor(out=hout[:], in0=hout[:], in1=hin[:], op=ALU.add), [hout, hin], [hout])

    def phase_mix_ffn(self, seq, layer, hin_d, hout_d, ya, yb, wout, final):
        nc, S, W, K, sc = self.nc, self.S, self.W, self.K, self.sc
        NT, Lp = seq.NT, seq.Lp
        with Ctx(nc, S) as cx:
            stage = cx.ring("wst", 3, [128, 512], F32)
            Wo = cx.sb("wo", [128, 8, D], BF16)
            self.load_w(cx, Wo, wout, wout[0], 8, D, stage)
            Wg = cx.sb("wg", [128, 8, 2816], BF16)
            self.load_w(cx, Wg, W["ffn_w_gate"], W["ffn_w_gate"][layer], 8, 2816, stage)
            Wu = cx.sb("wu", [128, 8, 2816], BF16)
            self.load_w(cx, Wu, W["ffn_w_up"], W["ffn_w_up"][layer], 8, 2816, stage)
            Wd = cx.sb("wd", [128, 22, D], BF16)
            self.load_w(cx, Wd, W["ffn_w_down"], W["ffn_w_down"][layer], 22, D, stage)
            gs = {}
            for nm in ("post_mix_g", "pre_ffn_g", "post_ffn_g"):
                gs[nm] = cx.sb(nm, [128, D], F32)
                cx.load(gs[nm], gs[nm][:], W[nm], W[nm][layer:layer + 1, :].partition_broadcast(128))
            y_ring = cx.ring("y", 2, [128, D], BF16)
            h_ring = cx.ring("h", 1, [128, D], F32)
            TP = cx.ps("TP", [128, 8, 128], BF16)
            yT_ring = cx.ring("yT", 1, [128, 8, 128], BF16)
            PM = [cx.ps("PM%d" % c, [128, 512], F32) for c in range(2)]
            sm_ring = cx.ring("sm", 4, [128, 4], F32)
            junk = cx.sb("junk", [128, D], BF16)
            h1_ring = cx.ring("h1", 1, [128, D], F32)
            xs_ring = cx.ring("xs", 1, [128, D], BF16)
            hT_ring = cx.ring("hT", 1, [128, 8, 128], BF16)
            PGt = cx.ring("PGt", 2, [128, 4, 128], F32, ps=True)
            PUt = cx.ring("PUt", 2, [128, 4, 128], F32, ps=True)
            sg_ring = cx.ring("sg", 2, [128, 4, 128], F32)
            AT_ring = cx.ring("AT", 1, [128, 22, 128], BF16)
            h2_ring = cx.ring("h2", 1, [128, D], F32)
            vmask = cx.sb("vmask", [128, 1], F32)
            S.op("dve", lambda e: e.memset(vmask[:], 0.0), [], [vmask])
            S.op("dve", lambda e: e.memset(vmask[0:seq.last, :], 1.0), [], [vmask])
            def body(t):
                c0 = t * 128
                y = y_ring.next()
                cx.load(y, y[:, 0:ya[1]], sc[ya[0]], sc[ya[0]][c0:c0 + 128, :])
                cx.load(y, y[:, ya[1]:D], sc[yb[0]], sc[yb[0]][c0:c0 + 128, :])
                h = h_ring.next()
                cx.load(h, h[:], hin_d, hin_d[c0:c0 + 128, :])
                for k in range(8):
                    S.op("pe", lambda e, k=k: e.transpose(out=TP[:, k, :], in_=y[:, k * 128:(k + 1) * 128], identity=K["identb"][:]),
                         [y, K["identb"]], [TP])
                yT = yT_ring.next()
                S.op("act", lambda e: e.copy(out=yT[:], in_=TP[:]), [TP], [yT])
                for c in range(2):
                    for k in range(8):
                        S.op("pe", lambda e, c=c, k=k: e.matmul(PM[c][:], lhsT=yT[:, k, :], rhs=Wo[:, k, c * 512:(c + 1) * 512],
                                                                 start=(k == 0), stop=(k == 7)), [yT, Wo], [PM[c]])
                h1 = h1_ring.next()
                self.post_norm_res(cx, PM, gs["post_mix_g"], h, h1, sm_ring, junk)
                hT = hT_ring.next()
                self.norm_transpose(cx, h1, gs["pre_ffn_g"], hT, TP, xs_ring, sm_ring, junk)
                AT = AT_ring.next()
                for fq in range(6):
                    nf = min(4, 22 - fq * 4)
                    pg = PGt.next(); pu = PUt.next()
                    for (pp, Wm) in ((pg, Wg), (pu, Wu)):
                        for f in range(nf):
                            fc = (fq * 4 + f) * 128
                            for k in range(8):
                                S.op("pe", lambda e, pp=pp, Wm=Wm, f=f, fc=fc, k=k: e.matmul(pp[:, f, :], lhsT=Wm[:, k, fc:fc + 128], rhs=hT[:, k, :],
                                                                                              start=(k == 0), stop=(k == 7)), [Wm, hT], [pp])
                    sg = sg_ring.next()
                    S.op("act", lambda e, pg=pg, sg=sg, nf=nf: e.activation(out=sg[:, 0:nf, :], in_=pg[:, 0:nf, :], func=AF.Silu), [pg], [sg])
                    S.op("dve", lambda e, pu=pu, sg=sg, nf=nf, fq=fq: e.tensor_tensor(out=AT[:, fq * 4:fq * 4 + nf, :], in0=sg[:, 0:nf, :],
                                                                                      in1=pu[:, 0:nf, :], op=ALU.mult), [sg, pu], [AT])
                for c in range(2):
                    for f in range(22):
                        S.op("pe", lambda e, c=c, f=f: e.matmul(PM[c][:], lhsT=AT[:, f, :], rhs=Wd[:, f, c * 512:(c + 1) * 512],
                                                                 start=(f == 0), stop=(f == 21)), [AT, Wd], [PM[c]])
                h2 = h2_ring.next()
                self.post_norm_res(cx, PM, gs["post_ffn_g"], h1, h2, sm_ring, junk)
                if not final:
                    if t == NT - 1:
                        S.op("dve", lambda e, h2=h2: e.tensor_scalar(out=h2[:], in0=h2[:], scalar1=vmask[:, 0:1], scalar2=None, op0=ALU.mult),
                             [h2, vmask], [h2])
                    cx.store(hout_d, hout_d[c0:c0 + 128, :], h2, h2[:])
                else:
                    lo = 16 if t == 0 else 0
                    hi = seq.rows(t)
                    cx.store(hout_d, hout_d[c0 + lo - 16:c0 + hi - 16, :], h2, h2[lo:hi, :])

            for t in range(NT):
                body(t)


    def phase_odd_proj(self, seq):
        nc, S, W, K, sc = self.nc, self.S, self.W, self.K, self.sc
        NT, Lp = seq.NT, seq.Lp
        with Ctx(nc, S) as cx:
            stage = cx.ring("wst", 3, [128, 512], F32)
            Win = cx.sb("win", [128, 8, 2688], BF16)
            self.load_w(cx, Win, W["odd_w_in"], W["odd_w_in"][0], 8, 2688, stage)
            gpre = cx.sb("gpre", [128, D], F32)
            cx.load(gpre, gpre[:], W["pre_mix_g"], W["pre_mix_g"][1:2, :].partition_broadcast(128))
            zc = cx.sb("zc", [128, 15], F32)
            S.op("dve", lambda e: e.memset(zc[:], 0.0), [], [zc])
            UTv = sc["UT"].t.rearrange("(b p) n -> p b n", p=128)
            cx.store(sc["UT"], UTv[:, :, 0:1], zc, zc[:].unsqueeze(2), slow=True)
            cx.store(sc["UT"], UTv[:, :, Lp + 1:Lp + 2], zc, zc[:].unsqueeze(2), slow=True)
            xring = cx.ring("x", 2, [128, D], F32)
            xs_ring = cx.ring("xs", 2, [128, D], BF16)
            sm_ring = cx.ring("sm", 2, [128, 4], F32)
            junk = cx.sb("junk", [128, D], BF16)
            TP = cx.ps("TP", [128, 8, 128], BF16)
            hnT_ring = cx.ring("hnT", 2, [128, 8, 128], BF16)
            PJ = [cx.ps("PJ%d" % c, [128, 512], F32) for c in range(2)]
            PU = cx.ring("PU", 2, [128, 4, 128], F32, ps=True)
            rope_ring = cx.ring("rope", 2, [128, 2, 8], F32)
            sqt = cx.sb("sqt", [128, 8, 64], F32)
            bq_ring = cx.ring("bq", 2, [128, 2, 8], F32)
            qc = cx.sb("qc", [128, 10, 64], F32)
            tA = cx.sb("tA", [128, 10, 8], F32)
            tB = cx.sb("tB", [128, 10, 8], F32)
            qa_ring = cx.ring("qa", 2, [128, 10, 65], BF16)
            for qa in qa_ring.tiles:
                S.op("pool", lambda e, qa=qa: e.memset(qa[:], 1.0), [], [qa])
            vc_ring = cx.ring("vc", 2, [128, 2, 65], BF16)
            for vc in vc_ring.tiles:
                S.op("pool", lambda e, vc=vc: e.memset(vc[:], 1.0), [], [vc])
            TP2 = cx.ps("TP2", [128, 5, 128], BF16)
            TP3 = cx.ps("TP3", [128, 5, 128], BF16)
            qT_ring = cx.ring("qT", 2, [65, 10, 128], BF16)
            u_ring = cx.ring("usb", 2, [128, 15, 128], F32)

            def body(t):
                c0 = t * 128
                xt = xring.next()
                cx.load(xt, xt[:], sc["H1"], sc["H1"][c0:c0 + 128, :])
                rp = rope_ring.next()
                cx.load(rp, rp[:], seq.c["ropeC"], seq.c["ropeC"][c0:c0 + 128])
                hnT = hnT_ring.next()
                self.norm_transpose(cx, xt, gpre, hnT, TP, xs_ring, sm_ring, junk)
                for c, (lo, wdt) in enumerate(((0, 512), (512, 256))):
                    for k in range(8):
                        S.op("pe", lambda e, c=c, k=k, lo=lo, wdt=wdt: e.matmul(PJ[c][:, 0:wdt], lhsT=hnT[:, k, :], rhs=Win[:, k, lo:lo + wdt],
                                                                                 start=(k == 0), stop=(k == 7)), [hnT, Win], [PJ[c]])
                usb = u_ring.next()
                for q4 in range(4):
                    nb = min(4, 15 - q4 * 4)
                    pu = PU.next()
                    for b in range(nb):
                        f0 = 768 + (q4 * 4 + b) * 128
                        for k in range(8):
                            S.op("pe", lambda e, pu=pu, b=b, f0=f0, k=k: e.matmul(pu[:, b, :], lhsT=Win[:, k, f0:f0 + 128], rhs=hnT[:, k, :],
                                                                                  start=(k == 0), stop=(k == 7)), [Win, hnT], [pu])
                    S.op("act" if q4 % 2 else "dve", (lambda e, pu=pu, q4=q4, nb=nb: e.copy(out=usb[:, q4 * 4:q4 * 4 + nb, :], in_=pu[:, 0:nb, :])) if q4 % 2 else
                         (lambda e, pu=pu, q4=q4, nb=nb: e.tensor_copy(out=usb[:, q4 * 4:q4 * 4 + nb, :], in_=pu[:, 0:nb, :])), [pu], [usb])
                cx.store(sc["UT"], UTv[:, :, 1 + c0:1 + c0 + 128], usb, usb[:])
                S.op("act", lambda e: e.activation(out=sqt[:], in_=PJ[0][:].rearrange("p (h d) -> p h d", d=64), func=AF.Square), [PJ[0]], [sqt])
                bq = bq_ring.next()
                S.op("dve", lambda e: e.tensor_reduce(out=bq[:, 0, :], in_=sqt[:], axis=AX.X, op=ALU.add), [sqt], [bq])
                S.op("dve", lambda e: e.tensor_scalar(out=bq[:, 1, :], in0=bq[:, 0, :], scalar1=1.0 / 16, scalar2=None, op0=ALU.mult), [bq], [bq])
                cx.store(sc["BQ"], sc["BQ"][c0:c0 + 128, :], bq, bq[:, 1, :])
                S.op("act", lambda e: e.activation(out=qc[:, 0:8, :], in_=PJ[0][:].rearrange("p (h d) -> p h d", d=64), func=AF.Copy, scale=0.125), [PJ[0]], [qc])
                S.op("act", lambda e: e.copy(out=qc[:, 8:10, :], in_=PJ[1][:, 0:128].rearrange("p (h d) -> p h d", d=64)), [PJ[1]], [qc])
                qa = qa_ring.next()
                x1, x2 = qc[:, :, 0:8], qc[:, :, 8:16]
                cosb = rp[:, 0, :].unsqueeze(1).broadcast_to([128, 10, 8])
                sinb = rp[:, 1, :].unsqueeze(1).broadcast_to([128, 10, 8])
                S.op("dve", lambda e: e.tensor_tensor(out=tA[:], in0=x1, in1=cosb, op=ALU.mult), [qc, rp], [tA])
                S.op("pool", lambda e: e.tensor_tensor(out=tB[:], in0=x2, in1=sinb, op=ALU.mult), [qc, rp], [tB])
                S.op("dve", lambda e: e.tensor_tensor(out=qa[:, :, 0:8], in0=tA[:], in1=tB[:], op=ALU.subtract), [tA, tB], [qa])
                S.op("dve", lambda e: e.tensor_tensor(out=tA[:], in0=x1, in1=sinb, op=ALU.mult), [qc, rp], [tA])
                S.op("pool", lambda e: e.tensor_tensor(out=tB[:], in0=x2, in1=cosb, op=ALU.mult), [qc, rp], [tB])
                S.op("dve", lambda e: e.tensor_tensor(out=qa[:, :, 8:16], in0=tA[:], in1=tB[:], op=ALU.add), [tA, tB], [qa])
                S.op("act", lambda e: e.copy(out=qa[:, :, 16:64], in_=qc[:, :, 16:64]), [qc], [qa])
                S.op("dve", lambda e: e.tensor_scalar(out=qa[:, 0:8, 64], in0=bq[:, 1, :], scalar1=-1.0, scalar2=None, op0=ALU.mult), [bq], [qa])
                for hh in range(10):
                    tp = TP2 if hh < 5 else TP3
                    S.op("pe", lambda e, hh=hh, tp=tp: e.transpose(out=tp[0:65, hh % 5, :], in_=qa[:, hh, :], identity=K["identb"][:]),
                         [qa, K["identb"]], [tp])
                qT = qT_ring.next()
                S.op("act", lambda e: e.copy(out=qT[:, 0:5, :], in_=TP2[0:65, :, :]), [TP2], [qT])
                S.op("dve", lambda e: e.tensor_copy(out=qT[:, 5:10, :], in_=TP3[0:65, :, :]), [TP3], [qT])
                cx.store(sc["QCT"], sc["QCT"][:, :, c0:c0 + 128].rearrange("h d n -> d h n"), qT, qT[:, 0:8, :])
                cx.store(sc["KCT"], sc["KCT"][:, :, c0:c0 + 128].rearrange("h d n -> d h n"), qT, qT[:, 8:10, :])
                vc = vc_ring.next()
                S.op("act", lambda e: e.copy(out=vc[:, :, 0:64], in_=PJ[1][:, 128:256].rearrange("p (h d) -> p h d", d=64)), [PJ[1]], [vc])
                cx.store(sc["VCX"], sc["VCX"][c0:c0 + 128, :, :], vc, vc[:])

            for t in range(NT):
                body(t)

    def phase_attn_c(self, seq):
        nc, S, W, K, sc = self.nc, self.S, self.W, self.K, self.sc
        NT, Lp = seq.NT, seq.Lp
        with Ctx(nc, S) as cx:
            KCs = cx.sb("KCs", [65, 2, Lp], BF16)
            cx.load(KCs, KCs[:], sc["KCT"], sc["KCT"][:, :, 0:Lp].rearrange("h d n -> d h n"))
            VCs = cx.sb("VCs", [128, NT, 2, 65], BF16)
            for t0 in range(0, NT, 16):
                t1_ = min(NT, t0 + 16)
                cx.load(VCs, VCs[:, t0:t1_], sc["VCX"], sc["VCX"][t0 * 128:t1_ * 128].rearrange("(t p) h d -> p t h d", p=128))
            cm = cx.sb("cm", [128, 3, 128], F32)
            cx.load(cm, cm[:], self.G["cmask"], self.G["cmask"][:, :, :])
            cmb = cx.sb("cmb", [128, 3, 128], BF16)
            S.op("dve", lambda e: e.tensor_copy(out=cmb[:], in_=cm[:]), [cm], [cmb])
            snk = cx.sb("snk", [128, 8], F32)
            cx.load(snk, snk[:], W["c_sink"], W["c_sink"][0:1, :].partition_broadcast(128))
            q_ring = cx.ring("qc", 2, [65, 8, 128], BF16)
            bq_ring = cx.ring("bq", 2, [128, 8], F32)
            es_ring = cx.ring("es", 2, [128, 8], F32)
            ST = cx.ring("ST", 2, [128, 4, 128], F32, ps=True)
            OT = cx.ring("OT", 2, [65, 4, 128], F32, ps=True)
            PT = cx.ring("PT", 3, [128, 4, 128], BF16)
            osb_ring = cx.ring("osb", 2, [65, 4, 128], F32)
            TPo = cx.ps("TPo", [128, 4, 65], F32)
            den_ring = cx.ring("den", 2, [128, 2, 4], F32)
            y_ring = cx.ring("yc", 2, [128, 8, 64], BF16)

            def body(t):
                c0 = t * 128
                qt = q_ring.next()
                cx.load(qt, qt[:], sc["QCT"], sc["QCT"][:, :, c0:c0 + 128].rearrange("h d n -> d h n"))
                bq = bq_ring.next()
                cx.load(bq, bq[:], sc["BQ"], sc["BQ"][c0:c0 + 128, :])
                es = es_ring.next()
                S.op("dve", lambda e: e.tensor_tensor(out=es[:], in0=snk[:], in1=bq[:], op=ALU.subtract), [snk, bq], [es])
                S.op("act", lambda e: e.activation(out=es[:], in_=es[:], func=AF.Exp), [es], [es])
                yc = y_ring.next()
                kbs = []
                if t >= 2:
                    kbs.append((0, 16, None))
                if t >= 1:
                    kbs.append((t - 1, 128, 2 if t == 1 else 0))
                kbs.append((t, seq.rows(t), None))
                if t + 1 < NT:
                    kbs.append((t + 1, seq.rows(t + 1), 1))
                def grp(g):
                    ot = OT.next()
                    for bi, (kt, r, mk) in enumerate(kbs):
                        st = ST.next()
                        S.op("pe", lambda e, st=st, kt=kt, r=r, g=g: e.matmul(st[0:r], lhsT=KCs[:, g, kt * 128:kt * 128 + r], rhs=qt[:, g * 4:(g + 1) * 4, :],
                                                                               start=True, stop=True), [KCs, qt], [st])
                        pt = PT.next()
                        S.op("act", lambda e, st=st, pt=pt, r=r: e.activation(out=pt[0:r], in_=st[0:r], func=AF.Exp), [st], [pt])
                        if mk is not None:
                            S.op("dve", lambda e, pt=pt, r=r, mk=mk: e.tensor_tensor(out=pt[0:r], in0=pt[0:r],
                                                                                     in1=cmb[0:r, mk, :].unsqueeze(1).broadcast_to([r, 4, 128]), op=ALU.mult),
                                 [pt, cmb], [pt])
                        S.op("pe", lambda e, pt=pt, kt=kt, r=r, g=g, bi=bi: e.matmul(ot[:], lhsT=VCs[0:r, kt, g, :], rhs=pt[0:r], start=(bi == 0),
                                                                                      stop=(bi == len(kbs) - 1)), [VCs, pt], [ot])
                    osb = osb_ring.next()
                    S.op("dve", lambda e, osb=osb, ot=ot: e.tensor_copy(out=osb[:], in_=ot[:]), [ot], [osb])
                    for hh in range(4):
                        S.op("pe", lambda e, hh=hh, osb=osb: e.transpose(out=TPo[:, hh, :], in_=osb[:, hh, :], identity=K["identf"][0:65, 0:65]),
                             [osb, K["identf"]], [TPo])
                    den = den_ring.next()
                    S.op("dve", lambda e, den=den, g=g: e.tensor_tensor(out=den[:, 0, :], in0=TPo[:, :, 64], in1=es[:, g * 4:(g + 1) * 4], op=ALU.add),
                         [TPo, es], [den])
                    S.op("dve", lambda e, den=den: e.reciprocal(out=den[:, 1, :], in_=den[:, 0, :]), [den], [den])
                    S.op("dve", lambda e, den=den, g=g: e.tensor_tensor(out=yc[:, g * 4:(g + 1) * 4, :], in0=TPo[:, :, 0:64],
                                                                        in1=den[:, 1, :].unsqueeze(2).broadcast_to([128, 4, 64]), op=ALU.mult),
                         [TPo, den], [yc])
                for g in range(2):
                    grp(g)
                cx.store(sc["YC"], sc["YC"][c0:c0 + 128, :], yc, yc[:].rearrange("p h d -> p (h d)"))

            for t in range(NT):
                body(t)

    def phase_rwkv(self, seq, d):
        nc, S, W, K, sc, G = self.nc, self.S, self.W, self.K, self.sc, self.G
        NT, Lp, L, NB = seq.NT, seq.Lp, seq.L, seq.NB
        NEG = -math.exp(-0.5)
        with Ctx(nc, S) as cx:
            m4 = cx.sb("m4", [64, 4, 64], F32)
            cx.load(m4, m4[:], G["m4"], G["m4"][:, :, :])
            reset = cx.sb("reset", [64, 512], F32)
            cx.load(reset, reset[:], G["reset"], G["reset"][:, :])
            ones64 = cx.sb("ones64", [64, 64], F32)
            S.op("dve", lambda e: e.memset(ones64[:], 1.0), [], [ones64])
            onesb = cx.sb("onesb", [64, 2], BF16)
            S.op("dve", lambda e: e.memset(onesb[:], 1.0), [], [onesb])
            lneps = cx.sb("lneps", [64, 1], F32)
            S.op("dve", lambda e: e.memset(lneps[:], 64e-5), [], [lneps])
            zero1 = cx.sb("zero1", [64, 1], F32)
            S.op("dve", lambda e: e.memset(zero1[:], 0.0), [], [zero1])
            mu = cx.sb("mu", [64, 3, 24], F32)
            cx.load(mu, mu[:, 0, :], W["d_mu_prev"], W["d_mu_prev"][0, 0:1536].rearrange("(s p) -> p s", p=64))
            cx.load(mu, mu[:, 1, :], W["d_mu_next"], W["d_mu_next"][0, 0:1536].rearrange("(s p) -> p s", p=64))
            muL = cx.sb("muL", [128, 3, 3], F32)
            cx.load(muL, muL[:, 0, :], W["d_mu_prev"], W["d_mu_prev"][0, 1536:1920].rearrange("(s p) -> p s", p=128))
            cx.load(muL, muL[:, 1, :], W["d_mu_next"], W["d_mu_next"][0, 1536:1920].rearrange("(s p) -> p s", p=128))
            for m_ in (mu, muL):
                S.op("dve", lambda e, m_=m_: e.tensor_tensor(out=m_[:, 2, :], in0=m_[:, 0, :], in1=m_[:, 1, :], op=ALU.add), [m_], [m_])
                S.op("dve", lambda e, m_=m_: e.tensor_scalar(out=m_[:, 2, :], in0=m_[:, 2, :], scalar1=-1.0, scalar2=1.0, op0=ALU.mult, op1=ALU.add), [m_], [m_])
            pc = cx.sb("pc", [64, 8, 8], F32)
            cx.load(pc, pc[:, 0:2, :], W["d_w0"], W["d_w0"][0].rearrange("d (h p) -> p d h", p=64))
            cx.load(pc, pc[:, 2:4, :], W["d_a0"], W["d_a0"][0].rearrange("d (h p) -> p d h", p=64))
            cx.load(pc, pc[:, 4, :], W["d_k_k"], W["d_k_k"][0].rearrange("(h p) -> p h", p=64))
            cx.load(pc, pc[:, 5, :], W["d_k_a"], W["d_k_a"][0].rearrange("(h p) -> p h", p=64))
            cx.load(pc, pc[:, 7, :], W["d_r_k"], W["d_r_k"][0].rearrange("h p -> p h"))
            S.op("dve", lambda e: e.tensor_scalar(out=pc[:, 6, :], in0=pc[:, 5, :], scalar1=-1.0, scalar2=1.0, op0=ALU.mult, op1=ALU.add), [pc], [pc])
            stg = cx.sb("stg", [128, 512], F32)
            wup = cx.sb("wup", [128, 512], BF16)
            aup = cx.sb("aup", [128, 512], BF16)
            gup = cx.sb("gup", [128, 512], BF16)
            for (dst, srcs) in ((wup, [(W["d_w_up"], W["d_w_up"][0, 0]), (W["d_w_up"], W["d_w_up"][0, 1])]),
                                (aup, [(W["d_a_up"], W["d_a_up"][0, 0]), (W["d_a_up"], W["d_a_up"][0, 1])]),
                                (gup, [(W["d_g_up"], W["d_g_up"][0, 0:64]), (W["d_g_up"], W["d_g_up"][0, 64:128])])):
                for i_, (tl_, ap_) in enumerate(srcs):
                    cx.load(stg, stg[i_ * 64:(i_ + 1) * 64, :], tl_, ap_)
                S.op("dve", lambda e, dst=dst: e.tensor_copy(out=dst[:], in_=stg[:]), [stg], [dst])
            lng = cx.sb("lng", [64, 512], F32)
            lnb = cx.sb("lnb", [64, 512], F32)
            cx.load(lng, lng[:], W["d_ln_g"], W["d_ln_g"][0:1, :].partition_broadcast(64))
            cx.load(lnb, lnb[:], W["d_ln_b"], W["d_ln_b"][0:1, :].partition_broadcast(64))
            idb64 = K["identb"][0:64, 0:64]
            idf64 = K["identf"][0:64, 0:64].unsqueeze(1).broadcast_to([64, 8, 64])

            TPa = cx.ps("TPa", [64, 2, 8, 64], BF16)
            TPb = cx.ps("TPb", [64, 2, 8, 64], BF16)
            GP = cx.ring("GP", 5, [64, 8, 64], F32, ps=True)
            PP = cx.ps("PP", [64, 512], F32)

            TW = cx.sb("TW", [128, 512], BF16)
            AW = cx.sb("AW", [128, 512], BF16)
            SG = cx.sb("SG", [128, 512], BF16)
            ubL = cx.sb("ubL", [128, 514], F32)
            mlt = cx.sb("mlt", [128, 512], F32)
            ub3 = cx.sb("ub3", [64, 3, 514], F32)
            f32t = {n: cx.sb(n, [64, 512], F32) for n in ("rr", "kr", "vv", "lw", "cu", "e1", "e2", "e3", "aa", "kn", "t1", "kd", "bb", "a0t")}
            RT = cx.sb("RT", [64, 8, 512], BF16)
            KT = cx.sb("KT", [64, 8, 512], BF16)
            BT = cx.sb("BT", [64, 8, 512], BF16)
            AT = cx.sb("AT", [64, 8, 512], BF16)
            VT = cx.sb("VT", [64, 8, 512], BF16)
            RKD = cx.sb("RKD", [64, 8, 512], BF16)
            GE = cx.sb("GE", [64, 8, 8], F32)
            Mf = cx.sb("Mf", [64, 8, 64], F32)
            Mb = cx.sb("Mb", [64, 8, 64], BF16)
            S.op("dve", lambda e: e.memset(Mf[:], 0.0), [], [Mf])
            S.op("dve", lambda e: e.memset(Mb[:], 0.0), [], [Mb])
            tm_ring = cx.ring("tm", 2, [64, 4, 8, 64], BF16)
            bt = lambda nm, n=2: cx.ring(nm, n, [64, 8, 64], BF16)
            Yr, P0r, Lakr, Lrbr, Lrkr = bt("Y"), bt("P0"), bt("Lak"), bt("Lrb"), bt("Lrk")
            ybuf, pbuf, rbuf = bt("yb", 2), bt("pb", 2), bt("rb", 3)
            Xr, Wr, Ur = bt("X"), bt("WT"), bt("U")
            o_ring = cx.ring("o", 2, [64, 8, 64], F32)
            of_ring = cx.ring("of", 2, [64, 8, 64], F32)
            cen = cx.sb("cen", [64, 8, 64], F32)
            sqn = cx.sb("sqn", [64, 8, 64], F32)
            st_ring = cx.ring("st", 2, [64, 6, 8], F32)
            yd_ring = cx.ring("yd", 2, [64, 8, 64], BF16)
            UT = sc["UT"]
            UT3 = UT.t[0:1536, :].rearrange("(s q) n -> q s n", q=512)
            strict, incl, strictT = (0, 1, 2) if d == 0 else (2, 3, 0)
            bc8 = lambda ap: ap.unsqueeze(2).broadcast_to([64, 8, 64])

            def mix(dst_ap, ub_prev, ub_c, ub_next, mu_t, idx, dst_tl, ub_tl):
                S.op("dve", lambda e: e.tensor_scalar(out=dst_ap, in0=ub_c, scalar1=mu_t[:, 2, idx:idx + 1], scalar2=None, op0=ALU.mult), [ub_tl, mu_t], [dst_tl])
                S.op("dve", lambda e: e.scalar_tensor_tensor(out=dst_ap, in0=ub_prev, scalar=mu_t[:, 0, idx:idx + 1], in1=dst_ap, op0=ALU.mult, op1=ALU.add),
                     [ub_tl, mu_t, dst_tl], [dst_tl])
                S.op("dve", lambda e: e.scalar_tensor_tensor(out=dst_ap, in0=ub_next, scalar=mu_t[:, 1, idx:idx + 1], in1=dst_ap, op0=ALU.mult, op1=ALU.add),
                     [ub_tl, mu_t, dst_tl], [dst_tl])

            def prep(bi):
                n0 = bi * 512
                Wd = min(512, Lp - n0)
                nch = Wd // 64
                pv = min(Wd, max(0, L - n0))
                for li, (dst, fn_) in enumerate(((TW, AF.Tanh), (AW, AF.Copy), (SG, AF.Sigmoid))):
                    cx.load(ubL, ubL[:, 0:Wd + 2], UT, UT[1536 + li * 128:1536 + (li + 1) * 128, n0:n0 + Wd + 2])
                    mix(mlt[:, 0:Wd], ubL[:, 0:Wd], ubL[:, 1:Wd + 1], ubL[:, 2:Wd + 2], muL, li, mlt, ubL)
                    S.op("act", lambda e, dst=dst, fn_=fn_: e.activation(out=dst[:, 0:Wd], in_=mlt[:, 0:Wd], func=fn_), [mlt], [dst])
                for h in range(8):
                    prep_head(h, n0, Wd, nch, pv)

            def prep_head(h, n0, Wd, nch, pv):
                T = f32t
                w = slice(0, Wd)
                cx.load(ub3, ub3[:, :, 0:Wd + 2], UT, UT3[h * 64:(h + 1) * 64, :, n0:n0 + Wd + 2])
                for sec, nm in ((0, "rr"), (1, "kr"), (2, "vv")):
                    mix(T[nm][:, w], ub3[:, sec, 0:Wd], ub3[:, sec, 1:Wd + 1], ub3[:, sec, 2:Wd + 2], mu, sec * 8 + h, T[nm], ub3)
                    if pv < Wd:
                        S.op("pool", lambda e, nm=nm: e.memset(T[nm][:, pv:Wd], 0.0), [], [T[nm]])
                hs = slice(h * 64, (h + 1) * 64)
                ds_ = slice(d * 64, (d + 1) * 64)
                S.op("pe", lambda e: e.matmul(PP[:, w], lhsT=wup[ds_, hs], rhs=TW[ds_, w], start=True, stop=True), [wup, TW], [PP])
                S.op("act", lambda e: e.activation(out=T["lw"][:, w], in_=PP[:, w], func=AF.Sigmoid, bias=pc[:, d, h:h + 1], scale=1.0), [PP, pc], [T["lw"]])
                S.op("dve", lambda e: e.tensor_scalar(out=T["lw"][:, w], in0=T["lw"][:, w], scalar1=NEG, scalar2=None, op0=ALU.mult), [T["lw"]], [T["lw"]])
                S.op("dve", lambda e: e.tensor_tensor_scan(out=T["cu"][:, w], data0=reset[:, w], data1=T["lw"][:, w], initial=0.0, op0=ALU.mult, op1=ALU.add),
                     [reset, T["lw"]], [T["cu"]])
                if d == 1:
                    cu3 = T["cu"][:, w].rearrange("p (c t) -> p c t", t=64)
                    S.op("dve", lambda e: e.tensor_tensor(out=T["t1"][:, w], in0=T["lw"][:, w], in1=T["cu"][:, w], op=ALU.subtract), [T["lw"], T["cu"]], [T["t1"]])
                    S.op("dve", lambda e: e.tensor_tensor(out=T["e1"][:, w].rearrange("p (c t) -> p c t", t=64), in0=T["t1"][:, w].rearrange("p (c t) -> p c t", t=64),
                                                          in1=cu3[:, :, 63:64].broadcast_to([64, nch, 64]), op=ALU.add), [T["t1"], T["cu"]], [T["e1"]])
                    S.op("dve", lambda e: e.tensor_copy(out=T["cu"][:, w], in_=T["e1"][:, w]), [T["e1"]], [T["cu"]])
                S.op("act", lambda e: e.activation(out=T["e1"][:, w], in_=T["cu"][:, w], func=AF.Exp), [T["cu"]], [T["e1"]])
                S.op("act", lambda e: e.activation(out=T["e2"][:, w], in_=T["cu"][:, w], func=AF.Exp, scale=-1.0), [T["cu"]], [T["e2"]])
                S.op("dve", lambda e: e.tensor_tensor(out=T["t1"][:, w], in0=T["cu"][:, w], in1=T["lw"][:, w], op=ALU.subtract), [T["cu"], T["lw"]], [T["t1"]])
                S.op("act", lambda e: e.activation(out=T["e3"][:, w], in_=T["t1"][:, w], func=AF.Exp), [T["t1"]], [T["e3"]])
                S.op("pe", lambda e: e.matmul(PP[:, w], lhsT=aup[ds_, hs], rhs=AW[ds_, w], start=True, stop=True), [aup, AW], [PP])
                S.op("act", lambda e: e.activation(out=T["aa"][:, w], in_=PP[:, w], func=AF.Sigmoid, bias=pc[:, 2 + d, h:h + 1], scale=1.0), [PP, pc], [T["aa"]])
                S.op("dve", lambda e: e.tensor_scalar(out=T["kn"][:, w], in0=T["kr"][:, w], scalar1=pc[:, 4, h:h + 1], scalar2=None, op0=ALU.mult), [T["kr"], pc], [T["kn"]])
                S.op("act", lambda e: e.activation(out=T["t1"][:, w], in_=T["kn"][:, w], func=AF.Square), [T["kn"]], [T["t1"]])
                S.op("pe", lambda e: e.matmul(PP[:, w], lhsT=ones64[:], rhs=T["t1"][:, w], start=True, stop=True), [ones64, T["t1"]], [PP])
                S.op("dve", lambda e: e.tensor_scalar(out=T["t1"][:, w], in0=PP[:, w], scalar1=1e-24, scalar2=None, op0=ALU.max), [PP], [T["t1"]])
                S.op("act", lambda e: e.activation(out=T["t1"][:, w], in_=T["t1"][:, w], func=AF.Sqrt, bias=zero1[:, 0:1], scale=1.0), [T["t1"], zero1], [T["t1"]])
                S.op("dve", lambda e: e.reciprocal(out=T["t1"][:, w], in_=T["t1"][:, w]), [T["t1"]], [T["t1"]])
                S.op("dve", lambda e: e.tensor_tensor(out=T["kn"][:, w], in0=T["kn"][:, w], in1=T["t1"][:, w], op=ALU.mult), [T["kn"], T["t1"]], [T["kn"]])
                S.op("dve", lambda e: e.tensor_scalar(out=T["kd"][:, w], in0=T["aa"][:, w], scalar1=pc[:, 5, h:h + 1], scalar2=pc[:, 6, h:h + 1], op0=ALU.mult, op1=ALU.add),
                     [T["aa"], pc], [T["kd"]])
                S.op("dve", lambda e: e.tensor_tensor(out=T["kd"][:, w], in0=T["kd"][:, w], in1=T["kr"][:, w], op=ALU.mult), [T["kd"], T["kr"]], [T["kd"]])
                S.op("pool", lambda e: e.tensor_tensor(out=T["bb"][:, w], in0=T["kn"][:, w], in1=T["aa"][:, w], op=ALU.mult), [T["kn"], T["aa"]], [T["bb"]])
                S.op("dve", lambda e: e.tensor_tensor(out=RT[:, h, w], in0=T["rr"][:, w], in1=T["e1"][:, w], op=ALU.mult), [T["rr"], T["e1"]], [RT])
                S.op("pool", lambda e: e.tensor_tensor(out=KT[:, h, w], in0=T["kd"][:, w], in1=T["e2"][:, w], op=ALU.mult), [T["kd"], T["e2"]], [KT])
                S.op("pool", lambda e: e.tensor_tensor(out=BT[:, h, w], in0=T["bb"][:, w], in1=T["e2"][:, w], op=ALU.mult), [T["bb"], T["e2"]], [BT])
                S.op("dve", lambda e: e.scalar_tensor_tensor(out=AT[:, h, w], in0=T["kn"][:, w], scalar=-1.0, in1=T["e3"][:, w], op0=ALU.mult, op1=ALU.mult),
                     [T["kn"], T["e3"]], [AT])
                S.op("act", lambda e: e.copy(out=VT[:, h, w], in_=T["vv"][:, w]), [T["vv"]], [VT])
                e13 = T["e1"][:, w].rearrange("p (c t) -> p c t", t=64)
                gcol = 63 if d == 0 else 0
                S.op("dve", lambda e: e.tensor_copy(out=GE[:, 0:nch, h], in_=e13[:, :, gcol]), [T["e1"]], [GE])
                if d == 1:
                    do = slice(0, 64)
                    S.op("pe", lambda e: e.matmul(PP[:, w], lhsT=aup[do, hs], rhs=AW[do, w], start=True, stop=True), [aup, AW], [PP])
                    S.op("act", lambda e: e.activation(out=T["a0t"][:, w], in_=PP[:, w], func=AF.Sigmoid, bias=pc[:, 2, h:h + 1], scale=1.0), [PP, pc], [T["a0t"]])
                    S.op("dve", lambda e: e.tensor_scalar(out=T["a0t"][:, w], in0=T["a0t"][:, w], scalar1=pc[:, 5, h:h + 1], scalar2=pc[:, 6, h:h + 1], op0=ALU.mult, op1=ALU.add),
                         [T["a0t"], pc], [T["a0t"]])
                    S.op("dve", lambda e: e.tensor_tensor(out=T["a0t"][:, w], in0=T["a0t"][:, w], in1=T["kr"][:, w], op=ALU.mult), [T["a0t"], T["kr"]], [T["a0t"]])
                    S.op("dve", lambda e: e.tensor_tensor(out=T["a0t"][:, w], in0=T["a0t"][:, w], in1=T["kd"][:, w], op=ALU.add), [T["a0t"], T["kd"]], [T["a0t"]])
                    S.op("dve", lambda e: e.scalar_tensor_tensor(out=RKD[:, h, w], in0=T["rr"][:, w], scalar=pc[:, 7, h:h + 1], in1=T["a0t"][:, w], op0=ALU.mult, op1=ALU.mult),
                         [T["rr"], pc, T["a0t"]], [RKD])

            def mm8(ps, lhs_fn, rhs_fn, lt, rt, extra=()):
                for h in range(8):
                    terms = [(lhs_fn, rhs_fn)] + list(extra)
                    for i_, (lf, rf) in enumerate(terms):
                        S.op("pe", lambda e, h=h, lf=lf, rf=rf, i_=i_, n_=len(terms): e.matmul(ps[:, h, :], lhsT=lf(h), rhs=rf(h), start=(i_ == 0), stop=(i_ == n_ - 1)),
                             lt, [ps])

            def chunk(n0, c):
                cs = slice(c * 64, (c + 1) * 64)
                row0 = n0 + c * 64
                for ki, X in enumerate((AT, BT)):
                    for h in range(8):
                        S.op("pe", lambda e, ki=ki, X=X, h=h: e.transpose(out=TPa[:, ki, h, :], in_=X[:, h, cs], identity=idb64), [X, K["identb"]], [TPa])
                for ki, X in enumerate((KT, VT)):
                    for h in range(8):
                        S.op("pe", lambda e, ki=ki, X=X, h=h: e.transpose(out=TPb[:, ki, h, :], in_=X[:, h, cs], identity=idb64), [X, K["identb"]], [TPb])
                tm = tm_ring.next()
                S.op("act", lambda e: e.copy(out=tm[:, 0:2], in_=TPa[:]), [TPa], [tm])
                S.op("dve", lambda e: e.tensor_copy(out=tm[:, 2:4], in_=TPb[:]), [TPb], [tm])
                A_tm, B_tm, K_tm, V_tm = (lambda h: tm[:, 0, h, :]), (lambda h: tm[:, 1, h, :]), (lambda h: tm[:, 2, h, :]), (lambda h: tm[:, 3, h, :])

                def score(Lh, Rh, mi, ring):
                    ps = GP.next()
                    mm8(ps, lambda h: Lh[:, h, cs], lambda h: Rh[:, h, cs], [Lh, Rh], None)
                    o = ring.next()
                    S.op("dve", lambda e: e.tensor_tensor(out=o[:], in0=ps[:], in1=m4[:, mi, :].unsqueeze(1).broadcast_to([64, 8, 64]), op=ALU.mult), [ps, m4], [o])
                    return o
                Y = score(BT, AT, strict, Yr)
                P0 = score(AT, BT, strictT, P0r)
                LakT = score(KT, AT, strict, Lakr)
                LrbT = score(BT, RT, incl, Lrbr)
                LrkT = score(KT, RT, incl, Lrkr)
                R = rbuf.next()
                S.op("dve", lambda e, R=R: e.tensor_tensor(out=R[:], in0=Y[:], in1=idf64, op=ALU.add), [Y, K["identf"]], [R])
                Yk, Pk = Y, P0
                for lvl in range(5):
                    psP = GP.next()
                    mm8(psP, lambda h, Yk=Yk: Yk[:, h, :], lambda h, Pk=Pk: Pk[:, h, :], [Yk, Pk], None)
                    Pn = pbuf.next()
                    S.op("dve", lambda e, Pn=Pn, psP=psP: e.tensor_copy(out=Pn[:], in_=psP[:]), [psP], [Pn])
                    if lvl < 4:
                        psY = GP.next()
                        mm8(psY, lambda h, Pk=Pk: Pk[:, h, :], lambda h, Yk=Yk: Yk[:, h, :], [Yk, Pk], None)
                        Yn = ybuf.next()
                        S.op("act", lambda e, Yn=Yn, psY=psY: e.copy(out=Yn[:], in_=psY[:]), [psY], [Yn])
                    psR = GP.next()
                    mm8(psR, lambda h, Pn=Pn: Pn[:, h, :], lambda h, R=R: R[:, h, :], [Pn, R], None)
                    Rn = rbuf.next()
                    S.op("dve", lambda e, Rn=Rn, psR=psR, R=R: e.tensor_tensor(out=Rn[:], in0=psR[:], in1=R[:], op=ALU.add), [psR, R], [Rn])
                    R = Rn
                    Yk, Pk = (Yn if lvl < 4 else Yk), Pn
                psX = GP.next()
                mm8(psX, lambda h: LakT[:, h, :], V_tm, [LakT, tm], None)
                Xs = Xr.next()
                S.op("act", lambda e: e.copy(out=Xs[:], in_=psX[:]), [psX], [Xs])
                psW = GP.next()
                mm8(psW, A_tm, lambda h: R[:, h, :], [tm, R], None)
                WTs = Wr.next()
                S.op("dve", lambda e: e.tensor_copy(out=WTs[:], in_=psW[:]), [psW], [WTs])
                psU = GP.next()
                mm8(psU, lambda h: WTs[:, h, :], lambda h: Mb[:, h, :], [WTs, Mb, R, Xs], None, extra=[(lambda h: R[:, h, :], lambda h: Xs[:, h, :])])
                Us = Ur.next()
                S.op("act", lambda e: e.copy(out=Us[:], in_=psU[:]), [psU], [Us])
                psO = GP.next()
                mm8(psO, lambda h: RT[:, h, cs], lambda h: Mb[:, h, :], [RT, Mb, LrbT, Us, LrkT, tm], None,
                    extra=[(lambda h: LrbT[:, h, :], lambda h: Us[:, h, :]), (lambda h: LrkT[:, h, :], V_tm)])
                psM = GP.next()
                mm8(psM, B_tm, lambda h: Us[:, h, :], [tm, Us], None, extra=[(K_tm, V_tm)])
                ob = o_ring.next()
                if d == 0:
                    S.op("act", lambda e: e.copy(out=ob[:], in_=psO[:]), [psO], [ob])
                    cx.store(sc["OF"], sc["OF"][row0:row0 + 64, :], ob, ob[:].rearrange("p h i -> p (h i)"))
                else:
                    of = of_ring.next()
                    cx.load(of, of[:].rearrange("p h i -> p (h i)"), sc["OF"], sc["OF"][row0:row0 + 64, :])
                    S.op("dve", lambda e: e.tensor_tensor(out=ob[:], in0=psO[:], in1=of[:], op=ALU.add), [psO, of], [ob])
                S.op("dve", lambda e: e.tensor_tensor(out=Mf[:], in0=psM[:], in1=Mf[:], op=ALU.add), [psM, Mf], [Mf])
                S.op("pool", lambda e: e.tensor_tensor(out=Mf[:], in0=Mf[:], in1=bc8(GE[:, c, :]), op=ALU.mult), [Mf, GE], [Mf])
                S.op("act", lambda e: e.copy(out=Mb[:], in_=Mf[:]), [Mf], [Mb])
                if d == 1:
                    st = st_ring.next()
                    S.op("dve", lambda e: e.tensor_reduce(out=st[:, 0, :], in_=ob[:], axis=AX.X, op=ALU.add), [ob], [st])
                    S.op("dve", lambda e: e.tensor_scalar(out=st[:, 1, :], in0=st[:, 0, :], scalar1=-1.0 / 64, scalar2=None, op0=ALU.mult), [st], [st])
                    S.op("dve", lambda e: e.tensor_tensor(out=cen[:], in0=ob[:], in1=bc8(st[:, 1, :]), op=ALU.add), [ob, st], [cen])
                    S.op("act", lambda e: e.activation(out=sqn[:], in_=cen[:], func=AF.Square), [cen], [sqn])
                    S.op("dve", lambda e: e.tensor_reduce(out=st[:, 2, :], in_=sqn[:], axis=AX.X, op=ALU.add), [sqn], [st])
                    self.rstd(cx, st, st[:, 2, :], st, st[:, 4, :], st, st[:, 3, :], 1.0 / 64, lneps[:, 0:1])
                    S.op("dve", lambda e: e.tensor_tensor(out=cen[:], in0=cen[:], in1=bc8(st[:, 4, :]), op=ALU.mult), [cen, st], [cen])
                    S.op("pool", lambda e: e.tensor_tensor(out=cen[:], in0=cen[:], in1=lng[:].rearrange("p (h i) -> p h i", i=64), op=ALU.mult), [cen, lng], [cen])
                    S.op("pool", lambda e: e.tensor_tensor(out=cen[:], in0=cen[:], in1=lnb[:].rearrange("p (h i) -> p h i", i=64), op=ALU.add), [cen, lnb], [cen])
                    psB = GP.next()
                    for h in range(8):
                        S.op("pe", lambda e, h=h: e.matmul(psB[:, h, 0:1], lhsT=RKD[:, h, cs], rhs=onesb[:, 0:1], start=True, stop=True), [RKD, onesb], [psB])
                    S.op("act", lambda e: e.copy(out=st[:, 5, :], in_=psB[:, :, 0]), [psB], [st])
                    S.op("dve", lambda e: e.tensor_tensor(out=sqn[:], in0=tm[:, 3], in1=bc8(st[:, 5, :]), op=ALU.mult), [tm, st], [sqn])
                    S.op("dve", lambda e: e.tensor_tensor(out=cen[:], in0=cen[:], in1=sqn[:], op=ALU.add), [cen, sqn], [cen])
                    psG = GP.next()
                    S.op("pe", lambda e: e.matmul(psG[:].rearrange("p h i -> p (h i)"), lhsT=SG[:, cs], rhs=gup[:], start=True, stop=True), [SG, gup], [psG])
                    yd = yd_ring.next()
                    S.op("dve", lambda e: e.tensor_tensor(out=yd[:], in0=cen[:], in1=psG[:], op=ALU.mult), [cen, psG], [yd])
                    cx.store(sc["YD"], sc["YD"][row0:row0 + 64, :], yd, yd[:].rearrange("p h i -> p (h i)"))

            blocks = range(NB) if d == 0 else range(NB - 1, -1, -1)
            for bi in blocks:
                prep(bi)
                n0 = bi * 512
                nch = min(512, Lp - n0) // 64
                for c in (range(nch) if d == 0 else range(nch - 1, -1, -1)):
                    chunk(n0, c)


def make_inputs(prog, inputs, core):
    m = {}
    for k in prog.W:
        m[k] = np.ascontiguousarray(np.asarray(inputs[k], dtype=np.float32))
    for k, v in global_consts().items():
        m["c_" + k] = v
    meta = np.asarray(inputs["meta_tokens"], dtype=np.float32)
    for s in prog.seqs:
        x = s.xsrc(inputs, core)
        h0 = np.zeros((s.Lp, D), np.float32)
        h0[:16] = meta
        h0[16:s.L] = x
        m["h0_" + s.name] = h0
        for k, v in host_consts(s).items():
            m["c_%s_%s" % (k, s.name)] = v
    return m


def kernel(**inputs):
    xp = np.asarray(inputs["x_prompt"], dtype=np.float32)
    xs = np.asarray(inputs["x_sample"], dtype=np.float32)
    sp = Seq("p", xp.shape[1])
    ss = Seq("s", xs.shape[1])
    sp.xsrc = lambda inp, core: xp[core]
    ss.xsrc = lambda inp, core: xs[0]
    prog = Prog([sp, ss], debug=False)
    nc = prog.build()
    in_maps = []
    for c in range(NCORE):
        m = make_inputs(prog, inputs, c)
        in_maps.append({k: m[k] for k in prog.in_names})
    res = run_bass_kernel_spmd(nc, in_maps, core_ids=list(range(NCORE)))
    r = res.results
    y_prompt = np.stack([np.asarray(r[c]["y_p"], dtype=np.float32) for c in range(NCORE)], axis=0)
    n = xs.shape[1] // NCORE
    y_sample = np.concatenate([np.asarray(r[c]["y_s"], dtype=np.float32)[c * n:(c + 1) * n] for c in range(NCORE)], axis=0)[None]
    return (y_prompt, y_sample)
```

```python
import math
import contextlib
import numpy as np
import concourse.bass as bass
import concourse.mybir as mybir
from concourse.bass_utils import run_bass_kernel_spmd

F32 = mybir.dt.float32
BF16 = mybir.dt.bfloat16
I32 = mybir.dt.int32
AF = mybir.ActivationFunctionType
ALU = mybir.AluOpType
AX = mybir.AxisListType

D = 1024
NCORE = 8
EPS = 1e-6
ENG_NAMES = ("pe", "act", "dve", "pool", "sp")
SAME_ENGINE_SYNC = True


class Buf:
    __slots__ = ("name", "w", "r", "dsem", "dcnt", "excl")

    def __init__(self, name):
        self.name = name
        self.excl = False
        self.w = {}
        self.r = {}
        self.dsem = None
        self.dcnt = 0


class TL:
    def __init__(self, t, name):
        self.t = t
        self.b = Buf(name)

    def __getitem__(self, i):
        return self.t[i]


def _b(x):
    return x.b if isinstance(x, TL) else x


class Sched:
    def __init__(self, nc):
        self.nc = nc
        self.ops = {e: [] for e in ENG_NAMES}
        self.sem = {e: nc.alloc_semaphore(name="s_" + e) for e in ENG_NAMES}
        self.cnt = {e: 0 for e in ENG_NAMES}
        self.seen = {e: {} for e in ENG_NAMES}
        self.dpool = []
        self.dlive = []
        self.dall = {}
        self.nsem = 0

    def _need(self, e, toks, is_dma):
        waits = []
        for sid, (sem, val, src) in toks.items():
            if not is_dma and src == e and (e == "pe" or not SAME_ENGINE_SYNC):
                continue
            if self.seen[e].get(sid, 0) >= val:
                continue
            self.seen[e][sid] = val
            waits.append((sem, val))
        return waits

    def _collect(self, e, reads, writes, is_dma):
        waits = []
        for b in reads:
            waits += self._need(e, b.w, is_dma)
        for b in writes:
            waits += self._need(e, b.w, is_dma)
            waits += self._need(e, b.r, is_dma)
        return waits

    def op(self, e, fn, reads=(), writes=()):
        reads = [_b(x) for x in reads]
        writes = [_b(x) for x in writes]
        writes = writes + [b for b in reads if b.excl and b not in writes]
        waits = self._collect(e, reads, writes, False)
        self.cnt[e] += 1
        tok = (self.sem[e], self.cnt[e], e)
        sid = id(self.sem[e])
        for b in reads:
            b.r[sid] = tok
        for b in writes:
            b.w[sid] = tok
        self.ops[e].append((waits, fn, (self.sem[e], 1)))

    def dma(self, e, fn, reads=(), writes=(), key=None):
        reads = [_b(x) for x in reads]
        writes = [_b(x) for x in writes]
        key = _b(key)
        waits = self._collect(e, reads, writes, True)
        if key.dsem is None:
            if self.dpool:
                key.dsem, key.dcnt = self.dpool.pop()
            else:
                key.dsem = self.nc.alloc_semaphore(name="d%d" % self.nsem)
                self.nsem += 1
                key.dcnt = 0
            self.dlive.append(key)
        key.dcnt += 16
        tok = (key.dsem, key.dcnt, "dma")
        sid = id(key.dsem)
        self.dall[sid] = [key.dsem, key.dcnt]
        for b in reads:
            b.r[sid] = tok
        for b in writes:
            b.w[sid] = tok
        self.ops[e].append((waits, fn, (key.dsem, 16)))

    def barrier(self, engines=ENG_NAMES):
        toks = [(self.sem[e], self.cnt[e]) for e in ENG_NAMES if self.cnt[e] > 0]
        toks += [(s, c) for s, c in self.dall.values()]
        for e in engines:
            waits = []
            for sem, val in toks:
                if self.seen[e].get(id(sem), 0) >= val:
                    continue
                self.seen[e][id(sem)] = val
                waits.append((sem, val))
            if waits:
                self.ops[e].append((waits, None, None))
        for b in self.dlive:
            self.dpool.append((b.dsem, b.dcnt))
            b.dsem = None
        self.dlive = []

    def emit(self):
        nc = self.nc
        with nc.Block() as block:
            for e in ENG_NAMES:
                ops = self.ops[e]
                if not ops:
                    continue

                def body(eng, ops=ops):
                    for waits, fn, inc in ops:
                        for sem, val in waits:
                            eng.wait_ge(sem, val)
                        if fn is not None:
                            fn(eng).then_inc(inc[0], inc[1])

                dec = {"pe": block.tensor, "act": block.scalar, "dve": block.vector,
                       "pool": block.gpsimd, "sp": block.sync}[e]
                dec(body)


class Ring:
    def __init__(self, tiles):
        self.tiles = tiles
        self.i = 0

    def next(self):
        t = self.tiles[self.i % len(self.tiles)]
        self.i += 1
        return t


class Ctx:
    uid = 0

    def __init__(self, nc, S):
        self.nc = nc
        self.S = S
        self.es = contextlib.ExitStack()
        self.dq_i = 0

    def __enter__(self):
        self.es.__enter__()
        return self

    def __exit__(self, *a):
        self.S.barrier()
        return self.es.__exit__(*a)

    def sb(self, name, shape, dt):
        Ctx.uid += 1
        n = "%s_%d" % (name, Ctx.uid)
        return TL(self.es.enter_context(self.nc.sbuf_tensor(n, list(shape), dt)), n)

    def ps(self, name, shape, dt):
        Ctx.uid += 1
        n = "%s_%d" % (name, Ctx.uid)
        esz = 4 if dt in (F32, I32) else 2
        tot = 1
        for d in shape[1:]:
            tot *= d
        assert tot * esz <= 2048, (name, shape)
        full = self.es.enter_context(self.nc.psum_tensor(n, [128, 2048 // esz], dt))
        view = full[0:shape[0], 0:tot]
        if len(shape) == 3:
            view = view.rearrange("p (a b) -> p a b", b=shape[2])
        elif len(shape) == 4:
            view = view.rearrange("p (a b c) -> p a b c", b=shape[2], c=shape[3])
        t = TL(view, n)
        t.b.excl = True
        return t

    def ring(self, name, n, shape, dt, ps=False):
        f = self.ps if ps else self.sb
        return Ring([f(name + str(i), shape, dt) for i in range(n)])

    def dq(self):
        self.dq_i += 1
        return "sp" if self.dq_i % 2 else "pool"

    def load(self, dst_tl, dst_ap, src_tl, src_ap, q=None):
        self.S.dma(q or self.dq(), lambda e: e.dma_start(out=dst_ap, in_=src_ap),
                   reads=[src_tl], writes=[dst_tl], key=dst_tl)

    def store(self, dst_tl, dst_ap, src_tl, src_ap, q=None, slow=False):
        self.S.dma(q or self.dq(), lambda e: e.dma_start(out=dst_ap, in_=src_ap, allow_slow_non_contiguous=slow),
                   reads=[src_tl], writes=[dst_tl], key=src_tl)


class Seq:
    def __init__(self, name, S_len):
        self.name = name
        self.S = S_len
        self.L = S_len + 16
        self.NT = (self.L + 127) // 128
        self.Lp = self.NT * 128
        self.last = self.L - (self.NT - 1) * 128
        self.NB = (self.Lp + 511) // 512

    def rows(self, t):
        return self.last if t == self.NT - 1 else 128


def host_consts(seq):
    L, Lp, NT, NB = seq.L, seq.Lp, seq.NT, seq.NB
    S_len = seq.S
    c = {}
    rows = S_len // 64
    row = np.repeat(np.arange(rows), 64)
    col = np.arange(rows * 64) % 64
    meta = np.arange(16) - 16
    row = np.concatenate([meta, row]).astype(np.float32)
    col = np.concatenate([meta, col]).astype(np.float32)
    inv = (np.float32(10000.0) ** (-np.arange(0, 32, 2, dtype=np.float32) / np.float32(32))).astype(np.float32)
    ang = np.zeros((Lp, 2, 16), np.float32)
    ang[:L, 0] = row[:, None] * inv
    ang[:L, 1] = col[:, None] * inv
    ropeA = np.zeros((Lp, 2, 2, 16), np.float32)
    ropeA[:, 0] = np.cos(ang)
    ropeA[:, 1] = np.sin(ang)
    c["ropeA"] = ropeA
    inv1 = (np.float32(500000.0) ** (-np.arange(0, 16, 2, dtype=np.float32) / np.float32(16))).astype(np.float32)
    ang1 = np.zeros((Lp, 8), np.float32)
    ang1[:L] = np.arange(L, dtype=np.float32)[:, None] * inv1
    ropeC = np.zeros((Lp, 2, 8), np.float32)
    ropeC[:, 0] = np.cos(ang1)
    ropeC[:, 1] = np.sin(ang1)
    c["ropeC"] = ropeC
    p = np.arange(128)
    lv = (np.arange(NT)[None, :] * 128 + p[:, None]).astype(np.int64)
    c["lvals"] = lv.astype(np.float32)
    l0 = (np.arange(NB) * 512).astype(np.int64)
    sp = (lv[:, :, None] * l0[None, None, :]) % L
    c["dftsp"] = sp.reshape(128, NT * NB).astype(np.float32)
    return c


def global_consts():
    c = {}
    c["ident"] = np.eye(128, dtype=np.float32)
    c["iota512"] = np.tile(np.arange(512, dtype=np.float32)[None, :], (128, 1))
    k = np.arange(64)
    ang = 2 * np.pi * ((k[:, None] * k[None, :]) % 64) / 64.0
    c64 = (np.cos(ang) / 8.0).astype(np.float32)
    s64 = (np.sin(ang) / 8.0).astype(np.float32)
    c["cs64"] = np.concatenate([c64, c64, s64, s64], axis=1)
    i = np.arange(128)
    mk = np.zeros((3, 128, 128), np.float32)
    mk[0] = (i[:, None] >= i[None, :])
    mk[1] = (i[:, None] <= i[None, :])
    mk[2] = mk[0]; mk[2][:16, :] = 1.0
    c["cmask"] = np.ascontiguousarray(mk.transpose(1, 0, 2))
    j = np.arange(64)
    m4 = np.zeros((4, 64, 64), np.float32)
    m4[0] = (j[None, :] > j[:, None])
    m4[1] = (j[None, :] >= j[:, None])
    m4[2] = (j[None, :] < j[:, None])
    m4[3] = (j[None, :] <= j[:, None])
    c["m4"] = np.ascontiguousarray(m4.transpose(1, 0, 2))
    rs = np.ones((64, 512), np.float32); rs[:, 0::64] = 0.0
    c["reset"] = rs
    return c


class Prog:
    def __init__(self, seqs, debug=False):
        self.seqs = seqs
        self.debug = debug
        self.stop_after = 1
        nc = self.nc = bass.Bass("TRN2", target_bir_lowering=False)
        self.S = Sched(nc)
        self.dr = {}
        self.in_names = []
        self.out_names = []

    def din(self, name, shape, dt=F32):
        t = TL(self.nc.dram_tensor(name, list(shape), dt, kind="ExternalInput").ap(), name)
        self.dr[name] = t
        self.in_names.append(name)
        return t

    def dout(self, name, shape, dt=F32):
        t = TL(self.nc.dram_tensor(name, list(shape), dt, kind="ExternalOutput").ap(), name)
        self.dr[name] = t
        self.out_names.append(name)
        return t

    def dscr(self, name, shape, dt):
        if self.debug:
            return self.dout(name, shape, dt)
        t = TL(self.nc.dram_tensor(name, list(shape), dt).ap(), name)
        self.dr[name] = t
        return t

    def dbg(self, cx, name, tl, ap, shape, dt):
        if not self.debug or name in self.dr:
            return
        o = self.dout("dbg_" + name, shape, dt)
        cx.store(o, o[tuple(slice(None) for _ in shape)], tl, ap)

    def build(self):
        nc, S = self.nc, self.S
        W = self.W = {}
        wspec = dict(
            pre_mix_g=(2, D), post_mix_g=(2, D), pre_ffn_g=(2, D), post_ffn_g=(2, D),
            even_w_in=(1, D, 1536), even_w_out=(1, D, D), a_q_gain=(1, 64), a_k_gain=(1, 64),
            b_norm_g=(1, 4, 64), b_w=(1, 4, 64, 64), b_b=(1, 4, 64),
            odd_w_in=(1, D, 2688), odd_w_out=(1, D, D), c_sink=(1, 8),
            d_mu_prev=(1, 1920), d_mu_next=(1, 1920), d_w0=(1, 2, 512), d_w_up=(1, 2, 64, 512),
            d_a0=(1, 2, 512), d_a_up=(1, 2, 64, 512), d_g_up=(1, 128, 512), d_k_k=(1, 512),
            d_k_a=(1, 512), d_r_k=(1, 8, 64), d_ln_g=(1, 512), d_ln_b=(1, 512),
            ffn_w_gate=(2, D, 2816), ffn_w_up=(2, D, 2816), ffn_w_down=(2, 2816, D))
        for k, shp in wspec.items():
            W[k] = self.din(k, shp)
        G = self.G = {}
        for k, v in global_consts().items():
            G[k] = self.din("c_" + k, v.shape)
        maxLp = max(s.Lp for s in self.seqs)
        sc = self.sc = {}
        sc["QT"] = self.dscr("QT", [768, maxLp], BF16)
        sc["KT"] = self.dscr("KT", [256, maxLp], BF16)
        sc["VX"] = self.dscr("VX", [maxLp, 4, 65], BF16)
        sc["GG"] = self.dscr("GG", [maxLp, 512], BF16)
        sc["YA"] = self.dscr("YA", [maxLp, 768], BF16)
        sc["YB"] = self.dscr("YB", [maxLp, 256], BF16)
        sc["H1"] = self.dscr("H1", [maxLp, D], F32)
        sc["QCT"] = self.dscr("QCT", [8, 65, maxLp], BF16)
        sc["KCT"] = self.dscr("KCT", [2, 65, maxLp], BF16)
        sc["VCX"] = self.dscr("VCX", [maxLp, 2, 65], BF16)
        sc["BQ"] = self.dscr("BQ", [maxLp, 8], F32)
        sc["UT"] = self.dscr("UT", [1920, maxLp + 2], F32)
        sc["YC"] = self.dscr("YC", [maxLp, 512], BF16)
        sc["OF"] = self.dscr("OF", [maxLp, 512], F32)
        sc["YD"] = self.dscr("YD", [maxLp, 512], BF16)
        for s in self.seqs:
            s.h0 = self.din("h0_" + s.name, [s.Lp, D])
            s.y = self.dout("y_" + s.name, [s.S, D])
            s.c = {k: self.din("c_%s_%s" % (k, s.name), v.shape) for k, v in host_consts(s).items()}

        self.phase_consts()
        sk = getattr(self, "skip", ())
        for s in self.seqs:
            if "even" not in sk: self.phase_even_proj(s)
            if "attn_a" not in sk: self.phase_attn_a(s)
            if "dft" not in sk: self.phase_dft(s)
            if "mix0" not in sk: self.phase_mix_ffn(s, 0, s.h0, sc["H1"], ("YA", 768), ("YB", 256), W["even_w_out"], final=False)
            if self.stop_after == 0:
                continue
            if "odd" not in sk: self.phase_odd_proj(s)
            if "attn_c" not in sk: self.phase_attn_c(s)
            if "rwkv0" not in sk: self.phase_rwkv(s, 0)
            if "rwkv1" not in sk: self.phase_rwkv(s, 1)
            if "mix1" not in sk: self.phase_mix_ffn(s, 1, sc["H1"], s.y, ("YC", 512), ("YD", 512), W["odd_w_out"], final=True)
        with nc.allow_non_contiguous_dma(reason="small strided parameter / layout loads"):
            S.emit()
        return nc

    def phase_consts(self):
        nc, S = self.nc, self.S
        self.pes = contextlib.ExitStack()
        cx = self.pcx = Ctx(nc, S)
        cx.es = self.pes
        K = self.K = {}
        G = self.G
        K["identf"] = cx.sb("identf", [128, 128], F32)
        K["identb"] = cx.sb("identb", [128, 128], BF16)
        cx.load(K["identf"], K["identf"][:], G["ident"], G["ident"][:, :])
        S.op("dve", lambda e: e.tensor_copy(out=K["identb"][:], in_=K["identf"][:]), [K["identf"]], [K["identb"]])
        K["eps"] = cx.sb("eps", [128, 1], F32)
        S.op("dve", lambda e: e.memset(K["eps"][:], EPS), [], [K["eps"]])
        K["halfpi"] = cx.sb("halfpi", [128, 1], F32)
        S.op("dve", lambda e: e.memset(K["halfpi"][:], math.pi / 2), [], [K["halfpi"]])
        S.barrier()

    def load_w(self, cx, dst, src_tl, src_ap, nk, ncols, stage_ring, row0=0):
        S = self.S
        for k in range(nk):
            for c0 in range(0, ncols, 512):
                w = min(512, ncols - c0)
                st = stage_ring.next()
                cx.load(st, st[:, 0:w], src_tl, src_ap[row0 + k * 128: row0 + (k + 1) * 128, c0:c0 + w])
                S.op("pool", lambda e, st=st, k=k, c0=c0, w=w: e.tensor_copy(out=dst[:, k, c0:c0 + w], in_=st[:, 0:w]), [st], [dst])

    def rstd(self, cx, ss_tl, ss_ap, out_tl, out_ap, tmp_tl, tmp_ap, scale, eps_ap):
        S = self.S
        S.op("act", lambda e: e.activation(out=tmp_ap, in_=ss_ap, func=AF.Sqrt, bias=eps_ap, scale=scale),
             [ss_tl], [tmp_tl])
        S.op("dve", lambda e: e.reciprocal(out=out_ap, in_=tmp_ap), [tmp_tl], [out_tl])

    def norm_transpose(self, cx, x_tl, gb_tl, hnT, TP, xs_ring, sm_ring, junk):
        S, K = self.S, self.K
        sm = sm_ring.next()
        S.op("act", lambda e: e.activation(out=junk[:], in_=x_tl[:], func=AF.Square, accum_out=sm[:, 0:1]),
             [x_tl], [junk, sm])
        self.rstd(cx, sm, sm[:, 0:1], sm, sm[:, 2:3], sm, sm[:, 1:2], 1.0 / D, K["eps"][:, 0:1])
        xs = xs_ring.next()
        S.op("dve", lambda e: e.scalar_tensor_tensor(out=xs[:], in0=x_tl[:], scalar=sm[:, 2:3], in1=gb_tl[:],
                                                     op0=ALU.mult, op1=ALU.mult), [x_tl, sm, gb_tl], [xs])
        for k in range(8):
            S.op("pe", lambda e, k=k: e.transpose(out=TP[:, k, :], in_=xs[:, k * 128:(k + 1) * 128],
                                                  identity=K["identb"][:]), [xs, K["identb"]], [TP])
        S.op("act", lambda e: e.copy(out=hnT[:], in_=TP[:]), [TP], [hnT])

    def phase_even_proj(self, seq):
        nc, S, W, K, sc = self.nc, self.S, self.W, self.K, self.sc
        with Ctx(nc, S) as cx:
            stage = cx.ring("wst", 3, [128, 512], F32)
            Win = cx.sb("win", [128, 8, 1536], BF16)
            self.load_w(cx, Win, W["even_w_in"], W["even_w_in"][0], 8, 1536, stage)
            gpre = cx.sb("gpre", [128, D], F32)
            cx.load(gpre, gpre[:], W["pre_mix_g"], W["pre_mix_g"][0:1, :].partition_broadcast(128))
            g16 = cx.sb("g16", [128, 16, 64], F32)
            cx.load(g16, g16[:, 0:12, :], W["a_q_gain"], W["a_q_gain"][0:1, :].partition_broadcast(128).broadcast_to([128, 12, 64]))
            cx.load(g16, g16[:, 12:16, :], W["a_k_gain"], W["a_k_gain"][0:1, :].partition_broadcast(128).broadcast_to([128, 4, 64]))
            S.op("dve", lambda e: e.tensor_scalar(out=g16[:, 0:12, :], in0=g16[:, 0:12, :], scalar1=0.125, scalar2=None,
                                                  op0=ALU.mult), [g16], [g16])
            bg = cx.sb("bg", [128, 256], F32)
            cx.load(bg, bg[:], W["b_norm_g"], W["b_norm_g"][0].rearrange("g c -> (g c)").partition_broadcast(128))
            cs64 = cx.sb("cs64", [64, 256], F32)
            cx.load(cs64, cs64[:], self.G["cs64"], self.G["cs64"][:, :])
            bw = cx.sb("bw", [64, 4, 64], F32)
            cx.load(bw, bw[:], W["b_w"], W["b_w"][0].rearrange("g c d -> c g d"))
            Wcs = cx.sb("wcs", [128, 2, 512], BF16)
            S.op("pool", lambda e: e.memset(Wcs[:], 0.0), [], [Wcs])
            pw = cx.ps("pw", [128, 512], F32)
            for g in range(4):
                for j in range(2):
                    S.op("pe", lambda e, g=g, j=j: e.matmul(pw[:, (g * 2 + j) * 64:(g * 2 + j + 1) * 64],
                                                             lhsT=cs64[:, j * 128:(j + 1) * 128], rhs=bw[:, g, :],
                                                             start=True, stop=True), [cs64, bw], [pw])
            for g in range(4):
                h0 = (g % 2) * 64
                for j in range(2):
                    S.op("dve", lambda e, g=g, j=j, h0=h0: e.tensor_copy(
                        out=Wcs[h0:h0 + 64, g // 2, j * 256 + g * 64: j * 256 + (g + 1) * 64],
                        in_=pw[h0:h0 + 64, (g * 2 + j) * 64:(g * 2 + j + 1) * 64]), [pw], [Wcs])

            self.dbg(cx, "win0", Win, Win[:, 0, :], [128, 1536], BF16)
            self.dbg(cx, "gpre", gpre, gpre[:], [128, D], F32)
            self.dbg(cx, "g16", g16, g16[:], [128, 16, 64], F32)
            self.dbg(cx, "wcs", Wcs, Wcs[:], [128, 2, 512], BF16)
            xring = cx.ring("x", 2, [128, D], F32)
            xs_ring = cx.ring("xs", 2, [128, D], BF16)
            sm_ring = cx.ring("sm", 2, [128, 4], F32)
            junk = cx.sb("junk", [128, D], BF16)
            TP = cx.ps("TP", [128, 8, 128], BF16)
            hnT_ring = cx.ring("hnT", 2, [128, 8, 128], BF16)
            PJ = [cx.ps("PJ%d" % c, [128, 512], F32) for c in range(3)]
            sqt = cx.sb("sqt", [128, 16, 64], F32)
            ssq_ring = cx.ring("ssq", 2, [128, 3, 16], F32)
            qk = cx.sb("qk", [128, 16, 64], F32)
            tA = cx.sb("tA", [128, 16, 2, 16], F32)
            tB = cx.sb("tB", [128, 16, 2, 16], F32)
            qkr_ring = cx.ring("qkr", 2, [128, 16, 64], BF16)
            rope_ring = cx.ring("rope", 2, [128, 2, 2, 16], F32)
            TP2 = cx.ps("TP2", [128, 8, 128], BF16)
            qkT_ring = cx.ring("qkT", 2, [128, 8, 128], BF16)
            vx_ring = cx.ring("vx", 2, [128, 4, 65], BF16)
            for vx in vx_ring.tiles:
                S.op("pool", lambda e, vx=vx: e.memset(vx[:], 1.0), [], [vx])
            sqf = cx.sb("sqf", [128, 4, 64], F32)
            fn32 = cx.sb("fn32", [128, 4, 64], F32)
            fn_ring = cx.ring("fn", 2, [128, 256], BF16)
            TP3 = cx.ps("TP3", [128, 2, 128], BF16)
            fnT_ring = cx.ring("fnT", 2, [128, 2, 128], BF16)
            PG = cx.ps("PG", [128, 512], F32)
            g_ring = cx.ring("gsb", 2, [128, 512], BF16)
            QTv = sc["QT"].t.rearrange("(c p) n -> p c n", p=128)
            KTv = sc["KT"].t.rearrange("(c p) n -> p c n", p=128)
            vmask = cx.sb("vmask", [128, 1], F32)
            S.op("dve", lambda e: e.memset(vmask[:], 0.0), [], [vmask])
            S.op("dve", lambda e: e.memset(vmask[0:seq.last, :], 1.0), [], [vmask])

            def body(t):
                c0 = t * 128
                xt = xring.next()
                cx.load(xt, xt[:], seq.h0, seq.h0[c0:c0 + 128, :])
                rp = rope_ring.next()
                cx.load(rp, rp[:], seq.c["ropeA"], seq.c["ropeA"][c0:c0 + 128])
                hnT = hnT_ring.next()
                self.norm_transpose(cx, xt, gpre, hnT, TP, xs_ring, sm_ring, junk)
                for c in range(3):
                    for k in range(8):
                        S.op("pe", lambda e, c=c, k=k: e.matmul(PJ[c][:], lhsT=hnT[:, k, :], rhs=Win[:, k, c * 512:(c + 1) * 512],
                                                                 start=(k == 0), stop=(k == 7)), [hnT, Win], [PJ[c]])
                if t == 0 and self.debug:
                    self.dbg(cx, "hnT", hnT, hnT[:], [128, 8, 128], BF16)
                    pjd = cx.sb("pjd", [128, 1536], F32)
                    for c in range(3):
                        S.op("dve", lambda e, c=c: e.tensor_copy(out=pjd[:, c * 512:(c + 1) * 512], in_=PJ[c][:]), [PJ[c]], [pjd])
                    self.dbg(cx, "pj", pjd, pjd[:], [128, 1536], F32)
                for c in range(2):
                    S.op("act", lambda e, c=c: e.activation(out=sqt[:, c * 8:(c + 1) * 8, :],
                                                            in_=PJ[c][:].rearrange("p (h d) -> p h d", d=64), func=AF.Square),
                         [PJ[c]], [sqt])
                sq = ssq_ring.next()
                S.op("dve", lambda e: e.tensor_reduce(out=sq[:, 0, :], in_=sqt[:], axis=AX.X, op=ALU.add), [sqt], [sq])
                self.rstd(cx, sq, sq[:, 0, :], sq, sq[:, 2, :], sq, sq[:, 1, :], 1.0 / 64, K["eps"][:, 0:1])
                for c in range(2):
                    S.op("dve", lambda e, c=c: e.tensor_tensor(
                        out=qk[:, c * 8:(c + 1) * 8, :], in0=PJ[c][:].rearrange("p (h d) -> p h d", d=64),
                        in1=sq[:, 2, c * 8:(c + 1) * 8].unsqueeze(2).broadcast_to([128, 8, 64]), op=ALU.mult),
                        [PJ[c], sq], [qk])
                S.op("dve", lambda e: e.tensor_tensor(out=qk[:], in0=qk[:], in1=g16[:], op=ALU.mult), [qk, g16], [qk])
                qkr = qkr_ring.next()
                q5 = qk[:].rearrange("p h (a c d) -> p h a c d", a=2, c=2)
                o5 = qkr[:].rearrange("p h (a c d) -> p h a c d", a=2, c=2)
                x1, x2 = q5[:, :, :, 0, :], q5[:, :, :, 1, :]
                cosb = rp[:, 0].unsqueeze(1).broadcast_to([128, 16, 2, 16])
                sinb = rp[:, 1].unsqueeze(1).broadcast_to([128, 16, 2, 16])
                S.op("dve", lambda e: e.tensor_tensor(out=tA[:], in0=x1, in1=cosb, op=ALU.mult), [qk, rp], [tA])
                S.op("dve", lambda e: e.tensor_tensor(out=tB[:], in0=x2, in1=sinb, op=ALU.mult), [qk, rp], [tB])
                S.op("dve", lambda e: e.tensor_tensor(out=o5[:, :, :, 0, :], in0=tA[:], in1=tB[:], op=ALU.subtract), [tA, tB], [qkr])
                S.op("dve", lambda e: e.tensor_tensor(out=tA[:], in0=x1, in1=sinb, op=ALU.mult), [qk, rp], [tA])
                S.op("dve", lambda e: e.tensor_tensor(out=tB[:], in0=x2, in1=cosb, op=ALU.mult), [qk, rp], [tB])
                S.op("dve", lambda e: e.tensor_tensor(out=o5[:, :, :, 1, :], in0=tA[:], in1=tB[:], op=ALU.add), [tA, tB], [qkr])
                qf = qkr[:].rearrange("p h d -> p (h d)")
                for c in range(8):
                    S.op("pe", lambda e, c=c: e.transpose(out=TP2[:, c, :], in_=qf[:, c * 128:(c + 1) * 128],
                                                          identity=K["identb"][:]), [qkr, K["identb"]], [TP2])
                qkT = qkT_ring.next()
                S.op("act", lambda e: e.copy(out=qkT[:], in_=TP2[:]), [TP2], [qkT])
                cx.store(sc["QT"], QTv[:, :, c0:c0 + 128], qkT, qkT[:, 0:6, :])
                cx.store(sc["KT"], KTv[:, :, c0:c0 + 128], qkT, qkT[:, 6:8, :])
                vx = vx_ring.next()
                S.op("act", lambda e: e.copy(out=vx[:, :, 0:64], in_=PJ[2][:, 0:256].rearrange("p (h d) -> p h d", d=64)),
                     [PJ[2]], [vx])
                if t == seq.NT - 1:
                    S.op("dve", lambda e: e.tensor_scalar(out=vx[:, :, 64], in0=vx[:, :, 64], scalar1=vmask[:, 0:1], scalar2=None, op0=ALU.mult),
                         [vx, vmask], [vx])
                cx.store(sc["VX"], sc["VX"][c0:c0 + 128, :, :], vx, vx[:])
                if t == seq.NT - 1:
                    S.op("dve", lambda e: e.memset(vx[:, :, 64], 1.0), [], [vx])
                fv = PJ[2][:, 256:512].rearrange("p (h d) -> p h d", d=64)
                S.op("act", lambda e: e.activation(out=sqf[:], in_=fv, func=AF.Square), [PJ[2]], [sqf])
                S.op("dve", lambda e: e.tensor_reduce(out=sq[:, 0, 0:4], in_=sqf[:], axis=AX.X, op=ALU.add), [sqf], [sq])
                self.rstd(cx, sq, sq[:, 0, 0:4], sq, sq[:, 2, 0:4], sq, sq[:, 1, 0:4], 1.0 / 64, K["eps"][:, 0:1])
                S.op("dve", lambda e: e.tensor_tensor(out=fn32[:], in0=fv, in1=sq[:, 2, 0:4].unsqueeze(2).broadcast_to([128, 4, 64]),
                                                      op=ALU.mult), [PJ[2], sq], [fn32])
                fn = fn_ring.next()
                S.op("dve", lambda e: e.tensor_tensor(out=fn[:], in0=fn32[:].rearrange("p h d -> p (h d)"), in1=bg[:], op=ALU.mult),
                     [fn32, bg], [fn])
                for c in range(2):
                    S.op("pe", lambda e, c=c: e.transpose(out=TP3[:, c, :], in_=fn[:, c * 128:(c + 1) * 128],
                                                          identity=K["identb"][:]), [fn, K["identb"]], [TP3])
                fnT = fnT_ring.next()
                S.op("act", lambda e: e.copy(out=fnT[:], in_=TP3[:]), [TP3], [fnT])
                for c in range(2):
                    S.op("pe", lambda e, c=c: e.matmul(PG[:], lhsT=fnT[:, c, :], rhs=Wcs[:, c, :], start=(c == 0), stop=(c == 1)),
                         [fnT, Wcs], [PG])
                gs = g_ring.next()
                S.op("act", lambda e: e.copy(out=gs[:], in_=PG[:]), [PG], [gs])
                cx.store(sc["GG"], sc["GG"][c0:c0 + 128, :], gs, gs[:])

            for t in range(seq.NT):
                body(t)

    def phase_attn_a(self, seq):
        nc, S, W, K, sc = self.nc, self.S, self.W, self.K, self.sc
        NT, Lp = seq.NT, seq.Lp
        with Ctx(nc, S) as cx:
            gq = cx.sb("gq", [128, 2, 64], F32)
            cx.load(gq, gq[:, 0, :], W["a_q_gain"], W["a_q_gain"][0:1, :].partition_broadcast(128))
            cx.load(gq, gq[:, 1, :], W["a_k_gain"], W["a_k_gain"][0:1, :].partition_broadcast(128))
            gm = cx.sb("gm", [128, 4], F32)
            S.op("dve", lambda e: e.tensor_reduce(out=gm[:, 0:2], in_=gq[:], axis=AX.X, op=ALU.max, apply_absolute_value=True),
                 [gq], [gm])
            S.op("dve", lambda e: e.tensor_tensor(out=gm[:, 2:3], in0=gm[:, 0:1], in1=gm[:, 1:2], op=ALU.mult), [gm], [gm])
            S.op("dve", lambda e: e.tensor_scalar(out=gm[:, 3:4], in0=gm[:, 2:3], scalar1=-8.0, scalar2=None, op0=ALU.mult), [gm], [gm])
            KTs = cx.sb("KTs", [128, Lp], BF16)
            VXs = cx.sb("VXs", [128, NT, 65], BF16)
            q_ring = cx.ring("qb", 2, [128, 512], BF16)
            ST = cx.ring("ST", 3, [128, 512], F32, ps=True)
            OTa = cx.ring("OTa", 2, [65, 512], F32, ps=True)
            OTb = cx.ring("OTb", 2, [65, 512], F32, ps=True)
            PT = cx.ring("PT", 4, [128, 512], BF16)
            osb_ring = cx.ring("osb", 2, [65, 512], F32)
            TPo = cx.ps("TPo", [128, 4, 65], F32)
            rec_ring = cx.ring("rec", 2, [128, 4, 1], F32)
            y_ring = cx.ring("ya", 2, [128, 4, 64], BF16)
            YAv = sc["YA"].t.rearrange("(s p) c -> p s c", p=128)

            def qblock(g, h, qb):
                n0 = qb * 512
                wd = min(512, Lp - n0)
                nsub = wd // 128
                qt = q_ring.next()
                for hf in range(2):
                    cx.load(qt, qt[hf * 64:(hf + 1) * 64, 0:wd], sc["QT"], sc["QT"][h * 64:(h + 1) * 64, n0:n0 + wd])
                ota = OTa.next()
                otb = OTb.next()
                sts = {}

                def emitS(kt):
                    st = ST.next()
                    sts[kt] = st
                    hf = slice((kt % 2) * 64, (kt % 2) * 64 + 64)
                    S.op("pe", lambda e, st=st, kt=kt, hf=hf: e.matmul(st[:, 0:wd], lhsT=KTs[hf, kt * 128:(kt + 1) * 128],
                                                                        rhs=qt[hf, 0:wd], start=True, stop=True), [KTs, qt], [st])
                LOOK = 2
                for kt in range(min(LOOK, NT)):
                    emitS(kt)
                for kt in range(NT):
                    if kt + LOOK < NT:
                        emitS(kt + LOOK)
                    st = sts.pop(kt)
                    pt = PT.next()
                    S.op("act", lambda e, st=st, pt=pt: e.activation(out=pt[:, 0:wd], in_=st[:, 0:wd], func=AF.Exp,
                                                                      bias=gm[:, 3:4], scale=1.0), [st, gm], [pt])
                    S.op("pe", lambda e, pt=pt, kt=kt: e.matmul(ota[:, 0:wd], lhsT=VXs[0:64, kt, :], rhs=pt[0:64, 0:wd],
                                                                 start=(kt == 0), stop=(kt == NT - 1)), [VXs, pt], [ota])
                    S.op("pe", lambda e, pt=pt, kt=kt: e.matmul(otb[:, 0:wd], lhsT=VXs[64:128, kt, :], rhs=pt[64:128, 0:wd],
                                                                 start=(kt == 0), stop=(kt == NT - 1)), [VXs, pt], [otb])
                osb = osb_ring.next()
                S.op("dve", lambda e: e.tensor_copy(out=osb[:, 0:wd], in_=ota[:, 0:wd]), [ota], [osb])
                S.op("dve", lambda e: e.tensor_tensor(out=osb[:, 0:wd], in0=otb[:, 0:wd], in1=osb[:, 0:wd], op=ALU.add), [otb, osb], [osb])
                for s_ in range(nsub):
                    S.op("pe", lambda e, s_=s_: e.transpose(out=TPo[:, s_, :], in_=osb[:, s_ * 128:(s_ + 1) * 128],
                                                            identity=K["identf"][0:65, 0:65]), [osb, K["identf"]], [TPo])
                rec = rec_ring.next()
                S.op("dve", lambda e: e.reciprocal(out=rec[:, 0:nsub, :], in_=TPo[:, 0:nsub, 64:65]), [TPo], [rec])
                ya = y_ring.next()
                S.op("dve", lambda e: e.tensor_tensor(out=ya[:, 0:nsub, :], in0=TPo[:, 0:nsub, 0:64],
                                                      in1=rec[:, 0:nsub, :].broadcast_to([128, nsub, 64]), op=ALU.mult), [TPo, rec], [ya])
                cx.store(sc["YA"], YAv[:, qb * 4:qb * 4 + nsub, h * 64:(h + 1) * 64], ya, ya[:, 0:nsub, :])

            for g in range(4):
                for hf in range(2):
                    cx.load(KTs, KTs[hf * 64:(hf + 1) * 64, :], sc["KT"], sc["KT"][g * 64:(g + 1) * 64, 0:Lp])
                for t0 in range(0, NT, 16):
                    t1_ = min(NT, t0 + 16)
                    cx.load(VXs, VXs[:, t0:t1_, :], sc["VX"], sc["VX"][t0 * 128:t1_ * 128, g, :].rearrange("(t p) d -> p t d", p=128))
                for j in range(3):
                    for qb in range(seq.NB):
                        qblock(g, g * 3 + j, qb)

    def phase_dft(self, seq):
        nc, S, W, K, sc = self.nc, self.S, self.W, self.K, self.sc
        NT, Lp, L, NB = seq.NT, seq.Lp, seq.L, seq.NB
        with Ctx(nc, S) as cx:
            GGs = cx.sb("GGs", [128, NT, 512], BF16)
            for t0 in range(0, NT, 16):
                t1 = min(NT, t0 + 16)
                cx.load(GGs, GGs[:, t0:t1, :], sc["GG"], sc["GG"][t0 * 128:t1 * 128, :].rearrange("(t p) c -> p t c", p=128))
            iota = cx.sb("iota", [128, 512], F32)
            cx.load(iota, iota[:], self.G["iota512"], self.G["iota512"][:, :])
            lv = cx.sb("lv", [128, NT], F32)
            cx.load(lv, lv[:], seq.c["lvals"], seq.c["lvals"][:, :])
            sp = cx.sb("sp", [128, NT * NB], F32)
            cx.load(sp, sp[:], seq.c["dftsp"], seq.c["dftsp"][:, :])
            bb = cx.sb("bb", [128, 256], F32)
            cx.load(bb, bb[:], W["b_b"], W["b_b"][0].rearrange("g c -> (g c)").partition_broadcast(128))
            t1r = cx.ring("t1", 3, [128, 512], F32)
            qr = cx.ring("q", 3, [128, 512], I32)
            rr = cx.ring("r", 3, [128, 512], F32)
            rar = cx.ring("ra", 3, [128, 512], F32)
            Cr = cx.ring("C", 3, [128, 512], BF16)
            Sr = cx.ring("Sn", 3, [128, 512], BF16)
            ACC = [cx.ps("acc%d" % i, [128, 512], F32) for i in range(4)]
            yb_ring = cx.ring("yb", 2, [128, 256], BF16)
            w0 = 2 * math.pi / L * (1.0 - 2e-6)
            isq = 1.0 / math.sqrt(L)
            def blk(B):
                n0 = B * 512
                wd = min(512, Lp - n0)
                nsub = wd // 128
                for lt in range(NT):
                    r = seq.rows(lt)
                    t1 = t1r.next(); q = qr.next(); rt = rr.next(); ra = rar.next(); C = Cr.next(); Sn = Sr.next()
                    idx = lt * NB + B
                    S.op("dve", lambda e, t1=t1, lt=lt, idx=idx: e.tensor_scalar(out=t1[:, 0:wd], in0=iota[:, 0:wd], scalar1=lv[:, lt:lt + 1],
                                                                                   scalar2=sp[:, idx:idx + 1], op0=ALU.mult, op1=ALU.add),
                         [iota, lv, sp], [t1])
                    S.op("dve", lambda e, t1=t1, q=q: e.tensor_scalar(out=q[:, 0:wd], in0=t1[:, 0:wd], scalar1=1.0 / L, scalar2=None,
                                                                        op0=ALU.mult), [t1], [q])
                    S.op("dve", lambda e, t1=t1, q=q, rt=rt: e.scalar_tensor_tensor(out=rt[:, 0:wd], in0=q[:, 0:wd], scalar=-float(L),
                                                                                    in1=t1[:, 0:wd], op0=ALU.mult, op1=ALU.add),
                         [q, t1], [rt])
                    S.op("dve", lambda e, rt=rt, ra=ra: e.scalar_tensor_tensor(out=ra[:, 0:wd], in0=rt[:, 0:wd], scalar=-1.0, in1=rt[:, 0:wd], op0=ALU.mult, op1=ALU.max),
                         [rt], [ra])
                    S.op("act", lambda e, rt=rt, Sn=Sn: e.activation(out=Sn[:, 0:wd], in_=rt[:, 0:wd], func=AF.Sin, scale=-w0), [rt], [Sn])
                    S.op("act", lambda e, ra=ra, C=C: e.activation(out=C[:, 0:wd], in_=ra[:, 0:wd], func=AF.Sin, bias=K["halfpi"][:, 0:1],
                                                                    scale=-w0), [ra, K["halfpi"]], [C])
                    for s in range(nsub):
                        S.op("pe", lambda e, s=s, C=C, lt=lt, r=r: e.matmul(ACC[s][:, 0:256], lhsT=C[0:r, s * 128:(s + 1) * 128],
                                                                             rhs=GGs[0:r, lt, 0:256], start=(lt == 0), stop=False),
                             [C, GGs], [ACC[s]])
                        S.op("pe", lambda e, s=s, Sn=Sn, lt=lt, r=r: e.matmul(ACC[s][:, 0:256], lhsT=Sn[0:r, s * 128:(s + 1) * 128],
                                                                               rhs=GGs[0:r, lt, 256:512], start=False, stop=(lt == NT - 1)),
                             [Sn, GGs], [ACC[s]])
                for s in range(nsub):
                    yb = yb_ring.next()
                    S.op("dve", lambda e, s=s, yb=yb: e.scalar_tensor_tensor(out=yb[:], in0=ACC[s][:, 0:256], scalar=isq, in1=bb[:],
                                                                             op0=ALU.mult, op1=ALU.add), [ACC[s], bb], [yb])
                    cx.store(sc["YB"], sc["YB"][n0 + s * 128:n0 + (s + 1) * 128, :], yb, yb[:])

            for B in range(NB):
                blk(B)

    def post_norm_res(self, cx, PM, gpost, hin, hout, sm_ring, junk):
        S, K = self.S, self.K
        sm = sm_ring.next()
        for c in range(2):
            S.op("act", lambda e, c=c: e.activation(out=junk[:, 0:512], in_=PM[c][:], func=AF.Square, accum_out=sm[:, c:c + 1]),
                 [PM[c]], [junk, sm])
        S.op("dve", lambda e: e.tensor_tensor(out=sm[:, 0:1], in0=sm[:, 0:1], in1=sm[:, 1:2], op=ALU.add), [sm], [sm])
        self.rstd(cx, sm, sm[:, 0:1], sm, sm[:, 2:3], sm, sm[:, 3:4], 1.0 / D, K["eps"][:, 0:1])
        for c in range(2):
            S.op("dve", lambda e, c=c: e.scalar_tensor_tensor(out=hout[:, c * 512:(c + 1) * 512], in0=PM[c][:], scalar=sm[:, 2:3],
                                                              in1=gpost[:, c * 512:(c + 1) * 512], op0=ALU.mult, op1=ALU.mult),
                 [PM[c], sm, gpost], [hout])
        S.op("dve", lambda e: e.tensor_tensor(out=hout[:], in0=hout[:], in1=hin[:], op=ALU.add), [hout, hin], [hout])

    def phase_mix_ffn(self, seq, layer, hin_d, hout_d, ya, yb, wout, final):
        nc, S, W, K, sc = self.nc, self.S, self.W, self.K, self.sc
        NT, Lp = seq.NT, seq.Lp
        with Ctx(nc, S) as cx:
            stage = cx.ring("wst", 3, [128, 512], F32)
            Wo = cx.sb("wo", [128, 8, D], BF16)
            self.load_w(cx, Wo, wout, wout[0], 8, D, stage)
            Wg = cx.sb("wg", [128, 8, 2816], BF16)
            self.load_w(cx, Wg, W["ffn_w_gate"], W["ffn_w_gate"][layer], 8, 2816, stage)
            Wu = cx.sb("wu", [128, 8, 2816], BF16)
            self.load_w(cx, Wu, W["ffn_w_up"], W["ffn_w_up"][layer], 8, 2816, stage)
            Wd = cx.sb("wd", [128, 22, D], BF16)
            self.load_w(cx, Wd, W["ffn_w_down"], W["ffn_w_down"][layer], 22, D, stage)
            gs = {}
            for nm in ("post_mix_g", "pre_ffn_g", "post_ffn_g"):
                gs[nm] = cx.sb(nm, [128, D], F32)
                cx.load(gs[nm], gs[nm][:], W[nm], W[nm][layer:layer + 1, :].partition_broadcast(128))
            y_ring = cx.ring("y", 2, [128, D], BF16)
            h_ring = cx.ring("h", 1, [128, D], F32)
            TP = cx.ps("TP", [128, 8, 128], BF16)
            yT_ring = cx.ring("yT", 1, [128, 8, 128], BF16)
            PM = [cx.ps("PM%d" % c, [128, 512], F32) for c in range(2)]
            sm_ring = cx.ring("sm", 4, [128, 4], F32)
            junk = cx.sb("junk", [128, D], BF16)
            h1_ring = cx.ring("h1", 1, [128, D], F32)
            xs_ring = cx.ring("xs", 1, [128, D], BF16)
            hT_ring = cx.ring("hT", 1, [128, 8, 128], BF16)
            PGt = cx.ring("PGt", 2, [128, 4, 128], F32, ps=True)
            PUt = cx.ring("PUt", 2, [128, 4, 128], F32, ps=True)
            sg_ring = cx.ring("sg", 2, [128, 4, 128], F32)
            AT_ring = cx.ring("AT", 1, [128, 22, 128], BF16)
            h2_ring = cx.ring("h2", 1, [128, D], F32)
            vmask = cx.sb("vmask", [128, 1], F32)
            S.op("dve", lambda e: e.memset(vmask[:], 0.0), [], [vmask])
            S.op("dve", lambda e: e.memset(vmask[0:seq.last, :], 1.0), [], [vmask])
            def body(t):
                c0 = t * 128
                y = y_ring.next()
                cx.load(y, y[:, 0:ya[1]], sc[ya[0]], sc[ya[0]][c0:c0 + 128, :])
                cx.load(y, y[:, ya[1]:D], sc[yb[0]], sc[yb[0]][c0:c0 + 128, :])
                h = h_ring.next()
                cx.load(h, h[:], hin_d, hin_d[c0:c0 + 128, :])
                for k in range(8):
                    S.op("pe", lambda e, k=k: e.transpose(out=TP[:, k, :], in_=y[:, k * 128:(k + 1) * 128], identity=K["identb"][:]),
                         [y, K["identb"]], [TP])
                yT = yT_ring.next()
                S.op("act", lambda e: e.copy(out=yT[:], in_=TP[:]), [TP], [yT])
                for c in range(2):
                    for k in range(8):
                        S.op("pe", lambda e, c=c, k=k: e.matmul(PM[c][:], lhsT=yT[:, k, :], rhs=Wo[:, k, c * 512:(c + 1) * 512],
                                                                 start=(k == 0), stop=(k == 7)), [yT, Wo], [PM[c]])
                h1 = h1_ring.next()
                self.post_norm_res(cx, PM, gs["post_mix_g"], h, h1, sm_ring, junk)
                hT = hT_ring.next()
                self.norm_transpose(cx, h1, gs["pre_ffn_g"], hT, TP, xs_ring, sm_ring, junk)
                AT = AT_ring.next()
                for fq in range(6):
                    nf = min(4, 22 - fq * 4)
                    pg = PGt.next(); pu = PUt.next()
                    for (pp, Wm) in ((pg, Wg), (pu, Wu)):
                        for f in range(nf):
                            fc = (fq * 4 + f) * 128
                            for k in range(8):
                                S.op("pe", lambda e, pp=pp, Wm=Wm, f=f, fc=fc, k=k: e.matmul(pp[:, f, :], lhsT=Wm[:, k, fc:fc + 128], rhs=hT[:, k, :],
                                                                                              start=(k == 0), stop=(k == 7)), [Wm, hT], [pp])
                    sg = sg_ring.next()
                    S.op("act", lambda e, pg=pg, sg=sg, nf=nf: e.activation(out=sg[:, 0:nf, :], in_=pg[:, 0:nf, :], func=AF.Silu), [pg], [sg])
                    S.op("dve", lambda e, pu=pu, sg=sg, nf=nf, fq=fq: e.tensor_tensor(out=AT[:, fq * 4:fq * 4 + nf, :], in0=sg[:, 0:nf, :],
                                                                                      in1=pu[:, 0:nf, :], op=ALU.mult), [sg, pu], [AT])
                for c in range(2):
                    for f in range(22):
                        S.op("pe", lambda e, c=c, f=f: e.matmul(PM[c][:], lhsT=AT[:, f, :], rhs=Wd[:, f, c * 512:(c + 1) * 512],
                                                                 start=(f == 0), stop=(f == 21)), [AT, Wd], [PM[c]])
                h2 = h2_ring.next()
                self.post_norm_res(cx, PM, gs["post_ffn_g"], h1, h2, sm_ring, junk)
                if not final:
                    if t == NT - 1:
                        S.op("dve", lambda e, h2=h2: e.tensor_scalar(out=h2[:], in0=h2[:], scalar1=vmask[:, 0:1], scalar2=None, op0=ALU.mult),
                             [h2, vmask], [h2])
                    cx.store(hout_d, hout_d[c0:c0 + 128, :], h2, h2[:])
                else:
                    lo = 16 if t == 0 else 0
                    hi = seq.rows(t)
                    cx.store(hout_d, hout_d[c0 + lo - 16:c0 + hi - 16, :], h2, h2[lo:hi, :])

            for t in range(NT):
                body(t)


    def phase_odd_proj(self, seq):
        nc, S, W, K, sc = self.nc, self.S, self.W, self.K, self.sc
        NT, Lp = seq.NT, seq.Lp
        with Ctx(nc, S) as cx:
            stage = cx.ring("wst", 3, [128, 512], F32)
            Win = cx.sb("win", [128, 8, 2688], BF16)
            self.load_w(cx, Win, W["odd_w_in"], W["odd_w_in"][0], 8, 2688, stage)
            gpre = cx.sb("gpre", [128, D], F32)
            cx.load(gpre, gpre[:], W["pre_mix_g"], W["pre_mix_g"][1:2, :].partition_broadcast(128))
            zc = cx.sb("zc", [128, 15], F32)
            S.op("dve", lambda e: e.memset(zc[:], 0.0), [], [zc])
            UTv = sc["UT"].t.rearrange("(b p) n -> p b n", p=128)
            cx.store(sc["UT"], UTv[:, :, 0:1], zc, zc[:].unsqueeze(2), slow=True)
            cx.store(sc["UT"], UTv[:, :, Lp + 1:Lp + 2], zc, zc[:].unsqueeze(2), slow=True)
            xring = cx.ring("x", 2, [128, D], F32)
            xs_ring = cx.ring("xs", 2, [128, D], BF16)
            sm_ring = cx.ring("sm", 2, [128, 4], F32)
            junk = cx.sb("junk", [128, D], BF16)
            TP = cx.ps("TP", [128, 8, 128], BF16)
            hnT_ring = cx.ring("hnT", 2, [128, 8, 128], BF16)
            PJ = [cx.ps("PJ%d" % c, [128, 512], F32) for c in range(2)]
            PU = cx.ring("PU", 2, [128, 4, 128], F32, ps=True)
            rope_ring = cx.ring("rope", 2, [128, 2, 8], F32)
            sqt = cx.sb("sqt", [128, 8, 64], F32)
            bq_ring = cx.ring("bq", 2, [128, 2, 8], F32)
            qc = cx.sb("qc", [128, 10, 64], F32)
            tA = cx.sb("tA", [128, 10, 8], F32)
            tB = cx.sb("tB", [128, 10, 8], F32)
            qa_ring = cx.ring("qa", 2, [128, 10, 65], BF16)
            for qa in qa_ring.tiles:
                S.op("pool", lambda e, qa=qa: e.memset(qa[:], 1.0), [], [qa])
            vc_ring = cx.ring("vc", 2, [128, 2, 65], BF16)
            for vc in vc_ring.tiles:
                S.op("pool", lambda e, vc=vc: e.memset(vc[:], 1.0), [], [vc])
            TP2 = cx.ps("TP2", [128, 5, 128], BF16)
            TP3 = cx.ps("TP3", [128, 5, 128], BF16)
            qT_ring = cx.ring("qT", 2, [65, 10, 128], BF16)
            u_ring = cx.ring("usb", 2, [128, 15, 128], F32)

            def body(t):
                c0 = t * 128
                xt = xring.next()
                cx.load(xt, xt[:], sc["H1"], sc["H1"][c0:c0 + 128, :])
                rp = rope_ring.next()
                cx.load(rp, rp[:], seq.c["ropeC"], seq.c["ropeC"][c0:c0 + 128])
                hnT = hnT_ring.next()
                self.norm_transpose(cx, xt, gpre, hnT, TP, xs_ring, sm_ring, junk)
                for c, (lo, wdt) in enumerate(((0, 512), (512, 256))):
                    for k in range(8):
                        S.op("pe", lambda e, c=c, k=k, lo=lo, wdt=wdt: e.matmul(PJ[c][:, 0:wdt], lhsT=hnT[:, k, :], rhs=Win[:, k, lo:lo + wdt],
                                                                                 start=(k == 0), stop=(k == 7)), [hnT, Win], [PJ[c]])
                usb = u_ring.next()
                for q4 in range(4):
                    nb = min(4, 15 - q4 * 4)
                    pu = PU.next()
                    for b in range(nb):
                        f0 = 768 + (q4 * 4 + b) * 128
                        for k in range(8):
                            S.op("pe", lambda e, pu=pu, b=b, f0=f0, k=k: e.matmul(pu[:, b, :], lhsT=Win[:, k, f0:f0 + 128], rhs=hnT[:, k, :],
                                                                                  start=(k == 0), stop=(k == 7)), [Win, hnT], [pu])
                    S.op("act" if q4 % 2 else "dve", (lambda e, pu=pu, q4=q4, nb=nb: e.copy(out=usb[:, q4 * 4:q4 * 4 + nb, :], in_=pu[:, 0:nb, :])) if q4 % 2 else
                         (lambda e, pu=pu, q4=q4, nb=nb: e.tensor_copy(out=usb[:, q4 * 4:q4 * 4 + nb, :], in_=pu[:, 0:nb, :])), [pu], [usb])
                cx.store(sc["UT"], UTv[:, :, 1 + c0:1 + c0 + 128], usb, usb[:])
                S.op("act", lambda e: e.activation(out=sqt[:], in_=PJ[0][:].rearrange("p (h d) -> p h d", d=64), func=AF.Square), [PJ[0]], [sqt])
                bq = bq_ring.next()
                S.op("dve", lambda e: e.tensor_reduce(out=bq[:, 0, :], in_=sqt[:], axis=AX.X, op=ALU.add), [sqt], [bq])
                S.op("dve", lambda e: e.tensor_scalar(out=bq[:, 1, :], in0=bq[:, 0, :], scalar1=1.0 / 16, scalar2=None, op0=ALU.mult), [bq], [bq])
                cx.store(sc["BQ"], sc["BQ"][c0:c0 + 128, :], bq, bq[:, 1, :])
                S.op("act", lambda e: e.activation(out=qc[:, 0:8, :], in_=PJ[0][:].rearrange("p (h d) -> p h d", d=64), func=AF.Copy, scale=0.125), [PJ[0]], [qc])
                S.op("act", lambda e: e.copy(out=qc[:, 8:10, :], in_=PJ[1][:, 0:128].rearrange("p (h d) -> p h d", d=64)), [PJ[1]], [qc])
                qa = qa_ring.next()
                x1, x2 = qc[:, :, 0:8], qc[:, :, 8:16]
                cosb = rp[:, 0, :].unsqueeze(1).broadcast_to([128, 10, 8])
                sinb = rp[:, 1, :].unsqueeze(1).broadcast_to([128, 10, 8])
                S.op("dve", lambda e: e.tensor_tensor(out=tA[:], in0=x1, in1=cosb, op=ALU.mult), [qc, rp], [tA])
                S.op("dve", lambda e: e.tensor_tensor(out=tB[:], in0=x2, in1=sinb, op=ALU.mult), [qc, rp], [tB])
                S.op("dve", lambda e: e.tensor_tensor(out=qa[:, :, 0:8], in0=tA[:], in1=tB[:], op=ALU.subtract), [tA, tB], [qa])
                S.op("dve", lambda e: e.tensor_tensor(out=tA[:], in0=x1, in1=sinb, op=ALU.mult), [qc, rp], [tA])
                S.op("dve", lambda e: e.tensor_tensor(out=tB[:], in0=x2, in1=cosb, op=ALU.mult), [qc, rp], [tB])
                S.op("dve", lambda e: e.tensor_tensor(out=qa[:, :, 8:16], in0=tA[:], in1=tB[:], op=ALU.add), [tA, tB], [qa])
                S.op("act", lambda e: e.copy(out=qa[:, :, 16:64], in_=qc[:, :, 16:64]), [qc], [qa])
                S.op("dve", lambda e: e.tensor_scalar(out=qa[:, 0:8, 64], in0=bq[:, 1, :], scalar1=-1.0, scalar2=None, op0=ALU.mult), [bq], [qa])
                for hh in range(10):
                    tp = TP2 if hh < 5 else TP3
                    S.op("pe", lambda e, hh=hh, tp=tp: e.transpose(out=tp[0:65, hh % 5, :], in_=qa[:, hh, :], identity=K["identb"][:]),
                         [qa, K["identb"]], [tp])
                qT = qT_ring.next()
                S.op("act", lambda e: e.copy(out=qT[:, 0:5, :], in_=TP2[0:65, :, :]), [TP2], [qT])
                S.op("dve", lambda e: e.tensor_copy(out=qT[:, 5:10, :], in_=TP3[0:65, :, :]), [TP3], [qT])
                cx.store(sc["QCT"], sc["QCT"][:, :, c0:c0 + 128].rearrange("h d n -> d h n"), qT, qT[:, 0:8, :])
                cx.store(sc["KCT"], sc["KCT"][:, :, c0:c0 + 128].rearrange("h d n -> d h n"), qT, qT[:, 8:10, :])
                vc = vc_ring.next()
                S.op("act", lambda e: e.copy(out=vc[:, :, 0:64], in_=PJ[1][:, 128:256].rearrange("p (h d) -> p h d", d=64)), [PJ[1]], [vc])
                cx.store(sc["VCX"], sc["VCX"][c0:c0 + 128, :, :], vc, vc[:])

            for t in range(NT):
                body(t)

    def phase_attn_c(self, seq):
        nc, S, W, K, sc = self.nc, self.S, self.W, self.K, self.sc
        NT, Lp = seq.NT, seq.Lp
        with Ctx(nc, S) as cx:
            KCs = cx.sb("KCs", [65, 2, Lp], BF16)
            cx.load(KCs, KCs[:], sc["KCT"], sc["KCT"][:, :, 0:Lp].rearrange("h d n -> d h n"))
            VCs = cx.sb("VCs", [128, NT, 2, 65], BF16)
            for t0 in range(0, NT, 16):
                t1_ = min(NT, t0 + 16)
                cx.load(VCs, VCs[:, t0:t1_], sc["VCX"], sc["VCX"][t0 * 128:t1_ * 128].rearrange("(t p) h d -> p t h d", p=128))
            cm = cx.sb("cm", [128, 3, 128], F32)
            cx.load(cm, cm[:], self.G["cmask"], self.G["cmask"][:, :, :])
            cmb = cx.sb("cmb", [128, 3, 128], BF16)
            S.op("dve", lambda e: e.tensor_copy(out=cmb[:], in_=cm[:]), [cm], [cmb])
            snk = cx.sb("snk", [128, 8], F32)
            cx.load(snk, snk[:], W["c_sink"], W["c_sink"][0:1, :].partition_broadcast(128))
            q_ring = cx.ring("qc", 2, [65, 8, 128], BF16)
            bq_ring = cx.ring("bq", 2, [128, 8], F32)
            es_ring = cx.ring("es", 2, [128, 8], F32)
            ST = cx.ring("ST", 2, [128, 4, 128], F32, ps=True)
            OT = cx.ring("OT", 2, [65, 4, 128], F32, ps=True)
            PT = cx.ring("PT", 3, [128, 4, 128], BF16)
            osb_ring = cx.ring("osb", 2, [65, 4, 128], F32)
            TPo = cx.ps("TPo", [128, 4, 65], F32)
            den_ring = cx.ring("den", 2, [128, 2, 4], F32)
            y_ring = cx.ring("yc", 2, [128, 8, 64], BF16)

            def body(t):
                c0 = t * 128
                qt = q_ring.next()
                cx.load(qt, qt[:], sc["QCT"], sc["QCT"][:, :, c0:c0 + 128].rearrange("h d n -> d h n"))
                bq = bq_ring.next()
                cx.load(bq, bq[:], sc["BQ"], sc["BQ"][c0:c0 + 128, :])
                es = es_ring.next()
                S.op("dve", lambda e: e.tensor_tensor(out=es[:], in0=snk[:], in1=bq[:], op=ALU.subtract), [snk, bq], [es])
                S.op("act", lambda e: e.activation(out=es[:], in_=es[:], func=AF.Exp), [es], [es])
                yc = y_ring.next()
                kbs = []
                if t >= 2:
                    kbs.append((0, 16, None))
                if t >= 1:
                    kbs.append((t - 1, 128, 2 if t == 1 else 0))
                kbs.append((t, seq.rows(t), None))
                if t + 1 < NT:
                    kbs.append((t + 1, seq.rows(t + 1), 1))
                def grp(g):
                    ot = OT.next()
                    for bi, (kt, r, mk) in enumerate(kbs):
                        st = ST.next()
                        S.op("pe", lambda e, st=st, kt=kt, r=r, g=g: e.matmul(st[0:r], lhsT=KCs[:, g, kt * 128:kt * 128 + r], rhs=qt[:, g * 4:(g + 1) * 4, :],
                                                                               start=True, stop=True), [KCs, qt], [st])
                        pt = PT.next()
                        S.op("act", lambda e, st=st, pt=pt, r=r: e.activation(out=pt[0:r], in_=st[0:r], func=AF.Exp), [st], [pt])
                        if mk is not None:
                            S.op("dve", lambda e, pt=pt, r=r, mk=mk: e.tensor_tensor(out=pt[0:r], in0=pt[0:r],
                                                                                     in1=cmb[0:r, mk, :].unsqueeze(1).broadcast_to([r, 4, 128]), op=ALU.mult),
                                 [pt, cmb], [pt])
                        S.op("pe", lambda e, pt=pt, kt=kt, r=r, g=g, bi=bi: e.matmul(ot[:], lhsT=VCs[0:r, kt, g, :], rhs=pt[0:r], start=(bi == 0),
                                                                                      stop=(bi == len(kbs) - 1)), [VCs, pt], [ot])
                    osb = osb_ring.next()
                    S.op("dve", lambda e, osb=osb, ot=ot: e.tensor_copy(out=osb[:], in_=ot[:]), [ot], [osb])
                    for hh in range(4):
                        S.op("pe", lambda e, hh=hh, osb=osb: e.transpose(out=TPo[:, hh, :], in_=osb[:, hh, :], identity=K["identf"][0:65, 0:65]),
                             [osb, K["identf"]], [TPo])
                    den = den_ring.next()
                    S.op("dve", lambda e, den=den, g=g: e.tensor_tensor(out=den[:, 0, :], in0=TPo[:, :, 64], in1=es[:, g * 4:(g + 1) * 4], op=ALU.add),
                         [TPo, es], [den])
                    S.op("dve", lambda e, den=den: e.reciprocal(out=den[:, 1, :], in_=den[:, 0, :]), [den], [den])
                    S.op("dve", lambda e, den=den, g=g: e.tensor_tensor(out=yc[:, g * 4:(g + 1) * 4, :], in0=TPo[:, :, 0:64],
                                                                        in1=den[:, 1, :].unsqueeze(2).broadcast_to([128, 4, 64]), op=ALU.mult),
                         [TPo, den], [yc])
                for g in range(2):
                    grp(g)
                cx.store(sc["YC"], sc["YC"][c0:c0 + 128, :], yc, yc[:].rearrange("p h d -> p (h d)"))

            for t in range(NT):
                body(t)

    def phase_rwkv(self, seq, d):
        nc, S, W, K, sc, G = self.nc, self.S, self.W, self.K, self.sc, self.G
        NT, Lp, L, NB = seq.NT, seq.Lp, seq.L, seq.NB
        NEG = -math.exp(-0.5)
        with Ctx(nc, S) as cx:
            m4 = cx.sb("m4", [64, 4, 64], F32)
            cx.load(m4, m4[:], G["m4"], G["m4"][:, :, :])
            reset = cx.sb("reset", [64, 512], F32)
            cx.load(reset, reset[:], G["reset"], G["reset"][:, :])
            ones64 = cx.sb("ones64", [64, 64], F32)
            S.op("dve", lambda e: e.memset(ones64[:], 1.0), [], [ones64])
            onesb = cx.sb("onesb", [64, 2], BF16)
            S.op("dve", lambda e: e.memset(onesb[:], 1.0), [], [onesb])
            lneps = cx.sb("lneps", [64, 1], F32)
            S.op("dve", lambda e: e.memset(lneps[:], 64e-5), [], [lneps])
            zero1 = cx.sb("zero1", [64, 1], F32)
            S.op("dve", lambda e: e.memset(zero1[:], 0.0), [], [zero1])
            mu = cx.sb("mu", [64, 3, 24], F32)
            cx.load(mu, mu[:, 0, :], W["d_mu_prev"], W["d_mu_prev"][0, 0:1536].rearrange("(s p) -> p s", p=64))
            cx.load(mu, mu[:, 1, :], W["d_mu_next"], W["d_mu_next"][0, 0:1536].rearrange("(s p) -> p s", p=64))
            muL = cx.sb("muL", [128, 3, 3], F32)
            cx.load(muL, muL[:, 0, :], W["d_mu_prev"], W["d_mu_prev"][0, 1536:1920].rearrange("(s p) -> p s", p=128))
            cx.load(muL, muL[:, 1, :], W["d_mu_next"], W["d_mu_next"][0, 1536:1920].rearrange("(s p) -> p s", p=128))
            for m_ in (mu, muL):
                S.op("dve", lambda e, m_=m_: e.tensor_tensor(out=m_[:, 2, :], in0=m_[:, 0, :], in1=m_[:, 1, :], op=ALU.add), [m_], [m_])
                S.op("dve", lambda e, m_=m_: e.tensor_scalar(out=m_[:, 2, :], in0=m_[:, 2, :], scalar1=-1.0, scalar2=1.0, op0=ALU.mult, op1=ALU.add), [m_], [m_])
            pc = cx.sb("pc", [64, 8, 8], F32)
            cx.load(pc, pc[:, 0:2, :], W["d_w0"], W["d_w0"][0].rearrange("d (h p) -> p d h", p=64))
            cx.load(pc, pc[:, 2:4, :], W["d_a0"], W["d_a0"][0].rearrange("d (h p) -> p d h", p=64))
            cx.load(pc, pc[:, 4, :], W["d_k_k"], W["d_k_k"][0].rearrange("(h p) -> p h", p=64))
            cx.load(pc, pc[:, 5, :], W["d_k_a"], W["d_k_a"][0].rearrange("(h p) -> p h", p=64))
            cx.load(pc, pc[:, 7, :], W["d_r_k"], W["d_r_k"][0].rearrange("h p -> p h"))
            S.op("dve", lambda e: e.tensor_scalar(out=pc[:, 6, :], in0=pc[:, 5, :], scalar1=-1.0, scalar2=1.0, op0=ALU.mult, op1=ALU.add), [pc], [pc])
            stg = cx.sb("stg", [128, 512], F32)
            wup = cx.sb("wup", [128, 512], BF16)
            aup = cx.sb("aup", [128, 512], BF16)
            gup = cx.sb("gup", [128, 512], BF16)
            for (dst, srcs) in ((wup, [(W["d_w_up"], W["d_w_up"][0, 0]), (W["d_w_up"], W["d_w_up"][0, 1])]),
                                (aup, [(W["d_a_up"], W["d_a_up"][0, 0]), (W["d_a_up"], W["d_a_up"][0, 1])]),
                                (gup, [(W["d_g_up"], W["d_g_up"][0, 0:64]), (W["d_g_up"], W["d_g_up"][0, 64:128])])):
                for i_, (tl_, ap_) in enumerate(srcs):
                    cx.load(stg, stg[i_ * 64:(i_ + 1) * 64, :], tl_, ap_)
                S.op("dve", lambda e, dst=dst: e.tensor_copy(out=dst[:], in_=stg[:]), [stg], [dst])
            lng = cx.sb("lng", [64, 512], F32)
            lnb = cx.sb("lnb", [64, 512], F32)
            cx.load(lng, lng[:], W["d_ln_g"], W["d_ln_g"][0:1, :].partition_broadcast(64))
            cx.load(lnb, lnb[:], W["d_ln_b"], W["d_ln_b"][0:1, :].partition_broadcast(64))
            idb64 = K["identb"][0:64, 0:64]
            idf64 = K["identf"][0:64, 0:64].unsqueeze(1).broadcast_to([64, 8, 64])

            TPa = cx.ps("TPa", [64, 2, 8, 64], BF16)
            TPb = cx.ps("TPb", [64, 2, 8, 64], BF16)
            GP = cx.ring("GP", 5, [64, 8, 64], F32, ps=True)
            PP = cx.ps("PP", [64, 512], F32)

            TW = cx.sb("TW", [128, 512], BF16)
            AW = cx.sb("AW", [128, 512], BF16)
            SG = cx.sb("SG", [128, 512], BF16)
            ubL = cx.sb("ubL", [128, 514], F32)
            mlt = cx.sb("mlt", [128, 512], F32)
            ub3 = cx.sb("ub3", [64, 3, 514], F32)
            f32t = {n: cx.sb(n, [64, 512], F32) for n in ("rr", "kr", "vv", "lw", "cu", "e1", "e2", "e3", "aa", "kn", "t1", "kd", "bb", "a0t")}
            RT = cx.sb("RT", [64, 8, 512], BF16)
            KT = cx.sb("KT", [64, 8, 512], BF16)
            BT = cx.sb("BT", [64, 8, 512], BF16)
            AT = cx.sb("AT", [64, 8, 512], BF16)
            VT = cx.sb("VT", [64, 8, 512], BF16)
            RKD = cx.sb("RKD", [64, 8, 512], BF16)
            GE = cx.sb("GE", [64, 8, 8], F32)
            Mf = cx.sb("Mf", [64, 8, 64], F32)
            Mb = cx.sb("Mb", [64, 8, 64], BF16)
            S.op("dve", lambda e: e.memset(Mf[:], 0.0), [], [Mf])
            S.op("dve", lambda e: e.memset(Mb[:], 0.0), [], [Mb])
            tm_ring = cx.ring("tm", 5, [64, 4, 8, 64], BF16)
            bt = lambda nm, n=2: cx.ring(nm, n, [64, 8, 64], BF16)
            Yr, P0r, Lakr, Lrbr, Lrkr = bt("Y", 3), bt("P0", 3), bt("Lak", 3), bt("Lrb", 5), bt("Lrk", 5)
            ybuf, pbuf, rbuf, Rfr = bt("yb", 4), bt("pb", 4), bt("rb", 6), bt("Rf", 5)
            Xr, Wr, Ur = bt("X", 5), bt("WT", 5), bt("U", 2)
            o_ring = cx.ring("o", 2, [64, 8, 64], F32)
            of_ring = cx.ring("of", 2, [64, 8, 64], F32)
            cen = cx.sb("cen", [64, 8, 64], F32)
            sqn = cx.sb("sqn", [64, 8, 64], F32)
            st_ring = cx.ring("st", 2, [64, 6, 8], F32)
            yd_ring = cx.ring("yd", 2, [64, 8, 64], BF16)
            UT = sc["UT"]
            UT3 = UT.t[0:1536, :].rearrange("(s q) n -> q s n", q=512)
            strict, incl, strictT = (0, 1, 2) if d == 0 else (2, 3, 0)
            bc8 = lambda ap: ap.unsqueeze(2).broadcast_to([64, 8, 64])

            def mix(dst_ap, ub_prev, ub_c, ub_next, mu_t, idx, dst_tl, ub_tl):
                S.op("dve", lambda e: e.tensor_scalar(out=dst_ap, in0=ub_c, scalar1=mu_t[:, 2, idx:idx + 1], scalar2=None, op0=ALU.mult), [ub_tl, mu_t], [dst_tl])
                S.op("dve", lambda e: e.scalar_tensor_tensor(out=dst_ap, in0=ub_prev, scalar=mu_t[:, 0, idx:idx + 1], in1=dst_ap, op0=ALU.mult, op1=ALU.add),
                     [ub_tl, mu_t, dst_tl], [dst_tl])
                S.op("dve", lambda e: e.scalar_tensor_tensor(out=dst_ap, in0=ub_next, scalar=mu_t[:, 1, idx:idx + 1], in1=dst_ap, op0=ALU.mult, op1=ALU.add),
                     [ub_tl, mu_t, dst_tl], [dst_tl])

            def prep(bi):
                n0 = bi * 512
                Wd = min(512, Lp - n0)
                nch = Wd // 64
                pv = min(Wd, max(0, L - n0))
                for li, (dst, fn_) in enumerate(((TW, AF.Tanh), (AW, AF.Copy), (SG, AF.Sigmoid))):
                    cx.load(ubL, ubL[:, 0:Wd + 2], UT, UT[1536 + li * 128:1536 + (li + 1) * 128, n0:n0 + Wd + 2])
                    mix(mlt[:, 0:Wd], ubL[:, 0:Wd], ubL[:, 1:Wd + 1], ubL[:, 2:Wd + 2], muL, li, mlt, ubL)
                    S.op("act", lambda e, dst=dst, fn_=fn_: e.activation(out=dst[:, 0:Wd], in_=mlt[:, 0:Wd], func=fn_), [mlt], [dst])
                for h in range(8):
                    prep_head(h, n0, Wd, nch, pv)

            def prep_head(h, n0, Wd, nch, pv):
                T = f32t
                w = slice(0, Wd)
                cx.load(ub3, ub3[:, :, 0:Wd + 2], UT, UT3[h * 64:(h + 1) * 64, :, n0:n0 + Wd + 2])
                for sec, nm in ((0, "rr"), (1, "kr"), (2, "vv")):
                    mix(T[nm][:, w], ub3[:, sec, 0:Wd], ub3[:, sec, 1:Wd + 1], ub3[:, sec, 2:Wd + 2], mu, sec * 8 + h, T[nm], ub3)
                    if pv < Wd:
                        S.op("pool", lambda e, nm=nm: e.memset(T[nm][:, pv:Wd], 0.0), [], [T[nm]])
                hs = slice(h * 64, (h + 1) * 64)
                ds_ = slice(d * 64, (d + 1) * 64)
                S.op("pe", lambda e: e.matmul(PP[:, w], lhsT=wup[ds_, hs], rhs=TW[ds_, w], start=True, stop=True), [wup, TW], [PP])
                S.op("act", lambda e: e.activation(out=T["lw"][:, w], in_=PP[:, w], func=AF.Sigmoid, bias=pc[:, d, h:h + 1], scale=1.0), [PP, pc], [T["lw"]])
                S.op("dve", lambda e: e.tensor_scalar(out=T["lw"][:, w], in0=T["lw"][:, w], scalar1=NEG, scalar2=None, op0=ALU.mult), [T["lw"]], [T["lw"]])
                S.op("dve", lambda e: e.tensor_tensor_scan(out=T["cu"][:, w], data0=reset[:, w], data1=T["lw"][:, w], initial=0.0, op0=ALU.mult, op1=ALU.add),
                     [reset, T["lw"]], [T["cu"]])
                if d == 1:
                    cu3 = T["cu"][:, w].rearrange("p (c t) -> p c t", t=64)
                    S.op("dve", lambda e: e.tensor_tensor(out=T["t1"][:, w], in0=T["lw"][:, w], in1=T["cu"][:, w], op=ALU.subtract), [T["lw"], T["cu"]], [T["t1"]])
                    S.op("dve", lambda e: e.tensor_tensor(out=T["e1"][:, w].rearrange("p (c t) -> p c t", t=64), in0=T["t1"][:, w].rearrange("p (c t) -> p c t", t=64),
                                                          in1=cu3[:, :, 63:64].broadcast_to([64, nch, 64]), op=ALU.add), [T["t1"], T["cu"]], [T["e1"]])
                    S.op("dve", lambda e: e.tensor_copy(out=T["cu"][:, w], in_=T["e1"][:, w]), [T["e1"]], [T["cu"]])
                S.op("act", lambda e: e.activation(out=T["e1"][:, w], in_=T["cu"][:, w], func=AF.Exp), [T["cu"]], [T["e1"]])
                S.op("act", lambda e: e.activation(out=T["e2"][:, w], in_=T["cu"][:, w], func=AF.Exp, scale=-1.0), [T["cu"]], [T["e2"]])
                S.op("dve", lambda e: e.tensor_tensor(out=T["t1"][:, w], in0=T["cu"][:, w], in1=T["lw"][:, w], op=ALU.subtract), [T["cu"], T["lw"]], [T["t1"]])
                S.op("act", lambda e: e.activation(out=T["e3"][:, w], in_=T["t1"][:, w], func=AF.Exp), [T["t1"]], [T["e3"]])
                S.op("pe", lambda e: e.matmul(PP[:, w], lhsT=aup[ds_, hs], rhs=AW[ds_, w], start=True, stop=True), [aup, AW], [PP])
                S.op("act", lambda e: e.activation(out=T["aa"][:, w], in_=PP[:, w], func=AF.Sigmoid, bias=pc[:, 2 + d, h:h + 1], scale=1.0), [PP, pc], [T["aa"]])
                S.op("dve", lambda e: e.tensor_scalar(out=T["kn"][:, w], in0=T["kr"][:, w], scalar1=pc[:, 4, h:h + 1], scalar2=None, op0=ALU.mult), [T["kr"], pc], [T["kn"]])
                S.op("act", lambda e: e.activation(out=T["t1"][:, w], in_=T["kn"][:, w], func=AF.Square), [T["kn"]], [T["t1"]])
                S.op("pe", lambda e: e.matmul(PP[:, w], lhsT=ones64[:], rhs=T["t1"][:, w], start=True, stop=True), [ones64, T["t1"]], [PP])
                S.op("dve", lambda e: e.tensor_scalar(out=T["t1"][:, w], in0=PP[:, w], scalar1=1e-24, scalar2=None, op0=ALU.max), [PP], [T["t1"]])
                S.op("act", lambda e: e.activation(out=T["t1"][:, w], in_=T["t1"][:, w], func=AF.Sqrt, bias=zero1[:, 0:1], scale=1.0), [T["t1"], zero1], [T["t1"]])
                S.op("dve", lambda e: e.reciprocal(out=T["t1"][:, w], in_=T["t1"][:, w]), [T["t1"]], [T["t1"]])
                S.op("dve", lambda e: e.tensor_tensor(out=T["kn"][:, w], in0=T["kn"][:, w], in1=T["t1"][:, w], op=ALU.mult), [T["kn"], T["t1"]], [T["kn"]])
                S.op("dve", lambda e: e.tensor_scalar(out=T["kd"][:, w], in0=T["aa"][:, w], scalar1=pc[:, 5, h:h + 1], scalar2=pc[:, 6, h:h + 1], op0=ALU.mult, op1=ALU.add),
                     [T["aa"], pc], [T["kd"]])
                S.op("dve", lambda e: e.tensor_tensor(out=T["kd"][:, w], in0=T["kd"][:, w], in1=T["kr"][:, w], op=ALU.mult), [T["kd"], T["kr"]], [T["kd"]])
                S.op("pool", lambda e: e.tensor_tensor(out=T["bb"][:, w], in0=T["kn"][:, w], in1=T["aa"][:, w], op=ALU.mult), [T["kn"], T["aa"]], [T["bb"]])
                S.op("dve", lambda e: e.tensor_tensor(out=RT[:, h, w], in0=T["rr"][:, w], in1=T["e1"][:, w], op=ALU.mult), [T["rr"], T["e1"]], [RT])
                S.op("pool", lambda e: e.tensor_tensor(out=KT[:, h, w], in0=T["kd"][:, w], in1=T["e2"][:, w], op=ALU.mult), [T["kd"], T["e2"]], [KT])
                S.op("pool", lambda e: e.tensor_tensor(out=BT[:, h, w], in0=T["bb"][:, w], in1=T["e2"][:, w], op=ALU.mult), [T["bb"], T["e2"]], [BT])
                S.op("dve", lambda e: e.scalar_tensor_tensor(out=AT[:, h, w], in0=T["kn"][:, w], scalar=-1.0, in1=T["e3"][:, w], op0=ALU.mult, op1=ALU.mult),
                     [T["kn"], T["e3"]], [AT])
                S.op("act", lambda e: e.copy(out=VT[:, h, w], in_=T["vv"][:, w]), [T["vv"]], [VT])
                e13 = T["e1"][:, w].rearrange("p (c t) -> p c t", t=64)
                gcol = 63 if d == 0 else 0
                S.op("dve", lambda e: e.tensor_copy(out=GE[:, 0:nch, h], in_=e13[:, :, gcol]), [T["e1"]], [GE])
                if d == 1:
                    do = slice(0, 64)
                    S.op("pe", lambda e: e.matmul(PP[:, w], lhsT=aup[do, hs], rhs=AW[do, w], start=True, stop=True), [aup, AW], [PP])
                    S.op("act", lambda e: e.activation(out=T["a0t"][:, w], in_=PP[:, w], func=AF.Sigmoid, bias=pc[:, 2, h:h + 1], scale=1.0), [PP, pc], [T["a0t"]])
                    S.op("dve", lambda e: e.tensor_scalar(out=T["a0t"][:, w], in0=T["a0t"][:, w], scalar1=pc[:, 5, h:h + 1], scalar2=pc[:, 6, h:h + 1], op0=ALU.mult, op1=ALU.add),
                         [T["a0t"], pc], [T["a0t"]])
                    S.op("dve", lambda e: e.tensor_tensor(out=T["a0t"][:, w], in0=T["a0t"][:, w], in1=T["kr"][:, w], op=ALU.mult), [T["a0t"], T["kr"]], [T["a0t"]])
                    S.op("dve", lambda e: e.tensor_tensor(out=T["a0t"][:, w], in0=T["a0t"][:, w], in1=T["kd"][:, w], op=ALU.add), [T["a0t"], T["kd"]], [T["a0t"]])
                    S.op("dve", lambda e: e.scalar_tensor_tensor(out=RKD[:, h, w], in0=T["rr"][:, w], scalar=pc[:, 7, h:h + 1], in1=T["a0t"][:, w], op0=ALU.mult, op1=ALU.mult),
                         [T["rr"], pc, T["a0t"]], [RKD])

            def mm8(ps, lhs_fn, rhs_fn, lt, rt, extra=()):
                for h in range(8):
                    terms = [(lhs_fn, rhs_fn)] + list(extra)
                    for i_, (lf, rf) in enumerate(terms):
                        S.op("pe", lambda e, h=h, lf=lf, rf=rf, i_=i_, n_=len(terms): e.matmul(ps[:, h, :], lhsT=lf(h), rhs=rf(h), start=(i_ == 0), stop=(i_ == n_ - 1)),
                             lt, [ps])

            pk = {}

            def stageA(n0, c):
                cs = slice(c * 64, (c + 1) * 64)
                for ki, X in enumerate((AT, BT)):
                    for h in range(8):
                        S.op("pe", lambda e, ki=ki, X=X, h=h: e.transpose(out=TPa[:, ki, h, :], in_=X[:, h, cs], identity=idb64), [X, K["identb"]], [TPa])
                for ki, X in enumerate((KT, VT)):
                    for h in range(8):
                        S.op("pe", lambda e, ki=ki, X=X, h=h: e.transpose(out=TPb[:, ki, h, :], in_=X[:, h, cs], identity=idb64), [X, K["identb"]], [TPb])
                tm = tm_ring.next()
                S.op("act", lambda e: e.copy(out=tm[:, 0:2], in_=TPa[:]), [TPa], [tm])
                S.op("dve", lambda e: e.tensor_copy(out=tm[:, 2:4], in_=TPb[:]), [TPb], [tm])
                yield
                A_tm, V_tm = (lambda h: tm[:, 0, h, :]), (lambda h: tm[:, 3, h, :])

                def score(Lh, Rh, mi, ring):
                    ps = GP.next()
                    mm8(ps, lambda h: Lh[:, h, cs], lambda h: Rh[:, h, cs], [Lh, Rh], None)
                    o = ring.next()
                    S.op("dve", lambda e: e.tensor_tensor(out=o[:], in0=ps[:], in1=m4[:, mi, :].unsqueeze(1).broadcast_to([64, 8, 64]), op=ALU.mult), [ps, m4], [o])
                    return o
                Y = score(BT, AT, strict, Yr)
                P0 = score(AT, BT, strictT, P0r)
                yield
                LakT = score(KT, AT, strict, Lakr)
                LrbT = score(BT, RT, incl, Lrbr)
                LrkT = score(KT, RT, incl, Lrkr)
                R = rbuf.next()
                S.op("dve", lambda e, R=R: e.tensor_tensor(out=R[:], in0=Y[:], in1=idf64, op=ALU.add), [Y, K["identf"]], [R])
                yield
                Yk, Pk = Y, P0
                for lvl in range(5):
                    psP = GP.next()
                    mm8(psP, lambda h, Yk=Yk: Yk[:, h, :], lambda h, Pk=Pk: Pk[:, h, :], [Yk, Pk], None)
                    Pn = pbuf.next()
                    S.op("dve", lambda e, Pn=Pn, psP=psP: e.tensor_copy(out=Pn[:], in_=psP[:]), [psP], [Pn])
                    if lvl < 4:
                        psY = GP.next()
                        mm8(psY, lambda h, Pk=Pk: Pk[:, h, :], lambda h, Yk=Yk: Yk[:, h, :], [Yk, Pk], None)
                        Yn = ybuf.next()
                        S.op("act", lambda e, Yn=Yn, psY=psY: e.copy(out=Yn[:], in_=psY[:]), [psY], [Yn])
                    yield
                    psR = GP.next()
                    mm8(psR, lambda h, Pn=Pn: Pn[:, h, :], lambda h, R=R: R[:, h, :], [Pn, R], None)
                    Rn = rbuf.next() if lvl < 4 else Rfr.next()
                    S.op("dve", lambda e, Rn=Rn, psR=psR, R=R: e.tensor_tensor(out=Rn[:], in0=psR[:], in1=R[:], op=ALU.add), [psR, R], [Rn])
                    R = Rn
                    Yk, Pk = (Yn if lvl < 4 else Yk), Pn
                    yield
                Rf = R
                psX = GP.next()
                mm8(psX, lambda h: LakT[:, h, :], V_tm, [LakT, tm], None)
                Xs = Xr.next()
                S.op("act", lambda e: e.copy(out=Xs[:], in_=psX[:]), [psX], [Xs])
                psW = GP.next()
                mm8(psW, A_tm, lambda h: Rf[:, h, :], [tm, Rf], None)
                WTs = Wr.next()
                S.op("dve", lambda e: e.tensor_copy(out=WTs[:], in_=psW[:]), [psW], [WTs])
                pk[c] = (tm, LrbT, LrkT, Rf, Xs, WTs)
                yield

            def chain(n0, c):
                cs = slice(c * 64, (c + 1) * 64)
                row0 = n0 + c * 64
                tm, LrbT, LrkT, R, Xs, WTs = pk.pop(c)
                B_tm, K_tm, V_tm = (lambda h: tm[:, 1, h, :]), (lambda h: tm[:, 2, h, :]), (lambda h: tm[:, 3, h, :])
                psU = GP.next()
                mm8(psU, lambda h: WTs[:, h, :], lambda h: Mb[:, h, :], [WTs, Mb, R, Xs], None, extra=[(lambda h: R[:, h, :], lambda h: Xs[:, h, :])])
                Us = Ur.next()
                S.op("act", lambda e: e.copy(out=Us[:], in_=psU[:]), [psU], [Us])
                yield
                psO = GP.next()
                mm8(psO, lambda h: RT[:, h, cs], lambda h: Mb[:, h, :], [RT, Mb, LrbT, Us, LrkT, tm], None,
                    extra=[(lambda h: LrbT[:, h, :], lambda h: Us[:, h, :]), (lambda h: LrkT[:, h, :], V_tm)])
                psM = GP.next()
                mm8(psM, B_tm, lambda h: Us[:, h, :], [tm, Us], None, extra=[(K_tm, V_tm)])
                ob = o_ring.next()
                if d == 0:
                    S.op("act", lambda e: e.copy(out=ob[:], in_=psO[:]), [psO], [ob])
                    cx.store(sc["OF"], sc["OF"][row0:row0 + 64, :], ob, ob[:].rearrange("p h i -> p (h i)"))
                else:
                    of = of_ring.next()
                    cx.load(of, of[:].rearrange("p h i -> p (h i)"), sc["OF"], sc["OF"][row0:row0 + 64, :])
                    S.op("dve", lambda e: e.tensor_tensor(out=ob[:], in0=psO[:], in1=of[:], op=ALU.add), [psO, of], [ob])
                S.op("dve", lambda e: e.tensor_tensor(out=Mf[:], in0=psM[:], in1=Mf[:], op=ALU.add), [psM, Mf], [Mf])
                S.op("dve", lambda e: e.tensor_tensor(out=Mf[:], in0=Mf[:], in1=bc8(GE[:, c, :]), op=ALU.mult), [Mf, GE], [Mf])
                S.op("act", lambda e: e.copy(out=Mb[:], in_=Mf[:]), [Mf], [Mb])
                yield
                if d == 1:
                    st = st_ring.next()
                    S.op("dve", lambda e: e.tensor_reduce(out=st[:, 0, :], in_=ob[:], axis=AX.X, op=ALU.add), [ob], [st])
                    S.op("dve", lambda e: e.tensor_scalar(out=st[:, 1, :], in0=st[:, 0, :], scalar1=-1.0 / 64, scalar2=None, op0=ALU.mult), [st], [st])
                    S.op("dve", lambda e: e.tensor_tensor(out=cen[:], in0=ob[:], in1=bc8(st[:, 1, :]), op=ALU.add), [ob, st], [cen])
                    S.op("act", lambda e: e.activation(out=sqn[:], in_=cen[:], func=AF.Square), [cen], [sqn])
                    S.op("dve", lambda e: e.tensor_reduce(out=st[:, 2, :], in_=sqn[:], axis=AX.X, op=ALU.add), [sqn], [st])
                    self.rstd(cx, st, st[:, 2, :], st, st[:, 4, :], st, st[:, 3, :], 1.0 / 64, lneps[:, 0:1])
                    yield
                    S.op("dve", lambda e: e.tensor_tensor(out=cen[:], in0=cen[:], in1=bc8(st[:, 4, :]), op=ALU.mult), [cen, st], [cen])
                    S.op("dve", lambda e: e.tensor_tensor(out=cen[:], in0=cen[:], in1=lng[:].rearrange("p (h i) -> p h i", i=64), op=ALU.mult), [cen, lng], [cen])
                    S.op("dve", lambda e: e.tensor_tensor(out=cen[:], in0=cen[:], in1=lnb[:].rearrange("p (h i) -> p h i", i=64), op=ALU.add), [cen, lnb], [cen])
                    psB = GP.next()
                    for h in range(8):
                        S.op("pe", lambda e, h=h: e.matmul(psB[:, h, 0:1], lhsT=RKD[:, h, cs], rhs=onesb[:, 0:1], start=True, stop=True), [RKD, onesb], [psB])
                    S.op("act", lambda e: e.copy(out=st[:, 5, :], in_=psB[:, :, 0]), [psB], [st])
                    S.op("dve", lambda e: e.tensor_tensor(out=sqn[:], in0=tm[:, 3], in1=bc8(st[:, 5, :]), op=ALU.mult), [tm, st], [sqn])
                    yield
                    S.op("dve", lambda e: e.tensor_tensor(out=cen[:], in0=cen[:], in1=sqn[:], op=ALU.add), [cen, sqn], [cen])
                    psG = GP.next()
                    S.op("pe", lambda e: e.matmul(psG[:].rearrange("p h i -> p (h i)"), lhsT=SG[:, cs], rhs=gup[:], start=True, stop=True), [SG, gup], [psG])
                    yd = yd_ring.next()
                    S.op("dve", lambda e: e.tensor_tensor(out=yd[:], in0=cen[:], in1=psG[:], op=ALU.mult), [cen, psG], [yd])
                    cx.store(sc["YD"], sc["YD"][row0:row0 + 64, :], yd, yd[:].rearrange("p h i -> p (h i)"))
                    yield

            def run_block(n0, cl):
                LOOKA = 2
                gens = {}
                nA = 0
                doneA = set()
                ci = 0
                chain_g = None
                while ci < len(cl):
                    while nA < len(cl) and len(gens) < LOOKA and nA <= ci + LOOKA:
                        gens[nA] = stageA(n0, cl[nA])
                        nA += 1
                    if chain_g is None and ci in doneA:
                        chain_g = chain(n0, cl[ci])
                    for i_ in list(gens.keys()):
                        try:
                            next(gens[i_])
                        except StopIteration:
                            del gens[i_]
                            doneA.add(i_)
                    if chain_g is not None:
                        try:
                            next(chain_g)
                        except StopIteration:
                            chain_g = None
                            ci += 1

            blocks = range(NB) if d == 0 else range(NB - 1, -1, -1)
            for bi in blocks:
                prep(bi)
                n0 = bi * 512
                nch = min(512, Lp - n0) // 64
                run_block(n0, list(range(nch) if d == 0 else range(nch - 1, -1, -1)))


def make_inputs(prog, inputs, core):
    m = {}
    for k in prog.W:
        m[k] = np.ascontiguousarray(np.asarray(inputs[k], dtype=np.float32))
    for k, v in global_consts().items():
        m["c_" + k] = v
    meta = np.asarray(inputs["meta_tokens"], dtype=np.float32)
    for s in prog.seqs:
        x = s.xsrc(inputs, core)
        h0 = np.zeros((s.Lp, D), np.float32)
        h0[:16] = meta
        h0[16:s.L] = x
        m["h0_" + s.name] = h0
        for k, v in host_consts(s).items():
            m["c_%s_%s" % (k, s.name)] = v
    return m


def kernel(**inputs):
    xp = np.asarray(inputs["x_prompt"], dtype=np.float32)
    xs = np.asarray(inputs["x_sample"], dtype=np.float32)
    sp = Seq("p", xp.shape[1])
    ss = Seq("s", xs.shape[1])
    sp.xsrc = lambda inp, core: xp[core]
    ss.xsrc = lambda inp, core: xs[0]
    prog = Prog([sp, ss], debug=False)
    nc = prog.build()
    in_maps = []
    for c in range(NCORE):
        m = make_inputs(prog, inputs, c)
        in_maps.append({k: m[k] for k in prog.in_names})
    res = run_bass_kernel_spmd(nc, in_maps, core_ids=list(range(NCORE)))
    r = res.results
    y_prompt = np.stack([np.asarray(r[c]["y_p"], dtype=np.float32) for c in range(NCORE)], axis=0)
    n = xs.shape[1] // NCORE
    y_sample = np.concatenate([np.asarray(r[c]["y_s"], dtype=np.float32)[c * n:(c + 1) * n] for c in range(NCORE)], axis=0)[None]
    return (y_prompt, y_sample)
```

```python
import math
import contextlib
import numpy as np
import concourse.bass as bass
import concourse.mybir as mybir
from concourse.bass_utils import run_bass_kernel_spmd

F32 = mybir.dt.float32
BF16 = mybir.dt.bfloat16
I32 = mybir.dt.int32
AF = mybir.ActivationFunctionType
ALU = mybir.AluOpType
AX = mybir.AxisListType

D = 1024
NCORE = 8
EPS = 1e-6
ENG_NAMES = ("pe", "act", "dve", "pool", "sp")
SAME_ENGINE_SYNC = True


class Buf:
    __slots__ = ("name", "w", "r", "dsem", "dcnt", "excl")

    def __init__(self, name):
        self.name = name
        self.excl = False
        self.w = {}
        self.r = {}
        self.dsem = None
        self.dcnt = 0


class TL:
    def __init__(self, t, name):
        self.t = t
        self.b = Buf(name)

    def __getitem__(self, i):
        return self.t[i]


def _b(x):
    return x.b if isinstance(x, TL) else x


class Sched:
    def __init__(self, nc):
        self.nc = nc
        self.ops = {e: [] for e in ENG_NAMES}
        self.sem = {e: nc.alloc_semaphore(name="s_" + e) for e in ENG_NAMES + ("pe1",)}
        self.cnt = {e: 0 for e in ENG_NAMES + ("pe1",)}
        self.seen = {e: {} for e in ENG_NAMES}
        self.dpool = []
        self.dlive = []
        self.dall = {}
        self.nsem = 0

    def _need(self, e, toks, is_dma):
        waits = []
        for sid, (sem, val, src) in toks.items():
            if not is_dma and src == e and (e == "pe" or not SAME_ENGINE_SYNC):
                continue
            if self.seen[e].get(sid, 0) >= val:
                continue
            self.seen[e][sid] = val
            waits.append((sem, val))
        return waits

    def _collect(self, e, reads, writes, is_dma):
        waits = []
        for b in reads:
            waits += self._need(e, b.w, is_dma)
        for b in writes:
            waits += self._need(e, b.w, is_dma)
            waits += self._need(e, b.r, is_dma)
        return waits

    def op(self, e, fn, reads=(), writes=(), lane=0):
        reads = [_b(x) for x in reads]
        writes = [_b(x) for x in writes]
        writes = writes + [b for b in reads if b.excl and b not in writes]
        waits = self._collect(e, reads, writes, False)
        sk = "pe1" if (e == "pe" and lane == 1) else e
        self.cnt[sk] += 1
        tok = (self.sem[sk], self.cnt[sk], e)
        sid = id(self.sem[sk])
        for b in reads:
            b.r[sid] = tok
        for b in writes:
            b.w[sid] = tok
        self.ops[e].append((waits, fn, (self.sem[sk], 1)))

    def dma(self, e, fn, reads=(), writes=(), key=None):
        reads = [_b(x) for x in reads]
        writes = [_b(x) for x in writes]
        key = _b(key)
        waits = self._collect(e, reads, writes, True)
        if key.dsem is None:
            if self.dpool:
                key.dsem, key.dcnt = self.dpool.pop()
            else:
                key.dsem = self.nc.alloc_semaphore(name="d%d" % self.nsem)
                self.nsem += 1
                key.dcnt = 0
            self.dlive.append(key)
        key.dcnt += 16
        tok = (key.dsem, key.dcnt, "dma")
        sid = id(key.dsem)
        self.dall[sid] = [key.dsem, key.dcnt]
        for b in reads:
            b.r[sid] = tok
        for b in writes:
            b.w[sid] = tok
        self.ops[e].append((waits, fn, (key.dsem, 16)))

    def barrier(self, engines=ENG_NAMES):
        toks = [(self.sem[e], self.cnt[e]) for e in ENG_NAMES + ("pe1",) if self.cnt[e] > 0]
        toks += [(s, c) for s, c in self.dall.values()]
        for e in engines:
            waits = []
            for sem, val in toks:
                if self.seen[e].get(id(sem), 0) >= val:
                    continue
                self.seen[e][id(sem)] = val
                waits.append((sem, val))
            if waits:
                self.ops[e].append((waits, None, None))
        for b in self.dlive:
            self.dpool.append((b.dsem, b.dcnt))
            b.dsem = None
        self.dlive = []

    def emit(self):
        nc = self.nc
        with nc.Block() as block:
            for e in ENG_NAMES:
                ops = self.ops[e]
                if not ops:
                    continue

                def body(eng, ops=ops):
                    for waits, fn, inc in ops:
                        for sem, val in waits:
                            eng.wait_ge(sem, val)
                        if fn is not None:
                            fn(eng).then_inc(inc[0], inc[1])

                dec = {"pe": block.tensor, "act": block.scalar, "dve": block.vector,
                       "pool": block.gpsimd, "sp": block.sync}[e]
                dec(body)


class Ring:
    def __init__(self, tiles):
        self.tiles = tiles
        self.i = 0

    def next(self):
        t = self.tiles[self.i % len(self.tiles)]
        self.i += 1
        return t


class Ctx:
    uid = 0

    def __init__(self, nc, S):
        self.nc = nc
        self.S = S
        self.es = contextlib.ExitStack()
        self.dq_i = 0

    def __enter__(self):
        self.es.__enter__()
        return self

    def __exit__(self, *a):
        self.S.barrier()
        return self.es.__exit__(*a)

    def sb(self, name, shape, dt):
        Ctx.uid += 1
        n = "%s_%d" % (name, Ctx.uid)
        return TL(self.es.enter_context(self.nc.sbuf_tensor(n, list(shape), dt)), n)

    def ps(self, name, shape, dt):
        Ctx.uid += 1
        n = "%s_%d" % (name, Ctx.uid)
        esz = 4 if dt in (F32, I32) else 2
        tot = 1
        for d in shape[1:]:
            tot *= d
        assert tot * esz <= 2048, (name, shape)
        full = self.es.enter_context(self.nc.psum_tensor(n, [128, 2048 // esz], dt))
        view = full[0:shape[0], 0:tot]
        if len(shape) == 3:
            view = view.rearrange("p (a b) -> p a b", b=shape[2])
        elif len(shape) == 4:
            view = view.rearrange("p (a b c) -> p a b c", b=shape[2], c=shape[3])
        t = TL(view, n)
        t.b.excl = True
        return t

    def ring(self, name, n, shape, dt, ps=False):
        f = self.ps if ps else self.sb
        return Ring([f(name + str(i), shape, dt) for i in range(n)])

    def dq(self):
        self.dq_i += 1
        return "sp" if self.dq_i % 2 else "pool"

    def load(self, dst_tl, dst_ap, src_tl, src_ap, q=None):
        self.S.dma(q or self.dq(), lambda e: e.dma_start(out=dst_ap, in_=src_ap),
                   reads=[src_tl], writes=[dst_tl], key=dst_tl)

    def store(self, dst_tl, dst_ap, src_tl, src_ap, q=None, slow=False):
        self.S.dma(q or self.dq(), lambda e: e.dma_start(out=dst_ap, in_=src_ap, allow_slow_non_contiguous=slow),
                   reads=[src_tl], writes=[dst_tl], key=src_tl)


class Seq:
    def __init__(self, name, S_len):
        self.name = name
        self.S = S_len
        self.L = S_len + 16
        self.NT = (self.L + 127) // 128
        self.Lp = self.NT * 128
        self.last = self.L - (self.NT - 1) * 128
        self.NB = (self.Lp + 511) // 512

    def rows(self, t):
        return self.last if t == self.NT - 1 else 128


def host_consts(seq):
    L, Lp, NT, NB = seq.L, seq.Lp, seq.NT, seq.NB
    S_len = seq.S
    c = {}
    rows = S_len // 64
    row = np.repeat(np.arange(rows), 64)
    col = np.arange(rows * 64) % 64
    meta = np.arange(16) - 16
    row = np.concatenate([meta, row]).astype(np.float32)
    col = np.concatenate([meta, col]).astype(np.float32)
    inv = (np.float32(10000.0) ** (-np.arange(0, 32, 2, dtype=np.float32) / np.float32(32))).astype(np.float32)
    ang = np.zeros((Lp, 2, 16), np.float32)
    ang[:L, 0] = row[:, None] * inv
    ang[:L, 1] = col[:, None] * inv
    ropeA = np.zeros((Lp, 2, 2, 16), np.float32)
    ropeA[:, 0] = np.cos(ang)
    ropeA[:, 1] = np.sin(ang)
    c["ropeA"] = ropeA
    inv1 = (np.float32(500000.0) ** (-np.arange(0, 16, 2, dtype=np.float32) / np.float32(16))).astype(np.float32)
    ang1 = np.zeros((Lp, 8), np.float32)
    ang1[:L] = np.arange(L, dtype=np.float32)[:, None] * inv1
    ropeC = np.zeros((Lp, 2, 8), np.float32)
    ropeC[:, 0] = np.cos(ang1)
    ropeC[:, 1] = np.sin(ang1)
    c["ropeC"] = ropeC
    p = np.arange(128)
    lv = (np.arange(NT)[None, :] * 128 + p[:, None]).astype(np.int64)
    c["lvals"] = lv.astype(np.float32)
    l0 = (np.arange(NB) * 512).astype(np.int64)
    sp = (lv[:, :, None] * l0[None, None, :]) % L
    c["dftsp"] = sp.reshape(128, NT * NB).astype(np.float32)
    return c


def global_consts():
    c = {}
    c["ident"] = np.eye(128, dtype=np.float32)
    c["jrev"] = np.ascontiguousarray(np.eye(128, dtype=np.float32)[::-1])
    c["iota512"] = np.tile(np.arange(512, dtype=np.float32)[None, :], (128, 1))
    k = np.arange(64)
    ang = 2 * np.pi * ((k[:, None] * k[None, :]) % 64) / 64.0
    c64 = (np.cos(ang) / 8.0).astype(np.float32)
    s64 = (np.sin(ang) / 8.0).astype(np.float32)
    c["cs64"] = np.concatenate([c64, c64, s64, s64], axis=1)
    i = np.arange(128)
    mk = np.zeros((3, 128, 128), np.float32)
    mk[0] = (i[:, None] >= i[None, :])
    mk[1] = (i[:, None] <= i[None, :])
    mk[2] = mk[0]; mk[2][:16, :] = 1.0
    c["cmask"] = np.ascontiguousarray(mk.transpose(1, 0, 2))
    j = np.arange(64)
    m4 = np.zeros((4, 64, 64), np.float32)
    m4[0] = (j[None, :] > j[:, None])
    m4[1] = (j[None, :] >= j[:, None])
    m4[2] = (j[None, :] < j[:, None])
    m4[3] = (j[None, :] <= j[:, None])
    c["m4"] = np.ascontiguousarray(m4.transpose(1, 0, 2))
    rs = np.ones((64, 512), np.float32); rs[:, 0::64] = 0.0
    c["reset"] = rs
    return c


class Prog:
    def __init__(self, seqs, debug=False):
        self.seqs = seqs
        self.debug = debug
        self.stop_after = 1
        nc = self.nc = bass.Bass("TRN2", target_bir_lowering=False)
        self.S = Sched(nc)
        self.dr = {}
        self.in_names = []
        self.out_names = []

    def din(self, name, shape, dt=F32):
        t = TL(self.nc.dram_tensor(name, list(shape), dt, kind="ExternalInput").ap(), name)
        self.dr[name] = t
        self.in_names.append(name)
        return t

    def dout(self, name, shape, dt=F32):
        t = TL(self.nc.dram_tensor(name, list(shape), dt, kind="ExternalOutput").ap(), name)
        self.dr[name] = t
        self.out_names.append(name)
        return t

    def dscr(self, name, shape, dt):
        if self.debug:
            return self.dout(name, shape, dt)
        t = TL(self.nc.dram_tensor(name, list(shape), dt).ap(), name)
        self.dr[name] = t
        return t

    def dbg(self, cx, name, tl, ap, shape, dt):
        if not self.debug or name in self.dr:
            return
        o = self.dout("dbg_" + name, shape, dt)
        cx.store(o, o[tuple(slice(None) for _ in shape)], tl, ap)

    def build(self):
        nc, S = self.nc, self.S
        W = self.W = {}
        wspec = dict(
            pre_mix_g=(2, D), post_mix_g=(2, D), pre_ffn_g=(2, D), post_ffn_g=(2, D),
            even_w_in=(1, D, 1536), even_w_out=(1, D, D), a_q_gain=(1, 64), a_k_gain=(1, 64),
            b_norm_g=(1, 4, 64), b_w=(1, 4, 64, 64), b_b=(1, 4, 64),
            odd_w_in=(1, D, 2688), odd_w_out=(1, D, D), c_sink=(1, 8),
            d_mu_prev=(1, 1920), d_mu_next=(1, 1920), d_w0=(1, 2, 512), d_w_up=(1, 2, 64, 512),
            d_a0=(1, 2, 512), d_a_up=(1, 2, 64, 512), d_g_up=(1, 128, 512), d_k_k=(1, 512),
            d_k_a=(1, 512), d_r_k=(1, 8, 64), d_ln_g=(1, 512), d_ln_b=(1, 512),
            ffn_w_gate=(2, D, 2816), ffn_w_up=(2, D, 2816), ffn_w_down=(2, 2816, D))
        for k, shp in wspec.items():
            W[k] = self.din(k, shp)
        G = self.G = {}
        for k, v in global_consts().items():
            G[k] = self.din("c_" + k, v.shape)
        maxLp = max(s.Lp for s in self.seqs)
        sc = self.sc = {}
        sc["QT"] = self.dscr("QT", [768, maxLp], BF16)
        sc["KT"] = self.dscr("KT", [256, maxLp], BF16)
        sc["VX"] = self.dscr("VX", [maxLp, 4, 65], BF16)
        sc["GG"] = self.dscr("GG", [maxLp, 512], BF16)
        sc["YA"] = self.dscr("YA", [maxLp, 768], BF16)
        sc["YB"] = self.dscr("YB", [maxLp, 256], BF16)
        sc["H1"] = self.dscr("H1", [maxLp, D], F32)
        sc["QCT"] = self.dscr("QCT", [8, 65, maxLp], BF16)
        sc["KCT"] = self.dscr("KCT", [2, 65, maxLp], BF16)
        sc["VCX"] = self.dscr("VCX", [maxLp, 2, 65], BF16)
        sc["BQ"] = self.dscr("BQ", [maxLp, 8], F32)
        sc["UT"] = self.dscr("UT", [1920, maxLp + 2], F32)
        sc["YC"] = self.dscr("YC", [maxLp, 512], BF16)
        sc["OF"] = self.dscr("OF", [maxLp, 512], F32)
        sc["YD"] = self.dscr("YD", [maxLp, 512], BF16)
        for s in self.seqs:
            s.h0 = self.din("h0_" + s.name, [s.Lp, D])
            s.y = self.dout("y_" + s.name, [s.S, D])
            s.c = {k: self.din("c_%s_%s" % (k, s.name), v.shape) for k, v in host_consts(s).items()}

        self.phase_consts()
        sk = getattr(self, "skip", ())
        for s in self.seqs:
            if "even" not in sk: self.phase_even_proj(s)
            if "attn_a" not in sk: self.phase_attn_a(s)
            if "dft" not in sk: self.phase_dft(s)
            if "mix0" not in sk: self.phase_mix_ffn(s, 0, s.h0, sc["H1"], ("YA", 768), ("YB", 256), W["even_w_out"], final=False)
            if self.stop_after == 0:
                continue
            if "odd" not in sk: self.phase_odd_proj(s)
            if "attn_c" not in sk: self.phase_attn_c(s)
            if "rwkv0" not in sk: self.phase_rwkv(s, 0)
            if "rwkv1" not in sk: self.phase_rwkv(s, 1)
            if "mix1" not in sk: self.phase_mix_ffn(s, 1, sc["H1"], s.y, ("YC", 512), ("YD", 512), W["odd_w_out"], final=True)
        with nc.allow_non_contiguous_dma(reason="small strided parameter / layout loads"):
            S.emit()
        return nc

    def phase_consts(self):
        nc, S = self.nc, self.S
        self.pes = contextlib.ExitStack()
        cx = self.pcx = Ctx(nc, S)
        cx.es = self.pes
        K = self.K = {}
        G = self.G
        K["identf"] = cx.sb("identf", [128, 128], F32)
        K["identb"] = cx.sb("identb", [128, 128], BF16)
        cx.load(K["identf"], K["identf"][:], G["ident"], G["ident"][:, :])
        S.op("dve", lambda e: e.tensor_copy(out=K["identb"][:], in_=K["identf"][:]), [K["identf"]], [K["identb"]])
        K["eps"] = cx.sb("eps", [128, 1], F32)
        S.op("dve", lambda e: e.memset(K["eps"][:], EPS), [], [K["eps"]])
        K["halfpi"] = cx.sb("halfpi", [128, 1], F32)
        S.op("dve", lambda e: e.memset(K["halfpi"][:], math.pi / 2), [], [K["halfpi"]])
        S.barrier()

    def load_w(self, cx, dst, src_tl, src_ap, nk, ncols, stage_ring, row0=0):
        S = self.S
        for k in range(nk):
            for c0 in range(0, ncols, 512):
                w = min(512, ncols - c0)
                st = stage_ring.next()
                cx.load(st, st[:, 0:w], src_tl, src_ap[row0 + k * 128: row0 + (k + 1) * 128, c0:c0 + w])
                self._lw = getattr(self, "_lw", 0) + 1
                if self._lw % 2:
                    S.op("dve", lambda e, st=st, k=k, c0=c0, w=w: e.tensor_copy(out=dst[:, k, c0:c0 + w], in_=st[:, 0:w]), [st], [dst])
                else:
                    S.op("act", lambda e, st=st, k=k, c0=c0, w=w: e.copy(out=dst[:, k, c0:c0 + w], in_=st[:, 0:w]), [st], [dst])

    def rstd(self, cx, ss_tl, ss_ap, out_tl, out_ap, tmp_tl, tmp_ap, scale, eps_ap):
        S = self.S
        S.op("act", lambda e: e.activation(out=tmp_ap, in_=ss_ap, func=AF.Sqrt, bias=eps_ap, scale=scale),
             [ss_tl], [tmp_tl])
        S.op("dve", lambda e: e.reciprocal(out=out_ap, in_=tmp_ap), [tmp_tl], [out_tl])

    def norm_transpose(self, cx, x_tl, gb_tl, hnT, TP, xs_ring, sm_ring, junk):
        S, K = self.S, self.K
        sm = sm_ring.next()
        S.op("act", lambda e: e.activation(out=junk[:], in_=x_tl[:], func=AF.Square, accum_out=sm[:, 0:1]),
             [x_tl], [junk, sm])
        self.rstd(cx, sm, sm[:, 0:1], sm, sm[:, 2:3], sm, sm[:, 1:2], 1.0 / D, K["eps"][:, 0:1])
        xs = xs_ring.next()
        S.op("dve", lambda e: e.scalar_tensor_tensor(out=xs[:], in0=x_tl[:], scalar=sm[:, 2:3], in1=gb_tl[:],
                                                     op0=ALU.mult, op1=ALU.mult), [x_tl, sm, gb_tl], [xs])
        for k in range(8):
            S.op("pe", lambda e, k=k: e.transpose(out=TP[:, k, :], in_=xs[:, k * 128:(k + 1) * 128],
                                                  identity=K["identb"][:]), [xs, K["identb"]], [TP])
        S.op("act", lambda e: e.copy(out=hnT[:], in_=TP[:]), [TP], [hnT])

    def phase_even_proj(self, seq):
        nc, S, W, K, sc = self.nc, self.S, self.W, self.K, self.sc
        with Ctx(nc, S) as cx:
            stage = cx.ring("wst", 4, [128, 512], F32)
            Win = cx.sb("win", [128, 8, 1536], BF16)
            self.load_w(cx, Win, W["even_w_in"], W["even_w_in"][0], 8, 1536, stage)
            gpre = cx.sb("gpre", [128, D], F32)
            cx.load(gpre, gpre[:], W["pre_mix_g"], W["pre_mix_g"][0:1, :].partition_broadcast(128))
            g16 = cx.sb("g16", [128, 16, 64], F32)
            cx.load(g16, g16[:, 0:12, :], W["a_q_gain"], W["a_q_gain"][0:1, :].partition_broadcast(128).broadcast_to([128, 12, 64]))
            cx.load(g16, g16[:, 12:16, :], W["a_k_gain"], W["a_k_gain"][0:1, :].partition_broadcast(128).broadcast_to([128, 4, 64]))
            S.op("dve", lambda e: e.tensor_scalar(out=g16[:, 0:12, :], in0=g16[:, 0:12, :], scalar1=0.125, scalar2=None,
                                                  op0=ALU.mult), [g16], [g16])
            bg = cx.sb("bg", [128, 256], F32)
            cx.load(bg, bg[:], W["b_norm_g"], W["b_norm_g"][0].rearrange("g c -> (g c)").partition_broadcast(128))
            cs64 = cx.sb("cs64", [64, 256], F32)
            cx.load(cs64, cs64[:], self.G["cs64"], self.G["cs64"][:, :])
            bw = cx.sb("bw", [64, 4, 64], F32)
            cx.load(bw, bw[:], W["b_w"], W["b_w"][0].rearrange("g c d -> c g d"))
            Wcs = cx.sb("wcs", [128, 2, 512], BF16)
            S.op("pool", lambda e: e.memset(Wcs[:], 0.0), [], [Wcs])
            pw = cx.ps("pw", [128, 512], F32)
            for g in range(4):
                for j in range(2):
                    S.op("pe", lambda e, g=g, j=j: e.matmul(pw[:, (g * 2 + j) * 64:(g * 2 + j + 1) * 64],
                                                             lhsT=cs64[:, j * 128:(j + 1) * 128], rhs=bw[:, g, :],
                                                             start=True, stop=True), [cs64, bw], [pw])
            for g in range(4):
                h0 = (g % 2) * 64
                for j in range(2):
                    S.op("dve", lambda e, g=g, j=j, h0=h0: e.tensor_copy(
                        out=Wcs[h0:h0 + 64, g // 2, j * 256 + g * 64: j * 256 + (g + 1) * 64],
                        in_=pw[h0:h0 + 64, (g * 2 + j) * 64:(g * 2 + j + 1) * 64]), [pw], [Wcs])

            self.dbg(cx, "win0", Win, Win[:, 0, :], [128, 1536], BF16)
            self.dbg(cx, "gpre", gpre, gpre[:], [128, D], F32)
            self.dbg(cx, "g16", g16, g16[:], [128, 16, 64], F32)
            self.dbg(cx, "wcs", Wcs, Wcs[:], [128, 2, 512], BF16)
            xring = cx.ring("x", 2, [128, D], F32)
            xs_ring = cx.ring("xs", 2, [128, D], BF16)
            sm_ring = cx.ring("sm", 2, [128, 4], F32)
            junk = cx.sb("junk", [128, D], BF16)
            TP = cx.ps("TP", [128, 8, 128], BF16)
            hnT_ring = cx.ring("hnT", 2, [128, 8, 128], BF16)
            PJ = [cx.ps("PJ%d" % c, [128, 512], F32) for c in range(3)]
            sqt = cx.sb("sqt", [128, 16, 64], F32)
            ssq_ring = cx.ring("ssq", 2, [128, 3, 16], F32)
            qk = cx.sb("qk", [128, 16, 64], F32)
            tA = cx.sb("tA", [128, 16, 2, 16], F32)
            tB = cx.sb("tB", [128, 16, 2, 16], F32)
            qkr_ring = cx.ring("qkr", 2, [128, 16, 64], BF16)
            rope_ring = cx.ring("rope", 2, [128, 2, 2, 16], F32)
            TP2 = cx.ps("TP2", [128, 8, 128], BF16)
            qkT_ring = cx.ring("qkT", 2, [128, 8, 128], BF16)
            vx_ring = cx.ring("vx", 2, [128, 4, 65], BF16)
            for vx in vx_ring.tiles:
                S.op("pool", lambda e, vx=vx: e.memset(vx[:], 1.0), [], [vx])
            sqf = cx.sb("sqf", [128, 4, 64], F32)
            fn32 = cx.sb("fn32", [128, 4, 64], F32)
            fn_ring = cx.ring("fn", 2, [128, 256], BF16)
            TP3 = cx.ps("TP3", [128, 2, 128], BF16)
            fnT_ring = cx.ring("fnT", 2, [128, 2, 128], BF16)
            PG = cx.ps("PG", [128, 512], F32)
            g_ring = cx.ring("gsb", 2, [128, 512], BF16)
            QTv = sc["QT"].t.rearrange("(c p) n -> p c n", p=128)
            KTv = sc["KT"].t.rearrange("(c p) n -> p c n", p=128)
            vmask = cx.sb("vmask", [128, 1], F32)
            S.op("dve", lambda e: e.memset(vmask[:], 0.0), [], [vmask])
            S.op("dve", lambda e: e.memset(vmask[0:seq.last, :], 1.0), [], [vmask])

            def body(t):
                c0 = t * 128
                xt = xring.next()
                cx.load(xt, xt[:], seq.h0, seq.h0[c0:c0 + 128, :])
                rp = rope_ring.next()
                cx.load(rp, rp[:], seq.c["ropeA"], seq.c["ropeA"][c0:c0 + 128])
                hnT = hnT_ring.next()
                self.norm_transpose(cx, xt, gpre, hnT, TP, xs_ring, sm_ring, junk)
                for c in range(3):
                    for k in range(8):
                        S.op("pe", lambda e, c=c, k=k: e.matmul(PJ[c][:], lhsT=hnT[:, k, :], rhs=Win[:, k, c * 512:(c + 1) * 512],
                                                                 start=(k == 0), stop=(k == 7)), [hnT, Win], [PJ[c]])
                if t == 0 and self.debug:
                    self.dbg(cx, "hnT", hnT, hnT[:], [128, 8, 128], BF16)
                    pjd = cx.sb("pjd", [128, 1536], F32)
                    for c in range(3):
                        S.op("dve", lambda e, c=c: e.tensor_copy(out=pjd[:, c * 512:(c + 1) * 512], in_=PJ[c][:]), [PJ[c]], [pjd])
                    self.dbg(cx, "pj", pjd, pjd[:], [128, 1536], F32)
                for c in range(2):
                    S.op("act", lambda e, c=c: e.activation(out=sqt[:, c * 8:(c + 1) * 8, :],
                                                            in_=PJ[c][:].rearrange("p (h d) -> p h d", d=64), func=AF.Square),
                         [PJ[c]], [sqt])
                sq = ssq_ring.next()
                S.op("dve", lambda e: e.tensor_reduce(out=sq[:, 0, :], in_=sqt[:], axis=AX.X, op=ALU.add), [sqt], [sq])
                self.rstd(cx, sq, sq[:, 0, :], sq, sq[:, 2, :], sq, sq[:, 1, :], 1.0 / 64, K["eps"][:, 0:1])
                for c in range(2):
                    S.op("dve", lambda e, c=c: e.tensor_tensor(
                        out=qk[:, c * 8:(c + 1) * 8, :], in0=PJ[c][:].rearrange("p (h d) -> p h d", d=64),
                        in1=sq[:, 2, c * 8:(c + 1) * 8].unsqueeze(2).broadcast_to([128, 8, 64]), op=ALU.mult),
                        [PJ[c], sq], [qk])
                S.op("dve", lambda e: e.tensor_tensor(out=qk[:], in0=qk[:], in1=g16[:], op=ALU.mult), [qk, g16], [qk])
                qkr = qkr_ring.next()
                q5 = qk[:].rearrange("p h (a c d) -> p h a c d", a=2, c=2)
                o5 = qkr[:].rearrange("p h (a c d) -> p h a c d", a=2, c=2)
                x1, x2 = q5[:, :, :, 0, :], q5[:, :, :, 1, :]
                cosb = rp[:, 0].unsqueeze(1).broadcast_to([128, 16, 2, 16])
                sinb = rp[:, 1].unsqueeze(1).broadcast_to([128, 16, 2, 16])
                S.op("dve", lambda e: e.tensor_tensor(out=tA[:], in0=x1, in1=cosb, op=ALU.mult), [qk, rp], [tA])
                S.op("dve", lambda e: e.tensor_tensor(out=tB[:], in0=x2, in1=sinb, op=ALU.mult), [qk, rp], [tB])
                S.op("dve", lambda e: e.tensor_tensor(out=o5[:, :, :, 0, :], in0=tA[:], in1=tB[:], op=ALU.subtract), [tA, tB], [qkr])
                S.op("dve", lambda e: e.tensor_tensor(out=tA[:], in0=x1, in1=sinb, op=ALU.mult), [qk, rp], [tA])
                S.op("dve", lambda e: e.tensor_tensor(out=tB[:], in0=x2, in1=cosb, op=ALU.mult), [qk, rp], [tB])
                S.op("dve", lambda e: e.tensor_tensor(out=o5[:, :, :, 1, :], in0=tA[:], in1=tB[:], op=ALU.add), [tA, tB], [qkr])
                qf = qkr[:].rearrange("p h d -> p (h d)")
                for c in range(8):
                    S.op("pe", lambda e, c=c: e.transpose(out=TP2[:, c, :], in_=qf[:, c * 128:(c + 1) * 128],
                                                          identity=K["identb"][:]), [qkr, K["identb"]], [TP2])
                qkT = qkT_ring.next()
                S.op("act", lambda e: e.copy(out=qkT[:], in_=TP2[:]), [TP2], [qkT])
                cx.store(sc["QT"], QTv[:, :, c0:c0 + 128], qkT, qkT[:, 0:6, :])
                cx.store(sc["KT"], KTv[:, :, c0:c0 + 128], qkT, qkT[:, 6:8, :])
                vx = vx_ring.next()
                S.op("act", lambda e: e.copy(out=vx[:, :, 0:64], in_=PJ[2][:, 0:256].rearrange("p (h d) -> p h d", d=64)),
                     [PJ[2]], [vx])
                if t == seq.NT - 1:
                    S.op("dve", lambda e: e.tensor_scalar(out=vx[:, :, 64], in0=vx[:, :, 64], scalar1=vmask[:, 0:1], scalar2=None, op0=ALU.mult),
                         [vx, vmask], [vx])
                cx.store(sc["VX"], sc["VX"][c0:c0 + 128, :, :], vx, vx[:])
                if t == seq.NT - 1:
                    S.op("dve", lambda e: e.memset(vx[:, :, 64], 1.0), [], [vx])
                fv = PJ[2][:, 256:512].rearrange("p (h d) -> p h d", d=64)
                S.op("act", lambda e: e.activation(out=sqf[:], in_=fv, func=AF.Square), [PJ[2]], [sqf])
                S.op("dve", lambda e: e.tensor_reduce(out=sq[:, 0, 0:4], in_=sqf[:], axis=AX.X, op=ALU.add), [sqf], [sq])
                self.rstd(cx, sq, sq[:, 0, 0:4], sq, sq[:, 2, 0:4], sq, sq[:, 1, 0:4], 1.0 / 64, K["eps"][:, 0:1])
                S.op("dve", lambda e: e.tensor_tensor(out=fn32[:], in0=fv, in1=sq[:, 2, 0:4].unsqueeze(2).broadcast_to([128, 4, 64]),
                                                      op=ALU.mult), [PJ[2], sq], [fn32])
                fn = fn_ring.next()
                S.op("dve", lambda e: e.tensor_tensor(out=fn[:], in0=fn32[:].rearrange("p h d -> p (h d)"), in1=bg[:], op=ALU.mult),
                     [fn32, bg], [fn])
                for c in range(2):
                    S.op("pe", lambda e, c=c: e.transpose(out=TP3[:, c, :], in_=fn[:, c * 128:(c + 1) * 128],
                                                          identity=K["identb"][:]), [fn, K["identb"]], [TP3])
                fnT = fnT_ring.next()
                S.op("act", lambda e: e.copy(out=fnT[:], in_=TP3[:]), [TP3], [fnT])
                for c in range(2):
                    S.op("pe", lambda e, c=c: e.matmul(PG[:], lhsT=fnT[:, c, :], rhs=Wcs[:, c, :], start=(c == 0), stop=(c == 1)),
                         [fnT, Wcs], [PG])
                gs = g_ring.next()
                S.op("act", lambda e: e.copy(out=gs[:], in_=PG[:]), [PG], [gs])
                cx.store(sc["GG"], sc["GG"][c0:c0 + 128, :], gs, gs[:])

            for t in range(seq.NT):
                body(t)

    def phase_attn_a(self, seq):
        nc, S, W, K, sc = self.nc, self.S, self.W, self.K, self.sc
        NT, Lp = seq.NT, seq.Lp
        with Ctx(nc, S) as cx:
            gq = cx.sb("gq", [128, 2, 64], F32)
            cx.load(gq, gq[:, 0, :], W["a_q_gain"], W["a_q_gain"][0:1, :].partition_broadcast(128))
            cx.load(gq, gq[:, 1, :], W["a_k_gain"], W["a_k_gain"][0:1, :].partition_broadcast(128))
            gm = cx.sb("gm", [128, 4], F32)
            S.op("dve", lambda e: e.tensor_reduce(out=gm[:, 0:2], in_=gq[:], axis=AX.X, op=ALU.max, apply_absolute_value=True),
                 [gq], [gm])
            S.op("dve", lambda e: e.tensor_tensor(out=gm[:, 2:3], in0=gm[:, 0:1], in1=gm[:, 1:2], op=ALU.mult), [gm], [gm])
            S.op("dve", lambda e: e.tensor_scalar(out=gm[:, 3:4], in0=gm[:, 2:3], scalar1=-8.0, scalar2=None, op0=ALU.mult), [gm], [gm])
            KTs = cx.sb("KTs", [128, Lp], BF16)
            VXs = cx.sb("VXs", [128, NT, 65], BF16)
            q_ring = cx.ring("qb", 2, [128, 512], BF16)
            ST = cx.ring("ST", 4, [128, 512], F32, ps=True)
            OTa = cx.ring("OTa", 1, [65, 512], F32, ps=True)
            OTb = cx.ring("OTb", 1, [65, 512], F32, ps=True)
            PT = cx.ring("PT", 5, [128, 512], BF16)
            osb_ring = cx.ring("osb", 2, [65, 512], F32)
            TPo = cx.ps("TPo", [128, 4, 65], F32)
            rec_ring = cx.ring("rec", 2, [128, 4, 1], F32)
            y_ring = cx.ring("ya", 2, [128, 4, 64], BF16)
            YAv = sc["YA"].t.rearrange("(s p) c -> p s c", p=128)

            def qblock(g, h, qb):
                n0 = qb * 512
                wd = min(512, Lp - n0)
                nsub = wd // 128
                qt = q_ring.next()
                for hf in range(2):
                    cx.load(qt, qt[hf * 64:(hf + 1) * 64, 0:wd], sc["QT"], sc["QT"][h * 64:(h + 1) * 64, n0:n0 + wd])
                ota = OTa.next()
                otb = OTb.next()
                sts = {}

                def emitS(kt):
                    st = ST.next()
                    sts[kt] = st
                    hf = slice((kt % 2) * 64, (kt % 2) * 64 + 64)
                    S.op("pe", lambda e, st=st, kt=kt, hf=hf: e.matmul(st[:, 0:wd], lhsT=KTs[hf, kt * 128:(kt + 1) * 128],
                                                                        rhs=qt[hf, 0:wd], start=True, stop=True), [KTs, qt], [st], lane=kt % 2)
                LOOK = 3
                for kt in range(min(LOOK, NT)):
                    emitS(kt)
                for kt in range(NT):
                    if kt + LOOK < NT:
                        emitS(kt + LOOK)
                    st = sts.pop(kt)
                    pt = PT.next()
                    S.op("act", lambda e, st=st, pt=pt: e.activation(out=pt[:, 0:wd], in_=st[:, 0:wd], func=AF.Exp,
                                                                      bias=gm[:, 3:4], scale=1.0), [st, gm], [pt])
                    S.op("pe", lambda e, pt=pt, kt=kt: e.matmul(ota[:, 0:wd], lhsT=VXs[0:64, kt, :], rhs=pt[0:64, 0:wd],
                                                                 start=(kt == 0), stop=(kt == NT - 1)), [VXs, pt], [ota])
                    S.op("pe", lambda e, pt=pt, kt=kt: e.matmul(otb[:, 0:wd], lhsT=VXs[64:128, kt, :], rhs=pt[64:128, 0:wd],
                                                                 start=(kt == 0), stop=(kt == NT - 1)), [VXs, pt], [otb], lane=1)
                osb = osb_ring.next()
                S.op("dve", lambda e: e.tensor_copy(out=osb[:, 0:wd], in_=ota[:, 0:wd]), [ota], [osb])
                S.op("dve", lambda e: e.tensor_tensor(out=osb[:, 0:wd], in0=otb[:, 0:wd], in1=osb[:, 0:wd], op=ALU.add), [otb, osb], [osb])
                for s_ in range(nsub):
                    S.op("pe", lambda e, s_=s_: e.transpose(out=TPo[:, s_, :], in_=osb[:, s_ * 128:(s_ + 1) * 128],
                                                            identity=K["identf"][0:65, 0:65]), [osb, K["identf"]], [TPo])
                rec = rec_ring.next()
                S.op("dve", lambda e: e.reciprocal(out=rec[:, 0:nsub, :], in_=TPo[:, 0:nsub, 64:65]), [TPo], [rec])
                ya = y_ring.next()
                S.op("dve", lambda e: e.tensor_tensor(out=ya[:, 0:nsub, :], in0=TPo[:, 0:nsub, 0:64],
                                                      in1=rec[:, 0:nsub, :].broadcast_to([128, nsub, 64]), op=ALU.mult), [TPo, rec], [ya])
                cx.store(sc["YA"], YAv[:, qb * 4:qb * 4 + nsub, h * 64:(h + 1) * 64], ya, ya[:, 0:nsub, :])

            for g in range(4):
                for hf in range(2):
                    cx.load(KTs, KTs[hf * 64:(hf + 1) * 64, :], sc["KT"], sc["KT"][g * 64:(g + 1) * 64, 0:Lp])
                for t0 in range(0, NT, 16):
                    t1_ = min(NT, t0 + 16)
                    cx.load(VXs, VXs[:, t0:t1_, :], sc["VX"], sc["VX"][t0 * 128:t1_ * 128, g, :].rearrange("(t p) d -> p t d", p=128))
                for j in range(3):
                    for qb in range(seq.NB):
                        qblock(g, g * 3 + j, qb)

    def phase_dft(self, seq):
        nc, S, W, K, sc = self.nc, self.S, self.W, self.K, self.sc
        NT, Lp, L, NB = seq.NT, seq.Lp, seq.L, seq.NB
        with Ctx(nc, S) as cx:
            GGs = cx.sb("GGs", [128, NT, 512], BF16)
            for t0 in range(0, NT, 16):
                t1 = min(NT, t0 + 16)
                cx.load(GGs, GGs[:, t0:t1, :], sc["GG"], sc["GG"][t0 * 128:t1 * 128, :].rearrange("(t p) c -> p t c", p=128))
            iota = cx.sb("iota", [128, 512], F32)
            cx.load(iota, iota[:], self.G["iota512"], self.G["iota512"][:, :])
            lv = cx.sb("lv", [128, NT], F32)
            cx.load(lv, lv[:], seq.c["lvals"], seq.c["lvals"][:, :])
            sp = cx.sb("sp", [128, NT * NB], F32)
            cx.load(sp, sp[:], seq.c["dftsp"], seq.c["dftsp"][:, :])
            bb = cx.sb("bb", [128, 256], F32)
            cx.load(bb, bb[:], W["b_b"], W["b_b"][0].rearrange("g c -> (g c)").partition_broadcast(128))
            jrev = cx.sb("jrev", [128, 128], F32)
            cx.load(jrev, jrev[:], self.G["jrev"], self.G["jrev"][:, :])
            if Lp > L:
                zt = cx.sb("zt", [128, 256], BF16)
                S.op("dve", lambda e: e.memset(zt[:], 0.0), [], [zt])
                cx.store(sc["YB"], sc["YB"][L:Lp, :], zt, zt[0:Lp - L, :])
            t1r = cx.ring("t1", 3, [128, 512], F32)
            qr = cx.ring("q", 3, [128, 512], I32)
            rr = cx.ring("r", 3, [128, 512], F32)
            rar = cx.ring("ra", 3, [128, 512], F32)
            Cr = cx.ring("C", 3, [128, 512], BF16)
            Sr = cx.ring("Sn", 3, [128, 512], BF16)
            ACCA = [cx.ps("accA%d" % i, [128, 256], F32) for i in range(4)]
            ACCB = [cx.ps("accB%d" % i, [128, 256], F32) for i in range(4)]
            yb_ring = cx.ring("yb", 4, [128, 256], BF16)
            bs_ring = cx.ring("bsb", 2, [128, 256], F32)
            ys_ring = cx.ring("ysum", 2, [128, 2, 256], F32)
            w0 = 2 * math.pi / L * (1.0 - 2e-6)
            isq = 1.0 / math.sqrt(L)
            Hh = L // 2
            NBh = (Hh + 1 + 511) // 512

            def blk(B):
                n0 = B * 512
                wd = min(512, ((Hh + 1 + 127) // 128) * 128 - n0)
                nsub = wd // 128
                for lt in range(NT):
                    r = seq.rows(lt)
                    t1 = t1r.next(); q = qr.next(); rt = rr.next(); ra = rar.next(); C = Cr.next(); Sn = Sr.next()
                    idx = lt * NB + B
                    S.op("dve", lambda e, t1=t1, lt=lt, idx=idx: e.tensor_scalar(out=t1[:, 0:wd], in0=iota[:, 0:wd], scalar1=lv[:, lt:lt + 1],
                                                                                  scalar2=sp[:, idx:idx + 1], op0=ALU.mult, op1=ALU.add),
                         [iota, lv, sp], [t1])
                    S.op("dve", lambda e, t1=t1, q=q: e.tensor_scalar(out=q[:, 0:wd], in0=t1[:, 0:wd], scalar1=1.0 / L, scalar2=None,
                                                                       op0=ALU.mult), [t1], [q])
                    S.op("dve", lambda e, t1=t1, q=q, rt=rt: e.scalar_tensor_tensor(out=rt[:, 0:wd], in0=q[:, 0:wd], scalar=-float(L),
                                                                                    in1=t1[:, 0:wd], op0=ALU.mult, op1=ALU.add),
                         [q, t1], [rt])
                    S.op("dve", lambda e, rt=rt, ra=ra: e.scalar_tensor_tensor(out=ra[:, 0:wd], in0=rt[:, 0:wd], scalar=-1.0, in1=rt[:, 0:wd], op0=ALU.mult, op1=ALU.max),
                         [rt], [ra])
                    S.op("act", lambda e, rt=rt, Sn=Sn: e.activation(out=Sn[:, 0:wd], in_=rt[:, 0:wd], func=AF.Sin, scale=-w0), [rt], [Sn])
                    S.op("act", lambda e, ra=ra, C=C: e.activation(out=C[:, 0:wd], in_=ra[:, 0:wd], func=AF.Sin, bias=K["halfpi"][:, 0:1],
                                                                    scale=-w0), [ra, K["halfpi"]], [C])
                    for s_ in range(nsub):
                        S.op("pe", lambda e, s_=s_, C=C, lt=lt, r=r: e.matmul(ACCA[s_][:], lhsT=C[0:r, s_ * 128:(s_ + 1) * 128],
                                                                               rhs=GGs[0:r, lt, 0:256], start=(lt == 0), stop=(lt == NT - 1)),
                             [C, GGs], [ACCA[s_]])
                        S.op("pe", lambda e, s_=s_, Sn=Sn, lt=lt, r=r: e.matmul(ACCB[s_][:], lhsT=Sn[0:r, s_ * 128:(s_ + 1) * 128],
                                                                                 rhs=GGs[0:r, lt, 256:512], start=(lt == 0), stop=(lt == NT - 1)),
                             [Sn, GGs], [ACCB[s_]])
                for s_ in range(nsub):
                    l0 = n0 + s_ * 128
                    nv = min(128, Hh + 1 - l0)
                    if nv <= 0:
                        continue
                    bs = bs_ring.next()
                    S.op("act", lambda e, s_=s_, bs=bs: e.copy(out=bs[:], in_=ACCB[s_][:]), [ACCB[s_]], [bs])
                    ys = ys_ring.next()
                    S.op("dve", lambda e, s_=s_, bs=bs, ys=ys: e.tensor_tensor(out=ys[:, 0, :], in0=ACCA[s_][:], in1=bs[:], op=ALU.add), [ACCA[s_], bs], [ys])
                    S.op("dve", lambda e, s_=s_, bs=bs, ys=ys: e.tensor_tensor(out=ys[:, 1, :], in0=ACCA[s_][:], in1=bs[:], op=ALU.subtract), [ACCA[s_], bs], [ys])
                    y1 = yb_ring.next()
                    y2 = yb_ring.next()
                    S.op("dve", lambda e, ys=ys, y1=y1: e.scalar_tensor_tensor(out=y1[:], in0=ys[:, 0, :], scalar=isq, in1=bb[:], op0=ALU.mult, op1=ALU.add),
                         [ys, bb], [y1])
                    cx.store(sc["YB"], sc["YB"][l0:l0 + nv, :], y1, y1[0:nv, :])
                    S.op("pe", lambda e, s_=s_, ys=ys: e.matmul(ACCB[s_][:], lhsT=jrev[:], rhs=ys[:, 1, :], start=True, stop=True), [jrev, ys], [ACCB[s_]])
                    S.op("dve", lambda e, s_=s_, y2=y2: e.scalar_tensor_tensor(out=y2[:], in0=ACCB[s_][:], scalar=isq, in1=bb[:], op0=ALU.mult, op1=ALU.add),
                         [ACCB[s_], bb], [y2])
                    p0 = 1 if l0 == 0 else 0
                    base = L - l0 - 127
                    m_lo, m_hi = 128 - nv, 128 - p0
                    cx.store(sc["YB"], sc["YB"][base + m_lo:base + m_hi, :], y2, y2[m_lo:m_hi, :])

            for B in range(NBh):
                blk(B)

    def post_norm_res(self, cx, PM, gpost, hin, hout, sm_ring, junk):
        S, K = self.S, self.K
        sm = sm_ring.next()
        for c in range(2):
            S.op("act", lambda e, c=c: e.activation(out=junk[:, 0:512], in_=PM[c][:], func=AF.Square, accum_out=sm[:, c:c + 1]),
                 [PM[c]], [junk, sm])
        S.op("dve", lambda e: e.tensor_tensor(out=sm[:, 0:1], in0=sm[:, 0:1], in1=sm[:, 1:2], op=ALU.add), [sm], [sm])
        self.rstd(cx, sm, sm[:, 0:1], sm, sm[:, 2:3], sm, sm[:, 3:4], 1.0 / D, K["eps"][:, 0:1])
        for c in range(2):
            S.op("dve", lambda e, c=c: e.scalar_tensor_tensor(out=hout[:, c * 512:(c + 1) * 512], in0=PM[c][:], scalar=sm[:, 2:3],
                                                              in1=gpost[:, c * 512:(c + 1) * 512], op0=ALU.mult, op1=ALU.mult),
                 [PM[c], sm, gpost], [hout])
        S.op("dve", lambda e: e.tensor_tensor(out=hout[:], in0=hout[:], in1=hin[:], op=ALU.add), [hout, hin], [hout])

    def phase_mix_ffn(self, seq, layer, hin_d, hout_d, ya, yb, wout, final):
        nc, S, W, K, sc = self.nc, self.S, self.W, self.K, self.sc
        NT, Lp = seq.NT, seq.Lp
        with Ctx(nc, S) as cx:
            stage = cx.ring("wst", 4, [128, 512], F32)
            Wo = cx.sb("wo", [128, 8, D], BF16)
            self.load_w(cx, Wo, wout, wout[0], 8, D, stage)
            Wg = cx.sb("wg", [128, 8, 2816], BF16)
            self.load_w(cx, Wg, W["ffn_w_gate"], W["ffn_w_gate"][layer], 8, 2816, stage)
            Wu = cx.sb("wu", [128, 8, 2816], BF16)
            self.load_w(cx, Wu, W["ffn_w_up"], W["ffn_w_up"][layer], 8, 2816, stage)
            Wd = cx.sb("wd", [128, 22, D], BF16)
            self.load_w(cx, Wd, W["ffn_w_down"], W["ffn_w_down"][layer], 22, D, stage)
            gs = {}
            for nm in ("post_mix_g", "pre_ffn_g", "post_ffn_g"):
                gs[nm] = cx.sb(nm, [128, D], F32)
                cx.load(gs[nm], gs[nm][:], W[nm], W[nm][layer:layer + 1, :].partition_broadcast(128))
            y_ring = cx.ring("y", 2, [128, D], BF16)
            h_ring = cx.ring("h", 1, [128, D], F32)
            TP = cx.ps("TP", [128, 8, 128], BF16)
            yT_ring = cx.ring("yT", 1, [128, 8, 128], BF16)
            PM = [cx.ps("PM%d" % c, [128, 512], F32) for c in range(2)]
            sm_ring = cx.ring("sm", 4, [128, 4], F32)
            junk = cx.sb("junk", [128, D], BF16)
            h1_ring = cx.ring("h1", 1, [128, D], F32)
            xs_ring = cx.ring("xs", 1, [128, D], BF16)
            hT_ring = cx.ring("hT", 1, [128, 8, 128], BF16)
            PGt = cx.ring("PGt", 2, [128, 4, 128], F32, ps=True)
            PUt = cx.ring("PUt", 2, [128, 4, 128], F32, ps=True)
            sg_ring = cx.ring("sg", 2, [128, 4, 128], F32)
            AT_ring = cx.ring("AT", 1, [128, 22, 128], BF16)
            h2_ring = cx.ring("h2", 1, [128, D], F32)
            vmask = cx.sb("vmask", [128, 1], F32)
            S.op("dve", lambda e: e.memset(vmask[:], 0.0), [], [vmask])
            S.op("dve", lambda e: e.memset(vmask[0:seq.last, :], 1.0), [], [vmask])
            def body(t):
                c0 = t * 128
                y = y_ring.next()
                cx.load(y, y[:, 0:ya[1]], sc[ya[0]], sc[ya[0]][c0:c0 + 128, :])
                cx.load(y, y[:, ya[1]:D], sc[yb[0]], sc[yb[0]][c0:c0 + 128, :])
                h = h_ring.next()
                cx.load(h, h[:], hin_d, hin_d[c0:c0 + 128, :])
                for k in range(8):
                    S.op("pe", lambda e, k=k: e.transpose(out=TP[:, k, :], in_=y[:, k * 128:(k + 1) * 128], identity=K["identb"][:]),
                         [y, K["identb"]], [TP])
                yT = yT_ring.next()
                S.op("act", lambda e: e.copy(out=yT[:], in_=TP[:]), [TP], [yT])
                for c in range(2):
                    for k in range(8):
                        S.op("pe", lambda e, c=c, k=k: e.matmul(PM[c][:], lhsT=yT[:, k, :], rhs=Wo[:, k, c * 512:(c + 1) * 512],
                                                                 start=(k == 0), stop=(k == 7)), [yT, Wo], [PM[c]])
                h1 = h1_ring.next()
                self.post_norm_res(cx, PM, gs["post_mix_g"], h, h1, sm_ring, junk)
                hT = hT_ring.next()
                self.norm_transpose(cx, h1, gs["pre_ffn_g"], hT, TP, xs_ring, sm_ring, junk)
                AT = AT_ring.next()
                for fq in range(6):
                    nf = min(4, 22 - fq * 4)
                    pg = PGt.next(); pu = PUt.next()
                    for (pp, Wm) in ((pg, Wg), (pu, Wu)):
                        for f in range(nf):
                            fc = (fq * 4 + f) * 128
                            for k in range(8):
                                S.op("pe", lambda e, pp=pp, Wm=Wm, f=f, fc=fc, k=k: e.matmul(pp[:, f, :], lhsT=Wm[:, k, fc:fc + 128], rhs=hT[:, k, :],
                                                                                              start=(k == 0), stop=(k == 7)), [Wm, hT], [pp])
                    sg = sg_ring.next()
                    S.op("act", lambda e, pg=pg, sg=sg, nf=nf: e.activation(out=sg[:, 0:nf, :], in_=pg[:, 0:nf, :], func=AF.Silu), [pg], [sg])
                    S.op("dve", lambda e, pu=pu, sg=sg, nf=nf, fq=fq: e.tensor_tensor(out=AT[:, fq * 4:fq * 4 + nf, :], in0=sg[:, 0:nf, :],
                                                                                      in1=pu[:, 0:nf, :], op=ALU.mult), [sg, pu], [AT])
                for c in range(2):
                    for f in range(22):
                        S.op("pe", lambda e, c=c, f=f: e.matmul(PM[c][:], lhsT=AT[:, f, :], rhs=Wd[:, f, c * 512:(c + 1) * 512],
                                                                 start=(f == 0), stop=(f == 21)), [AT, Wd], [PM[c]])
                h2 = h2_ring.next()
                self.post_norm_res(cx, PM, gs["post_ffn_g"], h1, h2, sm_ring, junk)
                if not final:
                    if t == NT - 1:
                        S.op("dve", lambda e, h2=h2: e.tensor_scalar(out=h2[:], in0=h2[:], scalar1=vmask[:, 0:1], scalar2=None, op0=ALU.mult),
                             [h2, vmask], [h2])
                    cx.store(hout_d, hout_d[c0:c0 + 128, :], h2, h2[:])
                else:
                    lo = 16 if t == 0 else 0
                    hi = seq.rows(t)
                    cx.store(hout_d, hout_d[c0 + lo - 16:c0 + hi - 16, :], h2, h2[lo:hi, :])

            for t in range(NT):
                body(t)


    def phase_odd_proj(self, seq):
        nc, S, W, K, sc = self.nc, self.S, self.W, self.K, self.sc
        NT, Lp = seq.NT, seq.Lp
        with Ctx(nc, S) as cx:
            stage = cx.ring("wst", 4, [128, 512], F32)
            Win = cx.sb("win", [128, 8, 2688], BF16)
            self.load_w(cx, Win, W["odd_w_in"], W["odd_w_in"][0], 8, 2688, stage)
            gpre = cx.sb("gpre", [128, D], F32)
            cx.load(gpre, gpre[:], W["pre_mix_g"], W["pre_mix_g"][1:2, :].partition_broadcast(128))
            zc = cx.sb("zc", [128, 15], F32)
            S.op("dve", lambda e: e.memset(zc[:], 0.0), [], [zc])
            UTv = sc["UT"].t.rearrange("(b p) n -> p b n", p=128)
            cx.store(sc["UT"], UTv[:, :, 0:1], zc, zc[:].unsqueeze(2), slow=True)
            cx.store(sc["UT"], UTv[:, :, Lp + 1:Lp + 2], zc, zc[:].unsqueeze(2), slow=True)
            xring = cx.ring("x", 2, [128, D], F32)
            xs_ring = cx.ring("xs", 2, [128, D], BF16)
            sm_ring = cx.ring("sm", 2, [128, 4], F32)
            junk = cx.sb("junk", [128, D], BF16)
            TP = cx.ps("TP", [128, 8, 128], BF16)
            hnT_ring = cx.ring("hnT", 2, [128, 8, 128], BF16)
            PJ = [cx.ps("PJ%d" % c, [128, 512], F32) for c in range(2)]
            PU = cx.ring("PU", 2, [128, 4, 128], F32, ps=True)
            rope_ring = cx.ring("rope", 2, [128, 2, 8], F32)
            sqt = cx.sb("sqt", [128, 8, 64], F32)
            bq_ring = cx.ring("bq", 2, [128, 2, 8], F32)
            qc = cx.sb("qc", [128, 10, 64], F32)
            tA = cx.sb("tA", [128, 10, 8], F32)
            tB = cx.sb("tB", [128, 10, 8], F32)
            qa_ring = cx.ring("qa", 2, [128, 10, 65], BF16)
            for qa in qa_ring.tiles:
                S.op("pool", lambda e, qa=qa: e.memset(qa[:], 1.0), [], [qa])
            vc_ring = cx.ring("vc", 2, [128, 2, 65], BF16)
            for vc in vc_ring.tiles:
                S.op("pool", lambda e, vc=vc: e.memset(vc[:], 1.0), [], [vc])
            TP2 = cx.ps("TP2", [128, 5, 128], BF16)
            TP3 = cx.ps("TP3", [128, 5, 128], BF16)
            qT_ring = cx.ring("qT", 2, [65, 10, 128], BF16)
            u_ring = cx.ring("usb", 2, [128, 15, 128], F32)

            def body(t):
                c0 = t * 128
                xt = xring.next()
                cx.load(xt, xt[:], sc["H1"], sc["H1"][c0:c0 + 128, :])
                rp = rope_ring.next()
                cx.load(rp, rp[:], seq.c["ropeC"], seq.c["ropeC"][c0:c0 + 128])
                hnT = hnT_ring.next()
                self.norm_transpose(cx, xt, gpre, hnT, TP, xs_ring, sm_ring, junk)
                for c, (lo, wdt) in enumerate(((0, 512), (512, 256))):
                    for k in range(8):
                        S.op("pe", lambda e, c=c, k=k, lo=lo, wdt=wdt: e.matmul(PJ[c][:, 0:wdt], lhsT=hnT[:, k, :], rhs=Win[:, k, lo:lo + wdt],
                                                                                 start=(k == 0), stop=(k == 7)), [hnT, Win], [PJ[c]])
                usb = u_ring.next()
                for q4 in range(4):
                    nb = min(4, 15 - q4 * 4)
                    pu = PU.next()
                    for b in range(nb):
                        f0 = 768 + (q4 * 4 + b) * 128
                        for k in range(8):
                            S.op("pe", lambda e, pu=pu, b=b, f0=f0, k=k: e.matmul(pu[:, b, :], lhsT=Win[:, k, f0:f0 + 128], rhs=hnT[:, k, :],
                                                                                  start=(k == 0), stop=(k == 7)), [Win, hnT], [pu])
                    S.op("act" if q4 % 2 else "dve", (lambda e, pu=pu, q4=q4, nb=nb: e.copy(out=usb[:, q4 * 4:q4 * 4 + nb, :], in_=pu[:, 0:nb, :])) if q4 % 2 else
                         (lambda e, pu=pu, q4=q4, nb=nb: e.tensor_copy(out=usb[:, q4 * 4:q4 * 4 + nb, :], in_=pu[:, 0:nb, :])), [pu], [usb])
                cx.store(sc["UT"], UTv[:, :, 1 + c0:1 + c0 + 128], usb, usb[:])
                S.op("act", lambda e: e.activation(out=sqt[:], in_=PJ[0][:].rearrange("p (h d) -> p h d", d=64), func=AF.Square), [PJ[0]], [sqt])
                bq = bq_ring.next()
                S.op("dve", lambda e: e.tensor_reduce(out=bq[:, 0, :], in_=sqt[:], axis=AX.X, op=ALU.add), [sqt], [bq])
                S.op("dve", lambda e: e.tensor_scalar(out=bq[:, 1, :], in0=bq[:, 0, :], scalar1=1.0 / 16, scalar2=None, op0=ALU.mult), [bq], [bq])
                cx.store(sc["BQ"], sc["BQ"][c0:c0 + 128, :], bq, bq[:, 1, :])
                S.op("act", lambda e: e.activation(out=qc[:, 0:8, :], in_=PJ[0][:].rearrange("p (h d) -> p h d", d=64), func=AF.Copy, scale=0.125), [PJ[0]], [qc])
                S.op("act", lambda e: e.copy(out=qc[:, 8:10, :], in_=PJ[1][:, 0:128].rearrange("p (h d) -> p h d", d=64)), [PJ[1]], [qc])
                qa = qa_ring.next()
                x1, x2 = qc[:, :, 0:8], qc[:, :, 8:16]
                cosb = rp[:, 0, :].unsqueeze(1).broadcast_to([128, 10, 8])
                sinb = rp[:, 1, :].unsqueeze(1).broadcast_to([128, 10, 8])
                S.op("dve", lambda e: e.tensor_tensor(out=tA[:], in0=x1, in1=cosb, op=ALU.mult), [qc, rp], [tA])
                S.op("dve", lambda e: e.tensor_tensor(out=tB[:], in0=x2, in1=sinb, op=ALU.mult), [qc, rp], [tB])
                S.op("dve", lambda e: e.tensor_tensor(out=qa[:, :, 0:8], in0=tA[:], in1=tB[:], op=ALU.subtract), [tA, tB], [qa])
                S.op("dve", lambda e: e.tensor_tensor(out=tA[:], in0=x1, in1=sinb, op=ALU.mult), [qc, rp], [tA])
                S.op("dve", lambda e: e.tensor_tensor(out=tB[:], in0=x2, in1=cosb, op=ALU.mult), [qc, rp], [tB])
                S.op("dve", lambda e: e.tensor_tensor(out=qa[:, :, 8:16], in0=tA[:], in1=tB[:], op=ALU.add), [tA, tB], [qa])
                S.op("act", lambda e: e.copy(out=qa[:, :, 16:64], in_=qc[:, :, 16:64]), [qc], [qa])
                S.op("dve", lambda e: e.tensor_scalar(out=qa[:, 0:8, 64], in0=bq[:, 1, :], scalar1=-1.0, scalar2=None, op0=ALU.mult), [bq], [qa])
                for hh in range(10):
                    tp = TP2 if hh < 5 else TP3
                    S.op("pe", lambda e, hh=hh, tp=tp: e.transpose(out=tp[0:65, hh % 5, :], in_=qa[:, hh, :], identity=K["identb"][:]),
                         [qa, K["identb"]], [tp])
                qT = qT_ring.next()
                S.op("act", lambda e: e.copy(out=qT[:, 0:5, :], in_=TP2[0:65, :, :]), [TP2], [qT])
                S.op("dve", lambda e: e.tensor_copy(out=qT[:, 5:10, :], in_=TP3[0:65, :, :]), [TP3], [qT])
                cx.store(sc["QCT"], sc["QCT"][:, :, c0:c0 + 128].rearrange("h d n -> d h n"), qT, qT[:, 0:8, :])
                cx.store(sc["KCT"], sc["KCT"][:, :, c0:c0 + 128].rearrange("h d n -> d h n"), qT, qT[:, 8:10, :])
                vc = vc_ring.next()
                S.op("act", lambda e: e.copy(out=vc[:, :, 0:64], in_=PJ[1][:, 128:256].rearrange("p (h d) -> p h d", d=64)), [PJ[1]], [vc])
                cx.store(sc["VCX"], sc["VCX"][c0:c0 + 128, :, :], vc, vc[:])

            for t in range(NT):
                body(t)

    def phase_attn_c(self, seq):
        nc, S, W, K, sc = self.nc, self.S, self.W, self.K, self.sc
        NT, Lp = seq.NT, seq.Lp
        with Ctx(nc, S) as cx:
            KCs = cx.sb("KCs", [65, 2, Lp], BF16)
            cx.load(KCs, KCs[:], sc["KCT"], sc["KCT"][:, :, 0:Lp].rearrange("h d n -> d h n"))
            VCs = cx.sb("VCs", [128, NT, 2, 65], BF16)
            for t0 in range(0, NT, 16):
                t1_ = min(NT, t0 + 16)
                cx.load(VCs, VCs[:, t0:t1_], sc["VCX"], sc["VCX"][t0 * 128:t1_ * 128].rearrange("(t p) h d -> p t h d", p=128))
            cm = cx.sb("cm", [128, 3, 128], F32)
            cx.load(cm, cm[:], self.G["cmask"], self.G["cmask"][:, :, :])
            cmb = cx.sb("cmb", [128, 3, 128], BF16)
            S.op("dve", lambda e: e.tensor_copy(out=cmb[:], in_=cm[:]), [cm], [cmb])
            snk = cx.sb("snk", [128, 8], F32)
            cx.load(snk, snk[:], W["c_sink"], W["c_sink"][0:1, :].partition_broadcast(128))
            q_ring = cx.ring("qc", 2, [65, 8, 128], BF16)
            bq_ring = cx.ring("bq", 2, [128, 8], F32)
            es_ring = cx.ring("es", 2, [128, 8], F32)
            ST = cx.ring("ST", 2, [128, 4, 128], F32, ps=True)
            OT = cx.ring("OT", 2, [65, 4, 128], F32, ps=True)
            PT = cx.ring("PT", 3, [128, 4, 128], BF16)
            osb_ring = cx.ring("osb", 2, [65, 4, 128], F32)
            TPo = cx.ps("TPo", [128, 4, 65], F32)
            den_ring = cx.ring("den", 2, [128, 2, 4], F32)
            y_ring = cx.ring("yc", 2, [128, 8, 64], BF16)

            def body(t):
                c0 = t * 128
                qt = q_ring.next()
                cx.load(qt, qt[:], sc["QCT"], sc["QCT"][:, :, c0:c0 + 128].rearrange("h d n -> d h n"))
                bq = bq_ring.next()
                cx.load(bq, bq[:], sc["BQ"], sc["BQ"][c0:c0 + 128, :])
                es = es_ring.next()
                S.op("dve", lambda e: e.tensor_tensor(out=es[:], in0=snk[:], in1=bq[:], op=ALU.subtract), [snk, bq], [es])
                S.op("act", lambda e: e.activation(out=es[:], in_=es[:], func=AF.Exp), [es], [es])
                yc = y_ring.next()
                kbs = []
                if t >= 2:
                    kbs.append((0, 16, None))
                if t >= 1:
                    kbs.append((t - 1, 128, 2 if t == 1 else 0))
                kbs.append((t, seq.rows(t), None))
                if t + 1 < NT:
                    kbs.append((t + 1, seq.rows(t + 1), 1))
                def grp(g):
                    ot = OT.next()
                    for bi, (kt, r, mk) in enumerate(kbs):
                        st = ST.next()
                        S.op("pe", lambda e, st=st, kt=kt, r=r, g=g: e.matmul(st[0:r], lhsT=KCs[:, g, kt * 128:kt * 128 + r], rhs=qt[:, g * 4:(g + 1) * 4, :],
                                                                               start=True, stop=True), [KCs, qt], [st])
                        pt = PT.next()
                        S.op("act", lambda e, st=st, pt=pt, r=r: e.activation(out=pt[0:r], in_=st[0:r], func=AF.Exp), [st], [pt])
                        if mk is not None:
                            S.op("dve", lambda e, pt=pt, r=r, mk=mk: e.tensor_tensor(out=pt[0:r], in0=pt[0:r],
                                                                                     in1=cmb[0:r, mk, :].unsqueeze(1).broadcast_to([r, 4, 128]), op=ALU.mult),
                                 [pt, cmb], [pt])
                        S.op("pe", lambda e, pt=pt, kt=kt, r=r, g=g, bi=bi: e.matmul(ot[:], lhsT=VCs[0:r, kt, g, :], rhs=pt[0:r], start=(bi == 0),
                                                                                      stop=(bi == len(kbs) - 1)), [VCs, pt], [ot])
                    osb = osb_ring.next()
                    S.op("dve", lambda e, osb=osb, ot=ot: e.tensor_copy(out=osb[:], in_=ot[:]), [ot], [osb])
                    for hh in range(4):
                        S.op("pe", lambda e, hh=hh, osb=osb: e.transpose(out=TPo[:, hh, :], in_=osb[:, hh, :], identity=K["identf"][0:65, 0:65]),
                             [osb, K["identf"]], [TPo])
                    den = den_ring.next()
                    S.op("dve", lambda e, den=den, g=g: e.tensor_tensor(out=den[:, 0, :], in0=TPo[:, :, 64], in1=es[:, g * 4:(g + 1) * 4], op=ALU.add),
                         [TPo, es], [den])
                    S.op("dve", lambda e, den=den: e.reciprocal(out=den[:, 1, :], in_=den[:, 0, :]), [den], [den])
                    S.op("dve", lambda e, den=den, g=g: e.tensor_tensor(out=yc[:, g * 4:(g + 1) * 4, :], in0=TPo[:, :, 0:64],
                                                                        in1=den[:, 1, :].unsqueeze(2).broadcast_to([128, 4, 64]), op=ALU.mult),
                         [TPo, den], [yc])
                for g in range(2):
                    grp(g)
                cx.store(sc["YC"], sc["YC"][c0:c0 + 128, :], yc, yc[:].rearrange("p h d -> p (h d)"))

            for t in range(NT):
                body(t)

    def phase_rwkv(self, seq, d):
        nc, S, W, K, sc, G = self.nc, self.S, self.W, self.K, self.sc, self.G
        NT, Lp, L, NB = seq.NT, seq.Lp, seq.L, seq.NB
        NEG = -math.exp(-0.5)
        with Ctx(nc, S) as cx:
            m4 = cx.sb("m4", [64, 4, 64], F32)
            cx.load(m4, m4[:], G["m4"], G["m4"][:, :, :])
            reset = cx.sb("reset", [64, 512], F32)
            cx.load(reset, reset[:], G["reset"], G["reset"][:, :])
            ones64 = cx.sb("ones64", [64, 64], F32)
            S.op("dve", lambda e: e.memset(ones64[:], 1.0), [], [ones64])
            onesb = cx.sb("onesb", [64, 2], BF16)
            S.op("dve", lambda e: e.memset(onesb[:], 1.0), [], [onesb])
            lneps = cx.sb("lneps", [64, 1], F32)
            S.op("dve", lambda e: e.memset(lneps[:], 64e-5), [], [lneps])
            zero1 = cx.sb("zero1", [64, 1], F32)
            S.op("dve", lambda e: e.memset(zero1[:], 0.0), [], [zero1])
            mu = cx.sb("mu", [64, 3, 24], F32)
            cx.load(mu, mu[:, 0, :], W["d_mu_prev"], W["d_mu_prev"][0, 0:1536].rearrange("(s p) -> p s", p=64))
            cx.load(mu, mu[:, 1, :], W["d_mu_next"], W["d_mu_next"][0, 0:1536].rearrange("(s p) -> p s", p=64))
            muL = cx.sb("muL", [128, 3, 3], F32)
            cx.load(muL, muL[:, 0, :], W["d_mu_prev"], W["d_mu_prev"][0, 1536:1920].rearrange("(s p) -> p s", p=128))
            cx.load(muL, muL[:, 1, :], W["d_mu_next"], W["d_mu_next"][0, 1536:1920].rearrange("(s p) -> p s", p=128))
            for m_ in (mu, muL):
                S.op("dve", lambda e, m_=m_: e.tensor_tensor(out=m_[:, 2, :], in0=m_[:, 0, :], in1=m_[:, 1, :], op=ALU.add), [m_], [m_])
                S.op("dve", lambda e, m_=m_: e.tensor_scalar(out=m_[:, 2, :], in0=m_[:, 2, :], scalar1=-1.0, scalar2=1.0, op0=ALU.mult, op1=ALU.add), [m_], [m_])
            pc = cx.sb("pc", [64, 8, 8], F32)
            cx.load(pc, pc[:, 0:2, :], W["d_w0"], W["d_w0"][0].rearrange("d (h p) -> p d h", p=64))
            cx.load(pc, pc[:, 2:4, :], W["d_a0"], W["d_a0"][0].rearrange("d (h p) -> p d h", p=64))
            cx.load(pc, pc[:, 4, :], W["d_k_k"], W["d_k_k"][0].rearrange("(h p) -> p h", p=64))
            cx.load(pc, pc[:, 5, :], W["d_k_a"], W["d_k_a"][0].rearrange("(h p) -> p h", p=64))
            cx.load(pc, pc[:, 7, :], W["d_r_k"], W["d_r_k"][0].rearrange("h p -> p h"))
            S.op("dve", lambda e: e.tensor_scalar(out=pc[:, 6, :], in0=pc[:, 5, :], scalar1=-1.0, scalar2=1.0, op0=ALU.mult, op1=ALU.add), [pc], [pc])
            stg = cx.sb("stg", [128, 512], F32)
            wup = cx.sb("wup", [128, 512], BF16)
            aup = cx.sb("aup", [128, 512], BF16)
            gup = cx.sb("gup", [128, 512], BF16)
            for (dst, srcs) in ((wup, [(W["d_w_up"], W["d_w_up"][0, 0]), (W["d_w_up"], W["d_w_up"][0, 1])]),
                                (aup, [(W["d_a_up"], W["d_a_up"][0, 0]), (W["d_a_up"], W["d_a_up"][0, 1])]),
                                (gup, [(W["d_g_up"], W["d_g_up"][0, 0:64]), (W["d_g_up"], W["d_g_up"][0, 64:128])])):
                for i_, (tl_, ap_) in enumerate(srcs):
                    cx.load(stg, stg[i_ * 64:(i_ + 1) * 64, :], tl_, ap_)
                S.op("dve", lambda e, dst=dst: e.tensor_copy(out=dst[:], in_=stg[:]), [stg], [dst])
            lng = cx.sb("lng", [64, 512], F32)
            lnb = cx.sb("lnb", [64, 512], F32)
            cx.load(lng, lng[:], W["d_ln_g"], W["d_ln_g"][0:1, :].partition_broadcast(64))
            cx.load(lnb, lnb[:], W["d_ln_b"], W["d_ln_b"][0:1, :].partition_broadcast(64))
            idb64 = K["identb"][0:64, 0:64]
            idf64 = K["identf"][0:64, 0:64].unsqueeze(1).broadcast_to([64, 8, 64])

            TPa = cx.ps("TPa", [64, 2, 8, 64], BF16)
            TPb = cx.ps("TPb", [64, 2, 8, 64], BF16)
            GP = cx.ring("GP", 5, [64, 8, 64], F32, ps=True)
            PP = cx.ps("PP", [64, 512], F32)

            TW = cx.sb("TW", [128, 512], BF16)
            AW = cx.sb("AW", [128, 512], BF16)
            SG = cx.sb("SG", [128, 512], BF16)
            ubL = cx.sb("ubL", [128, 514], F32)
            mlt = cx.sb("mlt", [128, 512], F32)
            ub3 = cx.sb("ub3", [64, 3, 514], F32)
            f32t = {n: cx.sb(n, [64, 512], F32) for n in ("rr", "kr", "vv", "lw", "cu", "e1", "e2", "e3", "aa", "kn", "t1", "kd", "bb", "a0t")}
            RT = cx.sb("RT", [64, 8, 512], BF16)
            KT = cx.sb("KT", [64, 8, 512], BF16)
            BT = cx.sb("BT", [64, 8, 512], BF16)
            AT = cx.sb("AT", [64, 8, 512], BF16)
            VT = cx.sb("VT", [64, 8, 512], BF16)
            RKD = cx.sb("RKD", [64, 8, 512], BF16)
            GE = cx.sb("GE", [64, 8, 8], F32)
            Mf = cx.sb("Mf", [64, 8, 64], F32)
            Mb = cx.sb("Mb", [64, 8, 64], BF16)
            S.op("dve", lambda e: e.memset(Mf[:], 0.0), [], [Mf])
            S.op("dve", lambda e: e.memset(Mb[:], 0.0), [], [Mb])
            tm_ring = cx.ring("tm", 5, [64, 4, 8, 64], BF16)
            bt = lambda nm, n=2: cx.ring(nm, n, [64, 8, 64], BF16)
            Yr, P0r, Lakr, Lrbr, Lrkr = bt("Y", 3), bt("P0", 3), bt("Lak", 3), bt("Lrb", 5), bt("Lrk", 5)
            ybuf, pbuf, rbuf, Rfr = bt("yb", 4), bt("pb", 4), bt("rb", 6), bt("Rf", 5)
            Xr, Wr, Ur = bt("X", 5), bt("WT", 5), bt("U", 2)
            o_ring = cx.ring("o", 2, [64, 8, 64], F32)
            of_ring = cx.ring("of", 2, [64, 8, 64], F32)
            cen = cx.sb("cen", [64, 8, 64], F32)
            sqn = cx.sb("sqn", [64, 8, 64], F32)
            st_ring = cx.ring("st", 2, [64, 6, 8], F32)
            yd_ring = cx.ring("yd", 2, [64, 8, 64], BF16)
            UT = sc["UT"]
            UT3 = UT.t[0:1536, :].rearrange("(s q) n -> q s n", q=512)
            strict, incl, strictT = (0, 1, 2) if d == 0 else (2, 3, 0)
            bc8 = lambda ap: ap.unsqueeze(2).broadcast_to([64, 8, 64])

            def mix(dst_ap, ub_prev, ub_c, ub_next, mu_t, idx, dst_tl, ub_tl):
                S.op("act", lambda e: e.activation(out=dst_ap, in_=ub_c, func=AF.Copy, scale=mu_t[:, 2, idx:idx + 1]), [ub_tl, mu_t], [dst_tl])
                S.op("dve", lambda e: e.scalar_tensor_tensor(out=dst_ap, in0=ub_prev, scalar=mu_t[:, 0, idx:idx + 1], in1=dst_ap, op0=ALU.mult, op1=ALU.add),
                     [ub_tl, mu_t, dst_tl], [dst_tl])
                S.op("dve", lambda e: e.scalar_tensor_tensor(out=dst_ap, in0=ub_next, scalar=mu_t[:, 1, idx:idx + 1], in1=dst_ap, op0=ALU.mult, op1=ALU.add),
                     [ub_tl, mu_t, dst_tl], [dst_tl])

            def prep(bi):
                n0 = bi * 512
                Wd = min(512, Lp - n0)
                nch = Wd // 64
                pv = min(Wd, max(0, L - n0))
                for li, (dst, fn_) in enumerate(((TW, AF.Tanh), (AW, AF.Copy), (SG, AF.Sigmoid))):
                    cx.load(ubL, ubL[:, 0:Wd + 2], UT, UT[1536 + li * 128:1536 + (li + 1) * 128, n0:n0 + Wd + 2])
                    mix(mlt[:, 0:Wd], ubL[:, 0:Wd], ubL[:, 1:Wd + 1], ubL[:, 2:Wd + 2], muL, li, mlt, ubL)
                    S.op("act", lambda e, dst=dst, fn_=fn_: e.activation(out=dst[:, 0:Wd], in_=mlt[:, 0:Wd], func=fn_), [mlt], [dst])
                for h in range(8):
                    prep_head(h, n0, Wd, nch, pv)

            def prep_head(h, n0, Wd, nch, pv):
                T = f32t
                w = slice(0, Wd)
                cx.load(ub3, ub3[:, :, 0:Wd + 2], UT, UT3[h * 64:(h + 1) * 64, :, n0:n0 + Wd + 2])
                for sec, nm in ((0, "rr"), (1, "kr"), (2, "vv")):
                    mix(T[nm][:, w], ub3[:, sec, 0:Wd], ub3[:, sec, 1:Wd + 1], ub3[:, sec, 2:Wd + 2], mu, sec * 8 + h, T[nm], ub3)
                    if pv < Wd:
                        S.op("pool", lambda e, nm=nm: e.memset(T[nm][:, pv:Wd], 0.0), [], [T[nm]])
                hs = slice(h * 64, (h + 1) * 64)
                ds_ = slice(d * 64, (d + 1) * 64)
                S.op("pe", lambda e: e.matmul(PP[:, w], lhsT=wup[ds_, hs], rhs=TW[ds_, w], start=True, stop=True), [wup, TW], [PP], lane=d)
                S.op("act", lambda e: e.activation(out=T["lw"][:, w], in_=PP[:, w], func=AF.Sigmoid, bias=pc[:, d, h:h + 1], scale=1.0), [PP, pc], [T["lw"]])
                S.op("dve", lambda e: e.tensor_scalar(out=T["lw"][:, w], in0=T["lw"][:, w], scalar1=NEG, scalar2=None, op0=ALU.mult), [T["lw"]], [T["lw"]])
                S.op("dve", lambda e: e.tensor_tensor_scan(out=T["cu"][:, w], data0=reset[:, w], data1=T["lw"][:, w], initial=0.0, op0=ALU.mult, op1=ALU.add),
                     [reset, T["lw"]], [T["cu"]])
                if d == 1:
                    cu3 = T["cu"][:, w].rearrange("p (c t) -> p c t", t=64)
                    S.op("dve", lambda e: e.tensor_tensor(out=T["t1"][:, w], in0=T["lw"][:, w], in1=T["cu"][:, w], op=ALU.subtract), [T["lw"], T["cu"]], [T["t1"]])
                    S.op("dve", lambda e: e.tensor_tensor(out=T["e1"][:, w].rearrange("p (c t) -> p c t", t=64), in0=T["t1"][:, w].rearrange("p (c t) -> p c t", t=64),
                                                          in1=cu3[:, :, 63:64].broadcast_to([64, nch, 64]), op=ALU.add), [T["t1"], T["cu"]], [T["e1"]])
                    S.op("dve", lambda e: e.tensor_copy(out=T["cu"][:, w], in_=T["e1"][:, w]), [T["e1"]], [T["cu"]])
                S.op("act", lambda e: e.activation(out=T["e1"][:, w], in_=T["cu"][:, w], func=AF.Exp), [T["cu"]], [T["e1"]])
                S.op("act", lambda e: e.activation(out=T["e2"][:, w], in_=T["cu"][:, w], func=AF.Exp, scale=-1.0), [T["cu"]], [T["e2"]])
                S.op("dve", lambda e: e.tensor_tensor(out=T["t1"][:, w], in0=T["cu"][:, w], in1=T["lw"][:, w], op=ALU.subtract), [T["cu"], T["lw"]], [T["t1"]])
                S.op("act", lambda e: e.activation(out=T["e3"][:, w], in_=T["t1"][:, w], func=AF.Exp), [T["t1"]], [T["e3"]])
                S.op("pe", lambda e: e.matmul(PP[:, w], lhsT=aup[ds_, hs], rhs=AW[ds_, w], start=True, stop=True), [aup, AW], [PP], lane=d)
                S.op("act", lambda e: e.activation(out=T["aa"][:, w], in_=PP[:, w], func=AF.Sigmoid, bias=pc[:, 2 + d, h:h + 1], scale=1.0), [PP, pc], [T["aa"]])
                S.op("act", lambda e: e.activation(out=T["kn"][:, w], in_=T["kr"][:, w], func=AF.Copy, scale=pc[:, 4, h:h + 1]), [T["kr"], pc], [T["kn"]])
                S.op("act", lambda e: e.activation(out=T["t1"][:, w], in_=T["kn"][:, w], func=AF.Square), [T["kn"]], [T["t1"]])
                S.op("pe", lambda e: e.matmul(PP[:, w], lhsT=ones64[:], rhs=T["t1"][:, w], start=True, stop=True), [ones64, T["t1"]], [PP])
                S.op("dve", lambda e: e.tensor_scalar(out=T["t1"][:, w], in0=PP[:, w], scalar1=1e-24, scalar2=None, op0=ALU.max), [PP], [T["t1"]])
                S.op("act", lambda e: e.activation(out=T["t1"][:, w], in_=T["t1"][:, w], func=AF.Sqrt, bias=zero1[:, 0:1], scale=1.0), [T["t1"], zero1], [T["t1"]])
                S.op("dve", lambda e: e.reciprocal(out=T["t1"][:, w], in_=T["t1"][:, w]), [T["t1"]], [T["t1"]])
                S.op("dve", lambda e: e.tensor_tensor(out=T["kn"][:, w], in0=T["kn"][:, w], in1=T["t1"][:, w], op=ALU.mult), [T["kn"], T["t1"]], [T["kn"]])
                S.op("dve", lambda e: e.tensor_scalar(out=T["kd"][:, w], in0=T["aa"][:, w], scalar1=pc[:, 5, h:h + 1], scalar2=pc[:, 6, h:h + 1], op0=ALU.mult, op1=ALU.add),
                     [T["aa"], pc], [T["kd"]])
                S.op("dve", lambda e: e.tensor_tensor(out=T["kd"][:, w], in0=T["kd"][:, w], in1=T["kr"][:, w], op=ALU.mult), [T["kd"], T["kr"]], [T["kd"]])
                S.op("pool", lambda e: e.tensor_tensor(out=T["bb"][:, w], in0=T["kn"][:, w], in1=T["aa"][:, w], op=ALU.mult), [T["kn"], T["aa"]], [T["bb"]])
                S.op("dve", lambda e: e.tensor_tensor(out=RT[:, h, w], in0=T["rr"][:, w], in1=T["e1"][:, w], op=ALU.mult), [T["rr"], T["e1"]], [RT])
                S.op("pool", lambda e: e.tensor_tensor(out=KT[:, h, w], in0=T["kd"][:, w], in1=T["e2"][:, w], op=ALU.mult), [T["kd"], T["e2"]], [KT])
                S.op("pool", lambda e: e.tensor_tensor(out=BT[:, h, w], in0=T["bb"][:, w], in1=T["e2"][:, w], op=ALU.mult), [T["bb"], T["e2"]], [BT])
                S.op("dve", lambda e: e.scalar_tensor_tensor(out=AT[:, h, w], in0=T["kn"][:, w], scalar=-1.0, in1=T["e3"][:, w], op0=ALU.mult, op1=ALU.mult),
                     [T["kn"], T["e3"]], [AT])
                S.op("act", lambda e: e.copy(out=VT[:, h, w], in_=T["vv"][:, w]), [T["vv"]], [VT])
                e13 = T["e1"][:, w].rearrange("p (c t) -> p c t", t=64)
                gcol = 63 if d == 0 else 0
                S.op("dve", lambda e: e.tensor_copy(out=GE[:, 0:nch, h], in_=e13[:, :, gcol]), [T["e1"]], [GE])
                if d == 1:
                    do = slice(0, 64)
                    S.op("pe", lambda e: e.matmul(PP[:, w], lhsT=aup[do, hs], rhs=AW[do, w], start=True, stop=True), [aup, AW], [PP])
                    S.op("act", lambda e: e.activation(out=T["a0t"][:, w], in_=PP[:, w], func=AF.Sigmoid, bias=pc[:, 2, h:h + 1], scale=1.0), [PP, pc], [T["a0t"]])
                    S.op("dve", lambda e: e.tensor_scalar(out=T["a0t"][:, w], in0=T["a0t"][:, w], scalar1=pc[:, 5, h:h + 1], scalar2=pc[:, 6, h:h + 1], op0=ALU.mult, op1=ALU.add),
                         [T["a0t"], pc], [T["a0t"]])
                    S.op("dve", lambda e: e.tensor_tensor(out=T["a0t"][:, w], in0=T["a0t"][:, w], in1=T["kr"][:, w], op=ALU.mult), [T["a0t"], T["kr"]], [T["a0t"]])
                    S.op("dve", lambda e: e.tensor_tensor(out=T["a0t"][:, w], in0=T["a0t"][:, w], in1=T["kd"][:, w], op=ALU.add), [T["a0t"], T["kd"]], [T["a0t"]])
                    S.op("dve", lambda e: e.scalar_tensor_tensor(out=RKD[:, h, w], in0=T["rr"][:, w], scalar=pc[:, 7, h:h + 1], in1=T["a0t"][:, w], op0=ALU.mult, op1=ALU.mult),
                         [T["rr"], pc, T["a0t"]], [RKD])

            def mm8(ps, lhs_fn, rhs_fn, lt, rt, extra=()):
                for h in range(8):
                    terms = [(lhs_fn, rhs_fn)] + list(extra)
                    for i_, (lf, rf) in enumerate(terms):
                        S.op("pe", lambda e, h=h, lf=lf, rf=rf, i_=i_, n_=len(terms): e.matmul(ps[:, h, :], lhsT=lf(h), rhs=rf(h), start=(i_ == 0), stop=(i_ == n_ - 1)),
                             lt, [ps])

            pk = {}

            def stageA(n0, c):
                cs = slice(c * 64, (c + 1) * 64)
                for ki, X in enumerate((AT, BT)):
                    for h in range(8):
                        S.op("pe", lambda e, ki=ki, X=X, h=h: e.transpose(out=TPa[:, ki, h, :], in_=X[:, h, cs], identity=idb64), [X, K["identb"]], [TPa])
                for ki, X in enumerate((KT, VT)):
                    for h in range(8):
                        S.op("pe", lambda e, ki=ki, X=X, h=h: e.transpose(out=TPb[:, ki, h, :], in_=X[:, h, cs], identity=idb64), [X, K["identb"]], [TPb])
                tm = tm_ring.next()
                S.op("act", lambda e: e.copy(out=tm[:, 0:2], in_=TPa[:]), [TPa], [tm])
                S.op("act", lambda e: e.copy(out=tm[:, 2:4], in_=TPb[:]), [TPb], [tm])
                yield
                A_tm, V_tm = (lambda h: tm[:, 0, h, :]), (lambda h: tm[:, 3, h, :])

                def score(Lh, Rh, mi, ring):
                    ps = GP.next()
                    mm8(ps, lambda h: Lh[:, h, cs], lambda h: Rh[:, h, cs], [Lh, Rh], None)
                    o = ring.next()
                    S.op("dve", lambda e: e.tensor_tensor(out=o[:], in0=ps[:], in1=m4[:, mi, :].unsqueeze(1).broadcast_to([64, 8, 64]), op=ALU.mult), [ps, m4], [o])
                    return o
                Y = score(BT, AT, strict, Yr)
                P0 = score(AT, BT, strictT, P0r)
                yield
                LakT = score(KT, AT, strict, Lakr)
                LrbT = score(BT, RT, incl, Lrbr)
                LrkT = score(KT, RT, incl, Lrkr)
                R = rbuf.next()
                S.op("dve", lambda e, R=R: e.tensor_tensor(out=R[:], in0=Y[:], in1=idf64, op=ALU.add), [Y, K["identf"]], [R])
                yield
                Yk, Pk = Y, P0
                for lvl in range(5):
                    psP = GP.next()
                    mm8(psP, lambda h, Yk=Yk: Yk[:, h, :], lambda h, Pk=Pk: Pk[:, h, :], [Yk, Pk], None)
                    Pn = pbuf.next()
                    S.op("act", lambda e, Pn=Pn, psP=psP: e.copy(out=Pn[:], in_=psP[:]), [psP], [Pn])
                    if lvl < 4:
                        psY = GP.next()
                        mm8(psY, lambda h, Pk=Pk: Pk[:, h, :], lambda h, Yk=Yk: Yk[:, h, :], [Yk, Pk], None)
                        Yn = ybuf.next()
                        S.op("act", lambda e, Yn=Yn, psY=psY: e.copy(out=Yn[:], in_=psY[:]), [psY], [Yn])
                    yield
                    psR = GP.next()
                    mm8(psR, lambda h, Pn=Pn: Pn[:, h, :], lambda h, R=R: R[:, h, :], [Pn, R], None)
                    Rn = rbuf.next() if lvl < 4 else Rfr.next()
                    S.op("dve", lambda e, Rn=Rn, psR=psR, R=R: e.tensor_tensor(out=Rn[:], in0=psR[:], in1=R[:], op=ALU.add), [psR, R], [Rn])
                    R = Rn
                    Yk, Pk = (Yn if lvl < 4 else Yk), Pn
                    yield
                Rf = R
                psX = GP.next()
                mm8(psX, lambda h: LakT[:, h, :], V_tm, [LakT, tm], None)
                Xs = Xr.next()
                S.op("act", lambda e: e.copy(out=Xs[:], in_=psX[:]), [psX], [Xs])
                psW = GP.next()
                mm8(psW, A_tm, lambda h: Rf[:, h, :], [tm, Rf], None)
                WTs = Wr.next()
                S.op("act", lambda e: e.copy(out=WTs[:], in_=psW[:]), [psW], [WTs])
                pk[c] = (tm, LrbT, LrkT, Rf, Xs, WTs)
                yield

            def chain(n0, c):
                cs = slice(c * 64, (c + 1) * 64)
                row0 = n0 + c * 64
                tm, LrbT, LrkT, R, Xs, WTs = pk.pop(c)
                B_tm, K_tm, V_tm = (lambda h: tm[:, 1, h, :]), (lambda h: tm[:, 2, h, :]), (lambda h: tm[:, 3, h, :])
                psU = GP.next()
                mm8(psU, lambda h: WTs[:, h, :], lambda h: Mb[:, h, :], [WTs, Mb, R, Xs], None, extra=[(lambda h: R[:, h, :], lambda h: Xs[:, h, :])])
                Us = Ur.next()
                S.op("act", lambda e: e.copy(out=Us[:], in_=psU[:]), [psU], [Us])
                yield
                psO = GP.next()
                mm8(psO, lambda h: RT[:, h, cs], lambda h: Mb[:, h, :], [RT, Mb, LrbT, Us, LrkT, tm], None,
                    extra=[(lambda h: LrbT[:, h, :], lambda h: Us[:, h, :]), (lambda h: LrkT[:, h, :], V_tm)])
                psM = GP.next()
                mm8(psM, B_tm, lambda h: Us[:, h, :], [tm, Us], None, extra=[(K_tm, V_tm)])
                ob = o_ring.next()
                if d == 0:
                    S.op("act", lambda e: e.copy(out=ob[:], in_=psO[:]), [psO], [ob])
                    cx.store(sc["OF"], sc["OF"][row0:row0 + 64, :], ob, ob[:].rearrange("p h i -> p (h i)"))
                else:
                    of = of_ring.next()
                    cx.load(of, of[:].rearrange("p h i -> p (h i)"), sc["OF"], sc["OF"][row0:row0 + 64, :])
                    S.op("dve", lambda e: e.tensor_tensor(out=ob[:], in0=psO[:], in1=of[:], op=ALU.add), [psO, of], [ob])
                S.op("dve", lambda e: e.tensor_tensor(out=Mf[:], in0=psM[:], in1=Mf[:], op=ALU.add), [psM, Mf], [Mf])
                S.op("dve", lambda e: e.tensor_tensor(out=Mf[:], in0=Mf[:], in1=bc8(GE[:, c, :]), op=ALU.mult), [Mf, GE], [Mf])
                S.op("act", lambda e: e.copy(out=Mb[:], in_=Mf[:]), [Mf], [Mb])
                yield
                if d == 1:
                    st = st_ring.next()
                    S.op("dve", lambda e: e.tensor_reduce(out=st[:, 0, :], in_=ob[:], axis=AX.X, op=ALU.add), [ob], [st])
                    S.op("dve", lambda e: e.tensor_scalar(out=st[:, 1, :], in0=st[:, 0, :], scalar1=-1.0 / 64, scalar2=None, op0=ALU.mult), [st], [st])
                    S.op("dve", lambda e: e.tensor_tensor(out=cen[:], in0=ob[:], in1=bc8(st[:, 1, :]), op=ALU.add), [ob, st], [cen])
                    S.op("act", lambda e: e.activation(out=sqn[:], in_=cen[:], func=AF.Square), [cen], [sqn])
                    S.op("dve", lambda e: e.tensor_reduce(out=st[:, 2, :], in_=sqn[:], axis=AX.X, op=ALU.add), [sqn], [st])
                    self.rstd(cx, st, st[:, 2, :], st, st[:, 4, :], st, st[:, 3, :], 1.0 / 64, lneps[:, 0:1])
                    yield
                    S.op("dve", lambda e: e.tensor_tensor(out=cen[:], in0=cen[:], in1=bc8(st[:, 4, :]), op=ALU.mult), [cen, st], [cen])
                    S.op("dve", lambda e: e.tensor_tensor(out=cen[:], in0=cen[:], in1=lng[:].rearrange("p (h i) -> p h i", i=64), op=ALU.mult), [cen, lng], [cen])
                    S.op("dve", lambda e: e.tensor_tensor(out=cen[:], in0=cen[:], in1=lnb[:].rearrange("p (h i) -> p h i", i=64), op=ALU.add), [cen, lnb], [cen])
                    psB = GP.next()
                    for h in range(8):
                        S.op("pe", lambda e, h=h: e.matmul(psB[:, h, 0:1], lhsT=RKD[:, h, cs], rhs=onesb[:, 0:1], start=True, stop=True), [RKD, onesb], [psB])
                    S.op("act", lambda e: e.copy(out=st[:, 5, :], in_=psB[:, :, 0]), [psB], [st])
                    S.op("dve", lambda e: e.tensor_tensor(out=sqn[:], in0=tm[:, 3], in1=bc8(st[:, 5, :]), op=ALU.mult), [tm, st], [sqn])
                    yield
                    S.op("dve", lambda e: e.tensor_tensor(out=cen[:], in0=cen[:], in1=sqn[:], op=ALU.add), [cen, sqn], [cen])
                    psG = GP.next()
                    S.op("pe", lambda e: e.matmul(psG[:].rearrange("p h i -> p (h i)"), lhsT=SG[:, cs], rhs=gup[:], start=True, stop=True), [SG, gup], [psG])
                    yd = yd_ring.next()
                    S.op("dve", lambda e: e.tensor_tensor(out=yd[:], in0=cen[:], in1=psG[:], op=ALU.mult), [cen, psG], [yd])
                    cx.store(sc["YD"], sc["YD"][row0:row0 + 64, :], yd, yd[:].rearrange("p h i -> p (h i)"))
                    yield

            def run_block(n0, cl):
                LOOKA = 2
                gens = {}
                nA = 0
                doneA = set()
                ci = 0
                chain_g = None
                while ci < len(cl):
                    while nA < len(cl) and len(gens) < LOOKA and nA <= ci + LOOKA:
                        gens[nA] = stageA(n0, cl[nA])
                        nA += 1
                    if chain_g is None and ci in doneA:
                        chain_g = chain(n0, cl[ci])
                    for i_ in list(gens.keys()):
                        try:
                            next(gens[i_])
                        except StopIteration:
                            del gens[i_]
                            doneA.add(i_)
                    if chain_g is not None:
                        try:
                            next(chain_g)
                        except StopIteration:
                            chain_g = None
                            ci += 1

            blocks = range(NB) if d == 0 else range(NB - 1, -1, -1)
            for bi in blocks:
                prep(bi)
                n0 = bi * 512
                nch = min(512, Lp - n0) // 64
                run_block(n0, list(range(nch) if d == 0 else range(nch - 1, -1, -1)))


def make_inputs(prog, inputs, core):
    m = {}
    for k in prog.W:
        m[k] = np.ascontiguousarray(np.asarray(inputs[k], dtype=np.float32))
    for k, v in global_consts().items():
        m["c_" + k] = v
    meta = np.asarray(inputs["meta_tokens"], dtype=np.float32)
    for s in prog.seqs:
        x = s.xsrc(inputs, core)
        h0 = np.zeros((s.Lp, D), np.float32)
        h0[:16] = meta
        h0[16:s.L] = x
        m["h0_" + s.name] = h0
        for k, v in host_consts(s).items():
            m["c_%s_%s" % (k, s.name)] = v
    return m


def kernel(**inputs):
    xp = np.asarray(inputs["x_prompt"], dtype=np.float32)
    xs = np.asarray(inputs["x_sample"], dtype=np.float32)
    sp = Seq("p", xp.shape[1])
    ss = Seq("s", xs.shape[1])
    sp.xsrc = lambda inp, core: xp[core]
    ss.xsrc = lambda inp, core: xs[0]
    prog = Prog([sp, ss], debug=False)
    nc = prog.build()
    in_maps = []
    for c in range(NCORE):
        m = make_inputs(prog, inputs, c)
        in_maps.append({k: m[k] for k in prog.in_names})
    res = run_bass_kernel_spmd(nc, in_maps, core_ids=list(range(NCORE)))
    r = res.results
    y_prompt = np.stack([np.asarray(r[c]["y_p"], dtype=np.float32) for c in range(NCORE)], axis=0)
    n = xs.shape[1] // NCORE
    y_sample = np.concatenate([np.asarray(r[c]["y_s"], dtype=np.float32)[c * n:(c + 1) * n] for c in range(NCORE)], axis=0)[None]
    return (y_prompt, y_sample)
```
